# Optimizing a Trainium2 kernel written in Bass

```python
import jax, jax.numpy as jnp
from jax import lax
import numpy as np

D_MODEL = 2048
BATCH = 1
SEQ = 8192
DEPTH = 1
DEC_BATCH = 32
DEC_SEQ = 8
PAST_LEN = 8192
PAGE_SIZE = 128

HEAD_DIM = 64
NSA_HEADS = 16
NSA_KV_HEADS = 4
NSA_GROUP = NSA_HEADS // NSA_KV_HEADS
NSA_WIDTH = NSA_HEADS * HEAD_DIM
NSA_BRANCHES = 3
CMP_LEN = 32
CMP_STRIDE = 16
CMP_HIDDEN = 2 * HEAD_DIM
SEL_BLOCK = 64
N_SELECT = 16
N_LOCAL = 2
WINDOW = 512
Q_BLOCK = 128
GLA_HEADS = 4
GLA_DK = 128
GLA_DV = 256
GLA_KEY_WIDTH = GLA_HEADS * GLA_DK
GLA_WIDTH = GLA_HEADS * GLA_DV
GLA_RANK = 16
GLA_TAU = 16.0
GLA_CHUNK = 64
ROPE_THETA = 10000.0
EPS = 1e-6
NEG = -1e30
BIG = 1e30
TINY = 1e-30

IN_SPLIT = (NSA_WIDTH,
            NSA_BRANCHES * 2 * NSA_KV_HEADS * HEAD_DIM,
            NSA_HEADS * NSA_BRANCHES,
            NSA_WIDTH,
            GLA_KEY_WIDTH, GLA_KEY_WIDTH, GLA_WIDTH,
            GLA_RANK,
            GLA_WIDTH,
            D_MODEL, D_MODEL)
IN_WIDTH = sum(IN_SPLIT)

kernel_name = 'nsa_gla_gated_hybrid_step'


def rms_norm(x, w):
    xf = x.astype(jnp.float32)
    y = xf * lax.rsqrt(jnp.mean(xf * xf, axis=-1, keepdims=True) + EPS)
    return (y * w.astype(jnp.float32)).astype(x.dtype)


def rope(x, pos):
    half = x.shape[-1] // 2
    inv = ROPE_THETA ** (-jnp.arange(half, dtype=jnp.float32) / half)
    ang = pos.astype(jnp.float32)[:, None] * inv[None, :]
    cos = jnp.cos(ang)[:, None, :]
    sin = jnp.sin(ang)[:, None, :]
    xf = x.astype(jnp.float32)
    x1, x2 = xf[..., :half], xf[..., half:]
    return jnp.concatenate([x1 * cos - x2 * sin, x2 * cos + x1 * sin], axis=-1).astype(x.dtype)


def masked_softmax(s, mask):
    s = jnp.where(mask, s.astype(jnp.float32), NEG)
    p = jnp.exp(s - jnp.max(s, axis=-1, keepdims=True)) * mask
    return p / jnp.maximum(jnp.sum(p, axis=-1, keepdims=True), TINY)


def compress_blocks(kv, cmp_pos, cmp_w1, cmp_b1, cmp_w2, cmp_b2):
    b, t = kv.shape[:2]
    r = CMP_LEN // CMP_STRIDE
    n_chunk = -(-t // CMP_STRIDE)
    kv = jnp.pad(kv, ((0, 0), (0, n_chunk * CMP_STRIDE - t), (0, 0), (0, 0), (0, 0)))
    ch = kv.reshape(b, n_chunk, CMP_STRIDE, 2, NSA_KV_HEADS, HEAD_DIM)
    w1 = cmp_w1.reshape(2, r, CMP_STRIDE, HEAD_DIM, CMP_HIDDEN)
    part = jnp.einsum('bnpxhd,xjpde->jbnxhe', ch, w1)
    nc = n_chunk - r + 1
    hid = part[0][:, :nc]
    for j in range(1, r):
        hid = hid + part[j][:, j:j + nc]
    pos_bias = jnp.einsum('xld,xlde->xe', cmp_pos, cmp_w1) + cmp_b1
    hid = jax.nn.silu(hid + pos_bias[:, None, :])
    out = jnp.einsum('bnxhe,xed->bnxhd', hid, cmp_w2) + cmp_b2[:, None, :]
    c_end = jnp.arange(nc, dtype=jnp.int32) * CMP_STRIDE + (CMP_LEN - 1)
    return out, c_end


def cmp_to_sel(nc, nb):
    cs = jnp.arange(nc)[:, None] * CMP_STRIDE
    js = jnp.arange(nb)[None, :] * SEL_BLOCK
    return ((cs < js + SEL_BLOCK) & (cs + CMP_LEN > js)).astype(jnp.float32)


def to_sel_blocks(kv):
    b, t = kv.shape[:2]
    nb = -(-t // SEL_BLOCK)
    kv = jnp.pad(kv, ((0, 0), (0, nb * SEL_BLOCK - t), (0, 0), (0, 0), (0, 0)))
    return kv.reshape(b, nb, SEL_BLOCK, 2, NSA_KV_HEADS, HEAD_DIM)


def nsa_attend(q, q_pos, kv_c, c_end, kv_s, kv_w, w_pos):
    b, nq = q.shape[:2]
    nc, nb = kv_c.shape[1], kv_s.shape[1]
    s_c = jnp.einsum('bqhgd,bchd->bhgqc', q, kv_c[:, :, 0])
    p_c = masked_softmax(s_c, c_end[None, :] <= q_pos[:, None])
    o_c = jnp.einsum('bhgqc,bchd->bqhgd', p_c.astype(q.dtype), kv_c[:, :, 1])
    imp = jnp.einsum('bhgqc,cn->bhqn', p_c, cmp_to_sel(nc, nb))
    blk = jnp.arange(nb, dtype=jnp.int32)[None, :]
    cur = (q_pos // SEL_BLOCK)[:, None]
    forced = (blk == 0) | ((blk <= cur) & (blk > cur - N_LOCAL))
    imp = jnp.where(forced, BIG, imp)
    imp = jnp.where(blk > cur, -BIG, imp)
    top_v, top_i = lax.top_k(imp, min(N_SELECT, nb))
    n_sel = top_i.shape[-1]
    sel_ok = top_v > -0.5 * BIG
    kv_sh = jnp.moveaxis(kv_s, 4, 1)
    g = jax.vmap(jax.vmap(lambda a, i: a[i]))(kv_sh, top_i)
    key_pos = top_i[..., None] * SEL_BLOCK + jnp.arange(SEL_BLOCK, dtype=jnp.int32)
    m_s = sel_ok[..., None] & (key_pos <= q_pos[:, None, None])
    s_s = jnp.einsum('bqhgd,bhqnsd->bhgqns', q, g[..., 0, :])
    nk = n_sel * SEL_BLOCK
    p_s = masked_softmax(s_s.reshape(b, NSA_KV_HEADS, NSA_GROUP, nq, nk),
                         m_s.reshape(b, NSA_KV_HEADS, nq, nk)[:, :, None])
    o_s = jnp.einsum('bhgqk,bhqkd->bqhgd', p_s.astype(q.dtype),
                     g[..., 1, :].reshape(b, NSA_KV_HEADS, nq, nk, HEAD_DIM))
    s_w = jnp.einsum('bqhgd,bkhd->bhgqk', q, kv_w[:, :, 0])
    m_w = ((w_pos[None, :] <= q_pos[:, None]) & (w_pos[None, :] > q_pos[:, None] - WINDOW)
           & (w_pos[None, :] >= 0))
    p_w = masked_softmax(s_w, m_w)
    o_w = jnp.einsum('bhgqk,bkhd->bqhgd', p_w.astype(q.dtype), kv_w[:, :, 1])
    return o_c, o_s, o_w


def nsa_prompt(q, kvc, kvs, kvw, cmp_pos, cmp_w1, cmp_b1, cmp_w2, cmp_b2):
    b, t = q.shape[:2]
    kv_c, c_end = compress_blocks(kvc, cmp_pos, cmp_w1, cmp_b1, cmp_w2, cmp_b2)
    kv_s = to_sel_blocks(kvs)
    kvw_pad = jnp.pad(kvw, ((0, 0), (WINDOW, 0), (0, 0), (0, 0), (0, 0)))

    def one_block(i):
        s = i * Q_BLOCK
        qb = lax.dynamic_slice_in_dim(q, s, Q_BLOCK, axis=1)
        qp = s + jnp.arange(Q_BLOCK, dtype=jnp.int32)
        kwb = lax.dynamic_slice_in_dim(kvw_pad, s, WINDOW + Q_BLOCK, axis=1)
        wp = s - WINDOW + jnp.arange(WINDOW + Q_BLOCK, dtype=jnp.int32)
        return nsa_attend(qb, qp, kv_c, c_end, kv_s, kwb, wp)

    o_c, o_s, o_w = lax.map(one_block, jnp.arange(t // Q_BLOCK, dtype=jnp.int32))
    shp = (b, t, NSA_KV_HEADS, NSA_GROUP, HEAD_DIM)
    return (jnp.moveaxis(o_c, 0, 1).reshape(shp), jnp.moveaxis(o_s, 0, 1).reshape(shp),
            jnp.moveaxis(o_w, 0, 1).reshape(shp))


def gather_pages(pool, page_table):
    rows = pool[page_table]
    return rows.reshape(page_table.shape[0], page_table.shape[1] * pool.shape[1], *pool.shape[2:])


def nsa_sample(q, kvc, kvs, kvw, cache_c, cache_s, cache_w, page_table,
               cmp_pos, cmp_w1, cmp_b1, cmp_w2, cmp_b2):
    nq = q.shape[1]
    past = page_table.shape[1] * cache_c.shape[1]
    full_c = jnp.concatenate([gather_pages(cache_c, page_table), kvc], axis=1)
    full_s = jnp.concatenate([gather_pages(cache_s, page_table), kvs], axis=1)
    kv_c, c_end = compress_blocks(full_c, cmp_pos, cmp_w1, cmp_b1, cmp_w2, cmp_b2)
    kv_s = to_sel_blocks(full_s)
    wb = cache_w.shape[1]
    kv_w = jnp.concatenate([cache_w, kvw], axis=1)
    w_pos = past - wb + jnp.arange(wb + nq, dtype=jnp.int32)
    q_pos = past + jnp.arange(nq, dtype=jnp.int32)
    return nsa_attend(q, q_pos, kv_c, c_end, kv_s, kv_w, w_pos)


def gla_scan(q, k, v, log_a, s0):
    b, t, h, _ = q.shape
    c = min(GLA_CHUNK, t)
    n = -(-t // c)
    pad = n * c - t

    def prep(a):
        a = jnp.pad(a.astype(jnp.float32), ((0, 0), (0, pad), (0, 0), (0, 0)))
        return jnp.moveaxis(a.reshape(b, n, c, h, a.shape[-1]), 1, 0)

    causal = jnp.tril(jnp.ones((c, c), bool))[None, :, :, None, None]

    def step(S, inp):
        qc, kc, vc, lc = inp
        cum = jnp.cumsum(lc, axis=1)
        o_inter = jnp.einsum('bthk,bhkv->bthv', qc * jnp.exp(cum), S)
        decay = jnp.exp(jnp.where(causal, cum[:, :, None] - cum[:, None, :], NEG))
        att = jnp.einsum('bthk,bshk,btshk->bhts', qc, kc, decay)
        o_intra = jnp.einsum('bhts,bshv->bthv', att, vc)
        c_last = cum[:, -1]
        S = (jnp.exp(c_last)[..., None] * S
             + jnp.einsum('bshk,bshv->bhkv', kc * jnp.exp(c_last[:, None] - cum), vc))
        return S, o_inter + o_intra

    S, o = lax.scan(step, s0.astype(jnp.float32), (prep(q), prep(k), prep(v), prep(log_a)))
    o = jnp.moveaxis(o, 0, 1).reshape(b, n * c, h, GLA_DV)[:, :t]
    return o, S


def sublayer(x, c, pos, s0, nsa_fn, norm_w, w_ada, b_ada, w_in, w_a2, b_a, gla_norm_w,
             w_o_nsa, w_o_gla, w_out):
    b, t, _ = x.shape
    mod = jnp.einsum('bd,de->be', c, w_ada) + b_ada
    shift, scale, gate = jnp.split(mod, 3, axis=-1)
    h = rms_norm(x, norm_w) * (1.0 + scale[:, None, :]) + shift[:, None, :]
    cuts = np.cumsum(IN_SPLIT)[:-1].tolist()
    (q_n, kv_n, g_n, z_n, q_g, k_g, v_g, a_g, z_g, m_n, m_g) = jnp.split(
        jnp.einsum('btd,de->bte', h, w_in), cuts, axis=-1)
    q = rope(q_n.reshape(b, t, NSA_HEADS, HEAD_DIM), pos) * (HEAD_DIM ** -0.5)
    q = q.reshape(b, t, NSA_KV_HEADS, NSA_GROUP, HEAD_DIM)
    kv = kv_n.reshape(b, t, NSA_BRANCHES, 2, NSA_KV_HEADS, HEAD_DIM)
    k = rope(kv[:, :, :, 0].reshape(b, t, NSA_BRANCHES * NSA_KV_HEADS, HEAD_DIM), pos)
    k = k.reshape(b, t, NSA_BRANCHES, NSA_KV_HEADS, HEAD_DIM)
    kv = jnp.stack([k, kv[:, :, :, 1]], axis=3)
    kvc, kvs, kvw = kv[:, :, 0], kv[:, :, 1], kv[:, :, 2]
    o_c, o_s, o_w = nsa_fn(q, kvc, kvs, kvw)
    gb = jax.nn.sigmoid(g_n).reshape(b, t, NSA_KV_HEADS, NSA_GROUP, NSA_BRANCHES)
    o_nsa = (gb[..., 0:1] * o_c + gb[..., 1:2] * o_s + gb[..., 2:3] * o_w).reshape(b, t, NSA_WIDTH)
    o_nsa = o_nsa * jax.nn.silu(z_n)
    qg = q_g.reshape(b, t, GLA_HEADS, GLA_DK) * (GLA_DK ** -0.5)
    kg = k_g.reshape(b, t, GLA_HEADS, GLA_DK)
    vg = v_g.reshape(b, t, GLA_HEADS, GLA_DV)
    log_a = jax.nn.log_sigmoid((jnp.einsum('btr,re->bte', a_g, w_a2) + b_a).astype(jnp.float32)) / GLA_TAU
    o_g, s_new = gla_scan(qg, kg, vg, log_a.reshape(b, t, GLA_HEADS, GLA_DK), s0)
    o_gla = rms_norm(o_g, gla_norm_w).astype(x.dtype).reshape(b, t, GLA_WIDTH) * jax.nn.silu(z_g)
    merged = (jax.nn.sigmoid(m_n) * jnp.einsum('bte,ed->btd', o_nsa, w_o_nsa)
              + jax.nn.sigmoid(m_g) * jnp.einsum('bte,ed->btd', o_gla, w_o_gla))
    y = x + gate[:, None, :] * jnp.einsum('btd,de->bte', merged, w_out)
    return y, kvc, kvs, kvw, s_new


def setup_inputs(seed: int = 0) -> dict:
    key = jax.random.key(seed)
    ks = jax.random.split(key, 25)
    n_pages = PAST_LEN // PAGE_SIZE
    n_pool = (DEC_BATCH * n_pages * 5) // 4
    win_buf = min(WINDOW, PAST_LEN)
    kv_row = (2, NSA_KV_HEADS, HEAD_DIM)

    def nrm(k, shape, scale):
        return scale * jax.random.normal(k, shape, jnp.float32)

    page_table = jax.random.permutation(ks[6], n_pool)[: DEC_BATCH * n_pages]
    page_table = page_table.reshape(DEC_BATCH, n_pages).astype(jnp.int32)
    return {
        'x_prompt': nrm(ks[0], (BATCH, SEQ, D_MODEL), 1.0),
        'x_sample': nrm(ks[1], (DEC_BATCH, DEC_SEQ, D_MODEL), 1.0),
        'cache_kv_cmp': nrm(ks[2], (DEPTH, n_pool, PAGE_SIZE) + kv_row, 1.0),
        'cache_kv_sel': nrm(ks[3], (DEPTH, n_pool, PAGE_SIZE) + kv_row, 1.0),
        'cache_kv_win': nrm(ks[4], (DEPTH, DEC_BATCH, win_buf) + kv_row, 1.0),
        'state_gla': nrm(ks[5], (DEPTH, DEC_BATCH, GLA_HEADS, GLA_DK, GLA_DV), 1.0),
        'page_table': page_table,
        'c_prompt': nrm(ks[7], (BATCH, D_MODEL), 1.0),
        'c_sample': nrm(ks[8], (DEC_BATCH, D_MODEL), 1.0),
        'norm_w': 1.0 + nrm(ks[9], (DEPTH, D_MODEL), 0.02),
        'w_ada': nrm(ks[10], (DEPTH, D_MODEL, 3 * D_MODEL), 0.5 * D_MODEL ** -0.5),
        'b_ada': nrm(ks[11], (DEPTH, 3 * D_MODEL), 0.01),
        'w_in': nrm(ks[12], (DEPTH, D_MODEL, IN_WIDTH), D_MODEL ** -0.5),
        'cmp_pos': nrm(ks[13], (DEPTH, 2, CMP_LEN, HEAD_DIM), 0.1),
        'cmp_w1': nrm(ks[14], (DEPTH, 2, CMP_LEN, HEAD_DIM, CMP_HIDDEN), (CMP_LEN * HEAD_DIM) ** -0.5),
        'cmp_b1': nrm(ks[15], (DEPTH, 2, CMP_HIDDEN), 0.01),
        'cmp_w2': nrm(ks[16], (DEPTH, 2, CMP_HIDDEN, HEAD_DIM), CMP_HIDDEN ** -0.5),
        'cmp_b2': nrm(ks[17], (DEPTH, 2, HEAD_DIM), 0.01),
        'w_a2': nrm(ks[18], (DEPTH, GLA_RANK, GLA_KEY_WIDTH), GLA_RANK ** -0.5),
        'b_a': nrm(ks[19], (DEPTH, GLA_KEY_WIDTH), 0.1),
        'gla_norm_w': 1.0 + nrm(ks[20], (DEPTH, GLA_DV), 0.02),
        'w_o_nsa': nrm(ks[21], (DEPTH, NSA_WIDTH, D_MODEL), NSA_WIDTH ** -0.5),
        'w_o_gla': nrm(ks[22], (DEPTH, GLA_WIDTH, D_MODEL), GLA_WIDTH ** -0.5),
        'w_out': nrm(ks[23], (DEPTH, D_MODEL, D_MODEL), D_MODEL ** -0.5),
        'final_norm_w': 1.0 + nrm(ks[24], (D_MODEL,), 0.02),
    }


def reference(x_prompt, x_sample, cache_kv_cmp, cache_kv_sel, cache_kv_win, state_gla, page_table,
              c_prompt, c_sample, norm_w, w_ada, b_ada, w_in, cmp_pos, cmp_w1, cmp_b1, cmp_w2, cmp_b2,
              w_a2, b_a, gla_norm_w, w_o_nsa, w_o_gla, w_out, final_norm_w):
    t_p = x_prompt.shape[1]
    t_s = x_sample.shape[1]
    past = page_table.shape[1] * cache_kv_cmp.shape[2]
    wb = cache_kv_win.shape[2]
    pos_p = jnp.arange(t_p, dtype=jnp.int32)
    pos_s = past + jnp.arange(t_s, dtype=jnp.int32)
    s0_p = jnp.zeros((x_prompt.shape[0], GLA_HEADS, GLA_DK, GLA_DV), jnp.float32)
    xp, xs = x_prompt, x_sample
    cmp_p, cmp_s, sel_p, sel_s, win_p, win_s, st_p, st_s = [], [], [], [], [], [], [], []
    for l in range(DEPTH):
        cmp_l = (cmp_pos[l], cmp_w1[l], cmp_b1[l], cmp_w2[l], cmp_b2[l])
        shared = (norm_w[l], w_ada[l], b_ada[l], w_in[l], w_a2[l], b_a[l], gla_norm_w[l],
                  w_o_nsa[l], w_o_gla[l], w_out[l])
        nsa_p = lambda q, kc, ks, kw: nsa_prompt(q, kc, ks, kw, *cmp_l)
        xp, kvc_p, kvs_p, kvw_p, sp = sublayer(xp, c_prompt, pos_p, s0_p, nsa_p, *shared)
        nsa_s = lambda q, kc, ks, kw: nsa_sample(q, kc, ks, kw, cache_kv_cmp[l], cache_kv_sel[l],
                                                 cache_kv_win[l], page_table, *cmp_l)
        xs, kvc_s, kvs_s, kvw_s, ss = sublayer(xs, c_sample, pos_s, state_gla[l], nsa_s, *shared)
        cmp_p.append(kvc_p)
        cmp_s.append(kvc_s)
        sel_p.append(kvs_p)
        sel_s.append(kvs_s)
        win_p.append(kvw_p[:, t_p - min(WINDOW, t_p):])
        win_s.append(jnp.concatenate([cache_kv_win[l], kvw_s], axis=1)[:, t_s:])
        st_p.append(sp.astype(state_gla.dtype))
        st_s.append(ss.astype(state_gla.dtype))
    y_prompt = rms_norm(xp, final_norm_w)
    y_sample = rms_norm(xs, final_norm_w)
    kv_cmp_prompt = jnp.stack(cmp_p, axis=0)
    kv_cmp_sample = jnp.stack(cmp_s, axis=0)
    kv_sel_prompt = jnp.stack(sel_p, axis=0)
    kv_sel_sample = jnp.stack(sel_s, axis=0)
    kv_win_prompt = jnp.stack(win_p, axis=0)
    kv_win_sample = jnp.stack(win_s, axis=0)
    gla_state_prompt = jnp.stack(st_p, axis=0)
    gla_state_sample = jnp.stack(st_s, axis=0)
    return (y_prompt, y_sample, kv_cmp_prompt, kv_cmp_sample, kv_sel_prompt, kv_sel_sample,
            kv_win_prompt, kv_win_sample, gla_state_prompt, gla_state_sample)
```

```python
import numpy as np
from contextlib import ExitStack
import concourse.bass as bass
import concourse.mybir as mybir
from concourse.bass_utils import run_bass_kernel_spmd

F32 = mybir.dt.float32
BF16 = mybir.dt.bfloat16
I32 = mybir.dt.int32
U8 = mybir.dt.uint8
AF = mybir.ActivationFunctionType
ALU = mybir.AluOpType

NCORES = 8
D = 2048
KC = 16
SEQ = 8192
NTG = SEQ // 128
NSLOT = 8
INW = 10816
C_QN, C_KV, C_GN, C_ZN, C_QG, C_KG, C_VG, C_AG, C_ZG, C_MN, C_MG = (
    0, 1024, 2560, 2608, 3632, 4144, 4656, 5680, 5696, 6720, 8768)
EPS = 1e-6
NEGB = -30000.0
BIG = 1e30


class Buf:
    __slots__ = ("name", "lw", "rd", "sem", "cnt", "lastop")

    def __init__(self, name):
        self.name = name
        self.lw = None
        self.rd = []
        self.sem = None
        self.cnt = 0
        self.lastop = None


class Op:
    __slots__ = ("eng", "fn", "deps", "signal", "idx", "is_dma", "dsem", "dval", "phase")

    def __init__(self, eng, fn, is_dma=False):
        self.eng = eng
        self.fn = fn
        self.deps = []
        self.signal = False
        self.idx = 0
        self.is_dma = is_dma
        self.dsem = None
        self.dval = 0


ENGS = ("pe", "act", "dve", "pool", "sp")


class TT:
    __slots__ = ("ap", "b")

    def __init__(self, ap, b):
        self.ap = ap
        self.b = b

    def __getitem__(self, k):
        return self.ap[k]


def _b(x):
    return x.b if isinstance(x, TT) else x


class Prog:
    def __init__(self, nc, es):
        self.nc = nc
        self.es = es
        self.ops = {e: [] for e in ENGS}
        self.dma_bufs = []
        self.nbuf = 0
        self.last = {e: None for e in ENGS}
        self.bar_deps = []
        self.bar_id = 0
        self.bar_seen = {e: 0 for e in ENGS}
        self.free_sems = []
        self.sem_tot = {}
        self.phase = "p0"

    def buf(self, name=None):
        self.nbuf += 1
        return Buf(name or f"b{self.nbuf}")

    def barrier(self):
        deps = [op for op in self.last.values() if op is not None and not op.is_dma]
        for b in self.dma_bufs:
            if b.lastop is not None:
                deps.append(b.lastop)
            self.free_sems.append((b.sem, b.cnt))
            b.sem = None
            b.lastop = None
        self.dma_bufs = []
        self.bar_deps = deps
        self.bar_id += 1

    def _add(self, op, reads, writes):
        deps = []
        if self.bar_seen[op.eng] < self.bar_id:
            self.bar_seen[op.eng] = self.bar_id
            deps.extend(self.bar_deps)
        for b in reads:
            if b.lw is not None:
                deps.append(b.lw)
        for b in writes:
            if b.lw is not None:
                deps.append(b.lw)
            deps.extend(b.rd)
        seen = set()
        for d in deps:
            if d is op or id(d) in seen:
                continue
            seen.add(id(d))
            if d.eng == "pe" and op.eng == "pe" and not d.is_dma and not op.is_dma:
                continue
            op.deps.append(d)
            if not d.is_dma:
                d.signal = True
        for b in reads:
            b.rd.append(op)
        for b in writes:
            b.lw = op
            b.rd = []
        op.phase = self.phase
        self.ops[op.eng].append(op)
        if not op.is_dma:
            self.last[op.eng] = op
        return op

    def op(self, eng, fn, reads=(), writes=()):
        return self._add(Op(eng, fn), [_b(x) for x in reads], [_b(x) for x in writes])

    def dma(self, q, fn, reads=(), writes=()):
        reads = [_b(x) for x in reads]
        writes = [_b(x) for x in writes]
        op = Op(q, fn, is_dma=True)
        prim = writes[0] if writes else reads[0]
        if prim.sem is None:
            if self.free_sems:
                prim.sem, prim.cnt = self.free_sems.pop()
            else:
                prim.sem, prim.cnt = self.es.enter_context(self.nc.semaphore(f"d{len(self.sem_tot)}")), 0
            self.dma_bufs.append(prim)
        prim.cnt += 1
        op.dsem = prim.sem
        op.dval = 16 * prim.cnt
        self.sem_tot[id(prim.sem)] = (prim.sem, op.dval)
        r = self._add(op, reads, writes)
        prim.lastop = op
        return r

    def emit(self):
        nc, es = self.nc, self.es
        psem = {e: es.enter_context(nc.semaphore(f"p_{e}")) for e in ENGS}
        for e in ENGS:
            k = 0
            for op in self.ops[e]:
                if op.signal:
                    k += 1
                    op.idx = k
        block = es.enter_context(nc.Block())
        final = list(self.sem_tot.values())

        def run(ename, eng):
            known = {}
            for op in self.ops[ename]:
                for d in op.deps:
                    if d.is_dma:
                        s, v = d.dsem, d.dval
                    else:
                        s, v = psem[d.eng], d.idx
                    key = id(s)
                    if known.get(key, 0) >= v:
                        continue
                    known[key] = v
                    eng.wait_ge(s, v)
                ins = op.fn(eng)
                if op.is_dma:
                    ins.then_inc(op.dsem, 16)
                elif op.signal:
                    ins.then_inc(psem[ename], 1)
            if ename == "sp":
                for s, v in final:
                    if known.get(id(s), 0) < v:
                        eng.wait_ge(s, v)

        @block.tensor
        def _(eng):
            run("pe", eng)

        @block.scalar
        def _(eng):
            run("act", eng)

        @block.vector
        def _(eng):
            run("dve", eng)

        @block.gpsimd
        def _(eng):
            run("pool", eng)

        @block.sync
        def _(eng):
            run("sp", eng)


class Arena:
    def __init__(self, P, h, lo, hi, name="a"):
        self.P, self.h, self.lo, self.hi, self.off, self.name = P, h, lo, hi, lo, name
        self.n = 0

    def alloc(self, shape, dt, name=None):
        isz = {F32: 4, BF16: 2, I32: 4, U8: 1}[dt]
        n = 1
        for s in shape[1:]:
            n *= s
        nb = (n * isz + 31) // 32 * 32
        assert self.off + nb <= self.hi, f"arena {self.name} overflow: need {nb} at {self.off} (hi {self.hi})"
        v = self.h[0:shape[0], self.off:self.off + n * isz].bitcast(dt)
        self.off += nb
        if len(shape) == 3:
            v = v.rearrange("p (a b) -> p a b", a=shape[1], b=shape[2])
        elif len(shape) == 4:
            v = v.rearrange("p (a b c) -> p a b c", a=shape[1], b=shape[2], c=shape[3])
        elif len(shape) == 5:
            v = v.rearrange("p (a b c d) -> p a b c d", a=shape[1], b=shape[2], c=shape[3], d=shape[4])
        self.n += 1
        return TT(v, self.P.buf(name or f"{self.name}{self.n}"))

    def sub(self, nbytes, name):
        nbytes = (nbytes + 31) // 32 * 32
        assert self.off + nbytes <= self.hi, f"arena {self.name} overflow (sub {name})"
        a = Arena(self.P, self.h, self.off, self.off + nbytes, name)
        self.off += nbytes
        return a

    def mark(self):
        return self.off

    def reset(self, m=None):
        self.off = self.lo if m is None else m


class K:
    def __init__(self, nc, es, dbg):
        self.nc, self.es, self.dbg = nc, es, dbg
        self.P = Prog(nc, es)
        self.dram = {}
        sb = es.enter_context(nc.sbuf_tensor("arena", [128, 192 * 1024], U8))
        self.A = Arena(self.P, sb, 0, 192 * 1024, "A")
        ps = es.enter_context(nc.psum_tensor("psum", [128, 8 * 512], F32))
        self.psb = [TT(ps[:, k * 512:(k + 1) * 512], self.P.buf(f"ps{k}")) for k in range(8)]
        self.rr = 0
        self.rot = list(range(8))

    def din(self, name, shape, dt=F32):
        t = self.nc.dram_tensor(name, list(shape), dt, kind="ExternalInput").ap()
        self.dram[name] = t
        return t

    def dout(self, name, shape, dt=F32):
        t = self.nc.dram_tensor(name, list(shape), dt, kind="ExternalOutput").ap()
        self.dram[name] = t
        return t

    def dscr(self, name, shape, dt):
        t = self.nc.dram_tensor(name, list(shape), dt, kind="Internal").ap()
        return TT(t, self.P.buf(name))

    def dump(self, name, t, shape, dt=F32):
        if not self.dbg.get("dumps"):
            return
        o = self.dout("dbg_" + name, shape, dt)
        self.dma("sp", o, t.ap if isinstance(t, TT) else t, [t], [])

    def ps(self):
        k = self.rot[self.rr % len(self.rot)]
        self.rr += 1
        return self.psb[k]

    def mm(self, out, lhsT, rhs, start, stop, R, W):
        self.P.op("pe", lambda e: e.matmul(out, lhsT=lhsT, rhs=rhs, start=start, stop=stop), R, W)

    def tr(self, out, in_, ident, R, W):
        self.P.op("pe", lambda e: e.transpose(out, in_, ident), R, W)

    def act(self, out, in_, func, R, W, **kw):
        self.P.op("act", lambda e: e.activation(out=out, in_=in_, func=func, **kw), R, W)

    def cp(self, eng, out, in_, R, W):
        if eng == "act":
            self.P.op("act", lambda e: e.copy(out=out, in_=in_), R, W)
        else:
            self.P.op(eng, lambda e: e.tensor_copy(out=out, in_=in_), R, W)

    def tt(self, eng, out, in0, in1, op, R, W):
        self.P.op(eng, lambda e: e.tensor_tensor(out=out, in0=in0, in1=in1, op=op), R, W)

    def ts(self, eng, out, in0, s1, s2, op0, op1, R, W):
        if op1 is None:
            self.P.op(eng, lambda e: e.tensor_scalar(out=out, in0=in0, scalar1=s1, scalar2=None, op0=op0), R, W)
        else:
            self.P.op(eng, lambda e: e.tensor_scalar(out=out, in0=in0, scalar1=s1, scalar2=s2, op0=op0, op1=op1), R, W)

    def stt(self, out, in0, scalar, in1, op0, op1, R, W):
        self.P.op("dve", lambda e: e.scalar_tensor_tensor(out=out, in0=in0, scalar=scalar, in1=in1, op0=op0, op1=op1), R, W)

    def memset(self, eng, ap, val, W):
        self.P.op(eng, lambda e: e.memset(ap, val), (), W)

    def dma(self, q, out, in_, R, W, **kw):
        self.P.dma(q, lambda e: e.dma_start(out=out, in_=in_, **kw), R, W)


def build(dbg=None):
    dbg = dbg or {}
    nc = bass.Bass("TRN2", target_bir_lowering=False)
    es = ExitStack()
    with es:
        k = K(nc, es, dbg)
        build_program(k)
        k.P.emit()
        if dbg.get("want_prog"):
            dbg["prog"] = k.P
    return nc


def build_program(k):
    P, A, dbg = k.P, k.A, k.dbg
    x_prompt = k.din("x_prompt", [SEQ, D])
    xl = k.din("xl", [NSLOT, 128, D])
    xs = k.din("xs", [32, D])
    c_all = k.din("c_all", [5, D])
    norm_w = k.din("norm_w", [1, D])
    w_ada = k.din("w_ada", [D, 3 * D])
    b_ada = k.din("b_ada", [1, 3 * D])
    w_in = k.din("w_in", [D, INW])
    w_a2 = k.din("w_a2", [16, 512])
    b_a = k.din("b_a", [1, 512])
    rope_g = k.din("rope_g", [SEQ, 64])
    mcol = k.din("mcol", [128, NTG])
    cU = k.din("cU", [128, 128])
    kv_cmp_p = k.dout("kv_cmp_p", [SEQ, 512])
    kv_sel_p = k.dout("kv_sel_p", [SEQ, 512])
    kv_win_p = k.dout("kv_win_p", [512, 512])
    gla_p = k.dout("gla_p", [128, 1024])

    Wb = k.dscr("Wb", [128, KC, INW], BF16)
    gsc = k.dscr("gsc", [5, D], F32)
    KTsel = k.dscr("KTsel", [128, 2, SEQ], BF16)
    KTwin = k.dscr("KTwin", [128, 2, SEQ], BF16)
    Vsel = k.dscr("Vsel", [SEQ, 260], BF16)
    Vwin = k.dscr("Vwin", [SEQ, 260], BF16)
    kvcT = k.dscr("kvcT", [128, 4, SEQ + 16], BF16)
    SS = k.dscr("SS", [NSLOT, 128, 1024], F32)

    ident_f = A.alloc([128, 128], F32, "ident_f")
    ident_b = A.alloc([128, 128], BF16, "ident_b")
    ones_f = A.alloc([128, 128], F32, "ones_f")
    Irep = A.alloc([128, 4, 128], BF16, "Irep")
    g1T = A.alloc([128, KC, 5], F32, "g1T")
    shT = A.alloc([128, KC, 5], F32, "shT")
    k.memset("pool", ones_f[:], 1.0, [ones_f])
    P.op("pool", lambda e: e.affine_select(out=ident_f[:], in_=ones_f[:], pattern=[[-1, 128]], compare_op=ALU.is_equal,
                                           fill=0.0, base=0, channel_multiplier=1), [ones_f], [ident_f])
    k.cp("dve", ident_b[:], ident_f[:], [ident_f], [ident_b])
    for g in range(4):
        k.cp("dve", Irep[:, g, :], ident_f[:], [ident_f], [Irep])
    pm = A.mark()

    c_nat = A.alloc([5, D], F32, "c_nat")
    cT = A.alloc([128, KC, 5], F32, "cT")
    modN = A.alloc([6, 3 * D], F32, "modN")
    bada = A.alloc([5, 3 * D], F32, "bada")
    wa = [A.alloc([128, 3072], F32, f"wa{i}") for i in range(2)]
    modT = A.alloc([128, 32, 6], F32, "modT")
    k.dma("sp", c_nat[:], c_all[:, :], [], [c_nat])
    k.dma("sp", bada[:], b_ada.partition_broadcast(5), [], [bada])
    k.memset("pool", modN[:], 0.0, [modN])
    pc = k.psb[7]
    for kc in range(KC):
        k.tr(pc[:, kc * 5:(kc + 1) * 5], c_nat[0:5, kc * 128:(kc + 1) * 128], ident_f[0:5, 0:5], [c_nat, ident_f], [pc])
    k.cp("dve", cT[:].rearrange("p a b -> p (a b)"), pc[:, 0:80], [pc], [cT])
    n = 0
    for half in range(2):
        for kc in range(KC):
            w = wa[n % 2]
            n += 1
            k.dma("sp", w[:], w_ada[kc * 128:(kc + 1) * 128, half * 3072:(half + 1) * 3072], [], [w])
            for g in range(6):
                k.mm(k.psb[g][0:5, :], cT[:, kc, :], w[:, g * 512:(g + 1) * 512], kc == 0, kc == KC - 1, [cT, w], [k.psb[g]])
        for g in range(6):
            c0 = half * 3072 + g * 512
            k.tt("dve", modN[0:5, c0:c0 + 512], k.psb[g][0:5, :], bada[:, c0:c0 + 512], ALU.add, [k.psb[g], bada], [modN])
    k.dma("sp", modN[5:6, 0:D], norm_w[:, :], [], [modN])
    k.ts("dve", modN[0:5, D:2 * D], modN[0:5, D:2 * D], 1.0, None, ALU.add, None, [modN], [modN])
    for blk in range(32):
        pb = k.psb[blk // 16]
        k.tr(pb[:, (blk % 16) * 6:(blk % 16) * 6 + 6], modN[0:6, blk * 128:(blk + 1) * 128], ident_f[0:6, 0:6], [modN, ident_f], [pb])
    for hb in range(2):
        k.cp("dve", modT[:, hb * 16:(hb + 1) * 16, :].rearrange("p a b -> p (a b)"), k.psb[hb][:, 0:96], [k.psb[hb]], [modT])
    k.tt("dve", g1T[:], modT[:, 16:32, 0:5], modT[:, 0:16, 5:6].broadcast_to([128, 16, 5]), ALU.mult, [modT], [g1T])
    k.cp("dve", shT[:], modT[:, 0:16, 0:5], [modT], [shT])
    k.dma("sp", gsc[:, :], modN[0:5, 2 * D:3 * D], [modN], [gsc])
    if dbg.get("mod"):
        o = k.dout("dbg_mod", [6, 3 * D])
        k.dma("sp", o[:, :], modN[:], [modN], [])
    P.barrier()
    A.reset(pm)

    P.phase = 'prep'
    w_o_nsa = k.din("w_o_nsa", [1024, D])
    w_o_gla = k.din("w_o_gla", [1024, D])
    w_out = k.din("w_out", [D, D])
    Won = k.dscr("Won", [128, 8, D], BF16)
    Wog = k.dscr("Wog", [128, 8, D], BF16)
    Wout = k.dscr("Wout", [128, KC, D], BF16)
    prep_n = [0]

    def prep_piece(stg, wbf, src, dst_ap, dst, kc, c0, c1):
        n_ = prep_n[0]
        prep_n[0] += 1
        s_, wb_ = stg[n_ % 2], wbf[n_ % 2]
        w = c1 - c0
        k.dma("sp", s_[:, 0:w], src[kc * 128:(kc + 1) * 128, c0:c1], [], [s_])
        k.cp("dve" if n_ % 2 == 0 else "act", wb_[:, 0:w], s_[:, 0:w], [s_], [wb_])
        k.dma("sp", dst_ap[:, kc, c0:c1], wb_[:, 0:w], [wb_], [dst])

    stg = [A.alloc([128, 2704], F32, f"stg{i}") for i in range(2)]
    wbf = [A.alloc([128, 2704], BF16, f"wbf{i}") for i in range(2)]
    for kc in range(KC):
        for (c0, c1) in ((C_KV, C_KV + 1536), (C_KG, C_KG + 1552)):
            prep_piece(stg, wbf, w_in, Wb.ap, Wb, kc, c0, c1)
    prepB = []
    for kc in range(KC):
        for (c0, c1) in ((0, 1024), (2560, 4144), (5696, 8256), (8256, INW)):
            prepB.append((w_in, Wb, kc, c0, c1))
    for src, dst, nk in ((w_o_nsa, Won, 8), (w_o_gla, Wog, 8), (w_out, Wout, KC)):
        for kc in range(nk):
            prepB.append((src, dst, kc, 0, D))
    P.barrier()
    A.reset(pm)

    P.phase = 'G'
    NG = 3088
    WG = A.alloc([128, KC, NG], BF16, "WG")
    k.dma("sp", WG[:, :, 0:1536], Wb[:, :, C_KV:C_KV + 1536], [Wb], [WG])
    k.dma("sp", WG[:, :, 1536:NG], Wb[:, :, C_KG:C_KG + 1552], [Wb], [WG])
    wa2b = A.alloc([17, 512], F32, "wa2b")
    k.dma("sp", wa2b[0:16, :], w_a2[:, :], [], [wa2b])
    k.dma("sp", wa2b[16:17, :], b_a[:, :], [], [wa2b])
    Ugt = A.alloc([128, 128], F32, "Ugt")
    k.dma("sp", Ugt[:], cU[:, :], [], [Ugt])
    negc = A.alloc([128, 1], F32, "negc")
    k.memset("pool", negc[:], -1.0 / 16.0, [negc])
    mc = A.alloc([128, NTG], F32, "mc")
    k.dma("sp", mc[:], mcol[:, :], [], [mc])
    S = A.alloc([128, 1024], F32, "S")
    k.memset("pool", S[:], 0.0, [S])
    Ssv = [A.alloc([128, 1024], F32, f"Ssv{i}") for i in range(2)]
    xt = [A.alloc([128, D], F32, f"xt{i}") for i in range(2)]
    xn = [A.alloc([128, D], BF16, f"xn{i}") for i in range(2)]
    hT = [A.alloc([128, KC, 128], BF16, f"hT{i}") for i in range(2)]
    cs = [A.alloc([128, 64], F32, f"cs{i}") for i in range(2)]
    st = [A.alloc([128, 8], F32, f"st{i}") for i in range(2)]
    kvst = [[A.alloc([128, 512], F32, f"kvst{i}_{b}") for b in range(3)] for i in range(2)]
    rt = [A.alloc([128, 4, 4, 32], F32, f"rt{i}") for i in range(2)]
    c16 = [A.alloc([128, 3, 512], BF16, f"c16_{i}") for i in range(2)]
    tT = [A.alloc([128, 8, 128], BF16, f"tT{i}") for i in range(2)]
    vaug = [A.alloc([128, 2, 4, 65], BF16, f"vaug{i}") for i in range(2)]
    agT = [A.alloc([17, 128], F32, f"agT{i}") for i in range(2)]
    ex = A.alloc([128, 512], F32, "ex")
    L = [A.alloc([128, 512], F32, f"L{i}") for i in range(2)]
    E = A.alloc([128, 512], F32, "E")
    Kd2 = [A.alloc([128, 512], BF16, f"Kd2_{i}") for i in range(2)]
    Vb = [A.alloc([128, 1024], BF16, f"Vb{i}") for i in range(2)]
    dec = [A.alloc([128, 4], F32, f"dec{i}") for i in range(2)]
    kgs = [A.alloc([128, 512], BF16, f"kgs{i}") for i in range(2)]
    for i in range(2):
        k.memset("pool", vaug[i][:], 1.0, [vaug[i]])
        k.memset("pool", agT[i][:], 1.0, [agT[i]])
    ntg = dbg.get("ntg", NTG)

    def g_pre(i):
        p = i % 2
        r0 = i * 128
        X, XN, H, CS, ST, RT = xt[p], xn[p], hT[p], cs[p], st[p], rt[p]
        k.dma("sp", X[:], x_prompt[r0:r0 + 128, :], [], [X])
        k.dma("sp", CS[:], rope_g[r0:r0 + 128, :], [], [CS])
        k.act(XN[:], X[:], AF.Square, [X], [XN, ST], accum_out=ST[:, 0:1])
        k.ts("dve", ST[:, 1:2], ST[:, 0:1], 1.0 / D, EPS, ALU.mult, ALU.add, [ST], [ST])
        k.act(ST[:, 2:3], ST[:, 1:2], AF.Sqrt, [ST], [ST])
        P.op("dve", lambda e, ST=ST: e.reciprocal(out=ST[:, 3:4], in_=ST[:, 2:3]), [ST], [ST])
        k.ts("dve", XN[:], X[:], ST[:, 3:4], None, ALU.mult, None, [X, ST], [XN])

    def g_pre_b(i):
        p = i % 2
        XN, H = xn[p], hT[p]
        for q4 in range(4):
            pb = k.ps()
            pbv = pb[:].bitcast(BF16)
            for u in range(4):
                kc = q4 * 4 + u
                k.tr(pbv[:, u * 128:(u + 1) * 128], XN[:, kc * 128:(kc + 1) * 128], ident_b[:], [XN, ident_b], [pb])
            for u in range(4):
                kc = q4 * 4 + u
                k.act(H[:, kc, :], pbv[:, u * 128:(u + 1) * 128], AF.Identity, [pb, g1T, shT], [H],
                      scale=g1T[:, kc, 0:1], bias=shT[:, kc, 0:1])

    def g_tailA(i):
        p = i % 2
        LL, KD, DC = L[p], Kd2[p], dec[p]
        pd = k.ps()
        k.mm(pd[:], Ugt[:], LL[:], True, True, [Ugt, LL], [pd])
        k.act(E[:], pd[:], AF.Exp, [pd], [E])
        k.tt("dve", KD[:], kgs[p][:], E[:], ALU.mult, [kgs[p], E], [KD])
        pl = k.ps()
        for h in range(4):
            k.mm(pl[:, h:h + 1], LL[:, h * 128:(h + 1) * 128], negc[:, 0:1], True, True, [LL, negc], [pl])
        k.act(DC[:], pl[:, 0:4], AF.Exp, [pl], [DC])

    def g_tailB(i):
        p = i % 2
        KD, VB, DC = Kd2[p], Vb[p], dec[p]
        SV = Ssv[(i // 8) % 2]
        if i % 8 == 0:
            k.ts("pool", SV[:], S[:], mc[:, i:i + 1], None, ALU.mult, None, [S, mc], [SV])
        else:
            k.stt(SV[:], S[:], mc[:, i:i + 1], SV[:], ALU.mult, ALU.add, [S, mc, SV], [SV])
        if i % 8 == 7:
            k.dma("pool", SS.ap[i // 8], SV[:], [SV], [SS])
        for hp in range(2):
            pb = k.ps()
            for u in range(2):
                h = hp * 2 + u
                k.mm(pb[:, u * 256:(u + 1) * 256], KD[:, h * 128:(h + 1) * 128], VB[:, h * 256:(h + 1) * 256], True, True, [KD, VB], [pb])
            for u in range(2):
                h = hp * 2 + u
                k.stt(S[:, h * 256:(h + 1) * 256], S[:, h * 256:(h + 1) * 256], DC[:, h:h + 1], pb[:, u * 256:(u + 1) * 256],
                      ALU.mult, ALU.add, [S, DC, pb], [S])

    g_pre(0)
    g_pre_b(0)
    for i in range(ntg):
        if i + 1 < ntg:
            g_pre(i + 1)
        p = i % 2
        r0 = i * 128
        X, XN, H, CS, ST, RT = xt[p], xn[p], hT[p], cs[p], st[p], rt[p]
        pa = k.ps()
        for kc in range(KC):
            k.mm(pa[0:16, 0:128], WG[:, kc, 3072:3088], H[:, kc, :], kc == 0, kc == KC - 1, [WG, H], [pa])
        AG = agT[p]
        k.cp("dve", AG[0:16, :], pa[0:16, 0:128], [pa], [AG])
        if i > 0:
            g_tailA(i - 1)
        pk = [None] * 6
        for n_, g in enumerate((3, 4, 5, 0, 1, 2)):
            if n_ == 3:
                VB = Vb[p]
                k.cp("act", VB[:, 0:512], pk[4][:], [pk[4]], [VB])
                k.cp("act", VB[:, 512:1024], pk[5][:], [pk[5]], [VB])
                k.cp("dve", kgs[p][:], pk[3][:], [pk[3]], [kgs[p]])
                if i + 1 < ntg:
                    g_pre_b(i + 1)
            pb = k.ps()
            for kc in range(KC):
                k.mm(pb[:], H[:, kc, :], WG[:, kc, g * 512:(g + 1) * 512], kc == 0, kc == KC - 1, [WG, H], [pb])
            pk[g] = pb
        cosb = CS[:, 0:32].unsqueeze(1).broadcast_to([128, 4, 32])
        sinb = CS[:, 32:64].unsqueeze(1).broadcast_to([128, 4, 32])
        C16 = c16[p]
        for br in range(3):
            pb = pk[br]
            KS = kvst[p][br]
            x1 = pb[:, 0:256].rearrange("p (h t e) -> p h t e", h=4, t=2, e=32)[:, :, 0, :]
            x2 = pb[:, 0:256].rearrange("p (h t e) -> p h t e", h=4, t=2, e=32)[:, :, 1, :]
            o1 = KS[:, 0:256].rearrange("p (h t e) -> p h t e", h=4, t=2, e=32)[:, :, 0, :]
            o2 = KS[:, 0:256].rearrange("p (h t e) -> p h t e", h=4, t=2, e=32)[:, :, 1, :]
            k.tt("dve", RT[:, 0], x1, cosb, ALU.mult, [pb, CS], [RT])
            k.tt("dve", RT[:, 1], x2, sinb, ALU.mult, [pb, CS], [RT])
            k.tt("dve", RT[:, 2], x2, cosb, ALU.mult, [pb, CS], [RT])
            k.tt("dve", RT[:, 3], x1, sinb, ALU.mult, [pb, CS], [RT])
            k.tt("pool", o1, RT[:, 0], RT[:, 1], ALU.subtract, [RT], [KS])
            k.tt("pool", o2, RT[:, 2], RT[:, 3], ALU.add, [RT], [KS])
            k.cp("act", KS[:, 256:512], pb[:, 256:512], [pb], [KS])
            k.cp("pool", C16[:, br, :], KS[:], [KS], [C16])
        k.dma("pool", kv_cmp_p[r0:r0 + 128, :], kvst[p][0][:], [kvst[p][0]], [])
        k.dma("pool", kv_sel_p[r0:r0 + 128, :], kvst[p][1][:], [kvst[p][1]], [])
        if i >= NTG - 4:
            w0 = (i - (NTG - 4)) * 128
            k.dma("pool", kv_win_p[w0:w0 + 128, :], kvst[p][2][:], [kvst[p][2]], [])
        VA = vaug[p]
        for j, br in enumerate((1, 2)):
            k.cp("pool", VA[:, j, :, 0:64], C16[:, br, 256:512].rearrange("p (h e) -> p h e", h=4), [C16], [VA])
        k.dma("pool", Vsel.ap[r0:r0 + 128, :], VA[:, 0].rearrange("p h e -> p (h e)"), [VA], [Vsel])
        k.dma("pool", Vwin.ap[r0:r0 + 128, :], VA[:, 1].rearrange("p h e -> p (h e)"), [VA], [Vwin])
        pb = k.ps()
        pbv = pb[:].bitcast(BF16)
        srcs = [(0, 0), (0, 128), (0, 256), (0, 384), (1, 0), (1, 128), (2, 0), (2, 128)]
        for u, (br, c0) in enumerate(srcs):
            k.tr(pbv[:, u * 128:(u + 1) * 128], C16[:, br, c0:c0 + 128], ident_b[:], [C16, ident_b], [pb])
        TTl = tT[p]
        k.cp("act", TTl[:].rearrange("p a b -> p (a b)"), pbv[:, 0:1024], [pb], [TTl])
        k.dma("pool", kvcT.ap[:, :, r0:r0 + 128], TTl[:, 0:4, :], [TTl], [kvcT])
        k.dma("pool", KTsel.ap[:, :, r0:r0 + 128], TTl[:, 4:6, :], [TTl], [KTsel])
        k.dma("pool", KTwin.ap[:, :, r0:r0 + 128], TTl[:, 6:8, :], [TTl], [KTwin])
        pp = k.ps()
        k.mm(pp[:], AG[:, :], wa2b[:, :], True, True, [AG, wa2b], [pp])
        k.act(ex[:], pp[:], AF.Exp, [pp], [ex], scale=-1.0)
        LL = L[p]
        k.act(LL[:], ex[:], AF.Ln, [ex, ones_f], [LL], bias=ones_f[:, 0:1], scale=1.0)
        if i > 0:
            g_tailB(i - 1)
    g_tailA(ntg - 1)
    g_tailB(ntg - 1)
    k.dma("pool", gla_p[:, :], S[:], [S], [])
    P.barrier()
    A.reset(pm)
    if dbg.get("stop_after_g"):
        return
    env = dict(ident_f=ident_f, ident_b=ident_b, ones_f=ones_f, Irep=Irep, g1T=g1T, shT=shT,
               Wb=Wb, gsc=gsc, KTsel=KTsel, KTwin=KTwin, Vsel=Vsel, Vwin=Vwin, kvcT=kvcT, SS=SS,
               Won=Won, Wog=Wog, Wout=Wout, xl=xl, xs=xs, w_a2=w_a2, b_a=b_a, prepB=prepB, prep_piece=prep_piece)
    phase_rest(k, env)


O_UGT, O_UGE, O_MTRI, O_UGE32, O_MTRI32, O_UGT32 = 0, 128, 256, 384, 416, 448
O_NEGIND, O_IND, O_INDROW, O_NEWB, O_QLOC, O_QPS, O_NB30, O_IOTA = 480, 484, 488, 616, 648, 649, 650, 654
NCST = O_IOTA + 1536
TINY = 1e-30


def phase_rest(k, env):
    P, A, dbg = k.P, k.A, k.dbg
    ident_f, ident_b, ones_f, Irep, g1T, shT = (env[n] for n in ("ident_f", "ident_b", "ones_f", "Irep", "g1T", "shT"))
    Wb, gsc, KTsel, KTwin, Vsel, Vwin, kvcT, SS = (env[n] for n in ("Wb", "gsc", "KTsel", "KTwin", "Vsel", "Vwin", "kvcT", "SS"))
    Won, Wog, Wout, xl, xs, w_a2, b_a = (env[n] for n in ("Won", "Wog", "Wout", "xl", "xs", "w_a2", "b_a"))
    cmp_pos = k.din("cmp_pos", [64, 64])
    cmp_w1 = k.din("cmp_w1", [2, 32, 64, 128])
    cmp_b1 = k.din("cmp_b1", [2, 128])
    cmp_w2 = k.din("cmp_w2", [2, 128, 64])
    cmp_b2 = k.din("cmp_b2", [2, 64])
    gnw_d = k.din("gla_norm_w", [1, 256])
    fnw_d = k.din("final_norm_w", [1, D])
    cache_cmp = k.din("cache_cmp", [327680, 512])
    cache_sel = k.din("cache_sel", [327680, 512])
    cache_win = k.din("cache_win", [4, 512, 512])
    state_gla = k.din("state_gla", [4, 4, 128, 256])
    page_table = k.din("page_table", [1, 256], I32)
    cst_d = k.din("cst", [128, NCST])
    cm2s = k.din("cm2s", [128, 4, 128])
    ropel = k.din("ropel", [9, 128, 64])
    ropes = k.din("ropes", [32, 64])
    y_l = k.dout("y_l", [NSLOT, 128, D])
    y_s = k.dout("y_s", [32, D])
    kv_cmp_s = k.dout("kv_cmp_s", [32, 512])
    kv_sel_s = k.dout("kv_sel_s", [32, 512])
    kv_win_s = k.dout("kv_win_s", [4, 512, 512])
    gla_s = k.dout("gla_s", [4, 128, 1024])
    KcS = k.dscr("KcS", [5, 128, 1024], BF16)
    VcS = k.dscr("VcS", [5, 128, 3088], BF16)
    hTs = k.dscr("hTs", [128, KC, 1056], BF16)

    cst = A.alloc([128, NCST], F32, "cst")
    k.dma("sp", cst[:], cst_d[:, :], [], [cst])
    idx = A.alloc([128, 256], I32, "idx")
    wa2b = A.alloc([17, 512], F32, "wa2b2")
    k.dma("sp", wa2b[0:16, :], w_a2[:, :], [], [wa2b])
    k.dma("sp", wa2b[16:17, :], b_a[:, :], [], [wa2b])
    pm2 = A.mark()
    iota = cst[:, O_IOTA:O_IOTA + 1536]
    qloc = cst[:, O_QLOC:O_QLOC + 1]

    P.phase = 'C'
    w1T = A.alloc([128, 2, 32, 128], BF16, "w1T")
    w1s = A.alloc([128, 32, 128], F32, "w1s")
    for x in range(2):
        for b in range(2):
            k.dma("sp", w1s[64 * b:64 * b + 64], cmp_w1[x].rearrange("l d e -> d l e"), [], [w1s])
        k.cp("dve", w1T[:, x], w1s[:], [w1s], [w1T])
    posn = A.alloc([64, 64], F32, "posn")
    k.dma("sp", posn[:], cmp_pos[:, :], [], [posn])
    pp = k.ps()
    k.tr(pp[0:64, 0:64], posn[:], ident_f[0:64, 0:64], [posn, ident_f], [pp])
    posT = A.alloc([64, 64], BF16, "posT")
    k.cp("dve", posT[:], pp[0:64, 0:64], [pp], [posT])
    pq = k.ps()
    for x in range(2):
        for l in range(32):
            k.mm(pq[:, x:x + 1], w1T[0:64, x, l, :], posT[0:64, x * 32 + l:x * 32 + l + 1], l == 0, l == 31, [w1T, posT], [pq])
    b1c = A.alloc([128, 2], F32, "b1c")
    k.dma("sp", b1c[:], cmp_b1.rearrange("x e -> e x"), [], [b1c], allow_slow_non_contiguous=True)
    posb = A.alloc([128, 2], F32, "posb")
    k.tt("dve", posb[:], pq[:, 0:2], b1c[:], ALU.add, [pq, b1c], [posb])
    w2s = A.alloc([128, 2, 64], F32, "w2s")
    k.dma("sp", w2s[:], cmp_w2.rearrange("x e d -> e x d"), [], [w2s])
    w2pad = A.alloc([128, 2, 128], BF16, "w2pad")
    k.memset("pool", w2pad[:], 0.0, [w2pad])
    k.cp("dve", w2pad[:, 0, 0:64], w2s[:, 0, :], [w2s], [w2pad])
    k.cp("dve", w2pad[:, 1, 64:128], w2s[:, 0, :], [w2s], [w2pad])
    w2v = A.alloc([128, 64], BF16, "w2v")
    k.cp("dve", w2v[:], w2s[:, 1, :], [w2s], [w2v])
    b2k = A.alloc([128, 1], F32, "b2k")
    for b in range(2):
        k.dma("sp", b2k[64 * b:64 * b + 64], cmp_b2[0:1, :].rearrange("o d -> d o"), [], [b2k], allow_slow_non_contiguous=True)
    b2v = A.alloc([128, 64], F32, "b2v")
    k.dma("sp", b2v[:], cmp_b2[1:2, :].partition_broadcast(128), [], [b2v])
    cm = A.alloc([128, 4, 128], F32, "cm")
    k.dma("sp", cm[:], cm2s[:, :, :], [], [cm])
    KcT = A.alloc([128, 2, 512], BF16, "KcT")
    Vca = A.alloc([128, 4, 4, 193], BF16, "Vca")
    k.memset("pool", Vca[:], 1.0, [Vca])
    for h in range(4):
        k.cp("pool", Vca[:, :, h, 65:193], cm[:], [cm], [Vca])
    kvT = A.alloc([128, 4, 16, 513], BF16, "kvT")
    kvg = [TT(kvT.ap, P.buf(f"kvTg{g}")) for g in range(9)]
    hid = [A.alloc([128, 512], BF16, f"hid{i}") for i in range(2)]
    hn = [0]

    def compress(dst):
        for x in range(2):
            for a in range(2):
                pK = k.ps() if x == 0 else None
                for b in range(2):
                    h = 2 * a + b
                    gi = x * 2 + a
                    ph = k.ps()
                    for l in range(32):
                        j, s_ = l // 16, l % 16
                        k.mm(ph[:], w1T[64 * b:64 * b + 64, x, l, :], kvT[64 * b:64 * b + 64, gi, s_, j:j + 512],
                             l == 0, l == 31, [w1T] + kvg, [ph])
                    H_ = hid[hn[0] % 2]
                    hn[0] += 1
                    k.act(H_[:], ph[:], AF.Silu, [ph, posb], [H_], bias=posb[:, x:x + 1])
                    k.memset("pool", H_[:, 511:512], 0.0, [H_])
                    if x == 0:
                        k.mm(pK[:], w2pad[:, b, :], H_[:], b == 0, b == 1, [w2pad, H_], [pK])
                    else:
                        pv = k.ps()
                        for ct in range(4):
                            k.mm(pv[:, ct * 64:(ct + 1) * 64], H_[:, ct * 128:(ct + 1) * 128], w2v[:], True, True, [H_, w2v], [pv])
                        k.tt("dve", Vca[:, :, h, 0:64], pv[:, 0:256].rearrange("p (c d) -> p c d", c=4),
                             b2v[:].unsqueeze(1).broadcast_to([128, 4, 64]), ALU.add, [pv, b2v], [Vca])
                if x == 0:
                    k.act(KcT[:, a, :], pK[:], AF.Identity, [pK, b2k], [KcT], bias=b2k[:, 0:1])
        k.dma("sp", KcS.ap[dst], KcT[:].rearrange("p a c -> p (a c)"), [KcT], [KcS])
        k.dma("sp", VcS.ap[dst], Vca[:].rearrange("p c h w -> p (c h w)"), [Vca], [VcS])

    kst = [A.alloc([128, 4, 1024], BF16, f"kst{i}") for i in range(2)]
    for g in range(8):
        KS_ = kst[g % 2]
        k.dma("sp", KS_[:], kvcT.ap[:, :, g * 1024:(g + 1) * 1024], [kvcT], [KS_])
        k.cp(("act", "dve", "pool")[g % 3], kvT[:, :, :, g * 64:(g + 1) * 64], KS_[:].rearrange("p g (c s) -> p g s c", s=16), [KS_], [kvg[g]])
    k.memset("pool", kvT[:, :, :, 512:513], 0.0, [kvg[8]])
    compress(0)

    P.phase = 'SC'
    ptb = A.alloc([128, 256], I32, "ptb")
    iop = A.alloc([128, 1], F32, "iop")
    k.dma("sp", ptb[:], page_table.partition_broadcast(128), [], [ptb])
    P.op("pool", lambda e: e.iota(iop[:], pattern=[[0, 1]], base=0, channel_multiplier=1, allow_small_or_imprecise_dtypes=True), [], [iop])
    k.ts("dve", idx[:], ptb[:], 128.0, iop[:, 0:1], ALU.mult, ALU.add, [ptb, iop], [idx])
    gb = [A.alloc([128, 512], F32, f"gb{i}") for i in range(8)]

    def gather(dst, cache, col):
        P.dma("pool", lambda e: e.indirect_dma_start(out=dst[:], out_offset=None, in_=cache[:, :],
                                                     in_offset=bass.IndirectOffsetOnAxis(ap=idx[:, col:col + 1], axis=0)), [idx.b], [dst.b])

    n = 0
    nsb = dbg.get("nsb", 4)
    stgB = [A.alloc([128, 2704], F32, f"stgB{i}") for i in range(2)]
    wbfB = [A.alloc([128, 2704], BF16, f"wbfB{i}") for i in range(2)]
    prepB = list(env["prepB"])

    pend = []
    pbn = [0]

    def prep_finish():
        if pend:
            s_, wb_, dst, kc, c0, c1 = pend.pop(0)
            w = c1 - c0
            k.cp("act", wb_[:, 0:w], s_[:, 0:w], [s_], [wb_])
            k.dma("act", dst.ap[:, kc, c0:c1], wb_[:, 0:w], [wb_], [dst])

    def prep_next():
        if prepB:
            src, dst, kc, c0, c1 = prepB.pop(0)
            s_, wb_ = stgB[pbn[0] % 2], wbfB[pbn[0] % 2]
            pbn[0] += 1
            k.dma("act", s_[:, 0:c1 - c0], src[kc * 128:(kc + 1) * 128, c0:c1], [], [s_])
            prep_finish()
            pend.append((s_, wb_, dst, kc, c0, c1))
        else:
            prep_finish()

    for b in range(nsb):
        for j in range(64):
            if j % 2 == 0:
                prep_next()
            G_ = gb[n % 8]
            gather(G_, cache_cmp, b * 64 + j)
            pb = k.ps()
            for u in range(4):
                k.tr(pb[:, u * 128:(u + 1) * 128], G_[:, u * 128:(u + 1) * 128], ident_f[:], [G_, ident_f], [pb])
            k.cp("dve", kvT[:, :, :, j * 8:(j + 1) * 8],
                 pb[:, 0:512].rearrange("p (g c s) -> p g s c", g=4, s=16), [pb], [kvg[j // 8]])
            n += 1
        compress(1 + b)
    while prepB or pend:
        prep_next()
    P.barrier()
    A.reset(pm2)

    P.phase = 'Learly'
    regC = A.sub(60 * 1024, "C")
    QT = regC.alloc([128, 2, 4, 1056], BF16, "QT")
    qgT = regC.alloc([128, 4, 1056], BF16, "qgT")
    kgT = regC.alloc([128, 4, 1056], BF16, "kgT")
    vg = regC.alloc([128, 9, 1024], BF16, "vg")
    agTa = regC.alloc([17, 1056], F32, "agTa")
    gsig = regC.alloc([128, 9, 48], F32, "gsig")
    KTn = regC.alloc([128, 2, 2, 32], BF16, "KTn")
    Van = regC.alloc([32, 2, 4, 65], BF16, "Van")
    kgN = regC.alloc([32, 512], BF16, "kgN")
    mB = A.mark()
    hTa = A.alloc([128, KC, 1056], BF16, "hTa")
    xt2 = [A.alloc([128, D], F32, f"xt2_{i}") for i in range(2)]
    xn2 = [A.alloc([128, D], BF16, f"xn2_{i}") for i in range(2)]
    st2 = [A.alloc([128, 8], F32, f"st2_{i}") for i in range(2)]
    rq = A.alloc([128, 9, 64], F32, "rq")
    rs = A.alloc([32, 64], F32, "rs")
    Wt = [A.alloc([128, KC, 512], BF16, f"Wt{i}") for i in range(2)]
    rtq = [A.alloc([128, 4, 8, 32], F32, f"rtq{i}") for i in range(2)]
    qrot = [A.alloc([128, 4, 2, 64], BF16, f"qrot{i}") for i in range(2)]
    kss = [A.alloc([32, 512], F32, f"kss{i}") for i in range(3)]
    c16s = [A.alloc([32, 512], BF16, f"c16s{i}") for i in range(3)]
    k.dma("sp", rq[:], ropel.rearrange("t p c -> p t c"), [], [rq])
    k.dma("sp", rs[:], ropes[:, :], [], [rs])
    k.memset("pool", agTa[:], 1.0, [agTa])
    k.memset("pool", Van[:], 1.0, [Van])
    TILES = [(t, 128, 128 * t) for t in range(8)] + [(8, 32, 1024)]
    for (t, NT, tok0) in TILES:
        X, XN, ST = xt2[t % 2], xn2[t % 2], st2[t % 2]
        if t < 8:
            k.dma("sp", X[:], xl[t], [], [X])
        else:
            k.dma("sp", X[0:32], xs[:, :], [], [X])
        k.act(XN[0:NT], X[0:NT], AF.Square, [X], [XN, ST], accum_out=ST[0:NT, 0:1])
        k.ts("dve", ST[0:NT, 1:2], ST[0:NT, 0:1], 1.0 / D, EPS, ALU.mult, ALU.add, [ST], [ST])
        k.act(ST[0:NT, 2:3], ST[0:NT, 1:2], AF.Sqrt, [ST], [ST])
        P.op("dve", lambda e, ST=ST, NT=NT: e.reciprocal(out=ST[0:NT, 3:4], in_=ST[0:NT, 2:3]), [ST], [ST])
        k.ts("dve", XN[0:NT], X[0:NT], ST[0:NT, 3:4], None, ALU.mult, None, [X, ST], [XN])
        for q4 in range(4):
            pb = k.ps()
            pbv = pb[:].bitcast(BF16)
            for u in range(4):
                kc = q4 * 4 + u
                k.tr(pbv[:, u * 128:u * 128 + NT], XN[0:NT, kc * 128:(kc + 1) * 128], ident_b[0:NT, 0:NT], [XN, ident_b], [pb])
            for u in range(4):
                kc = q4 * 4 + u
                if t < 8:
                    k.act(hTa[:, kc, tok0:tok0 + 128], pbv[:, u * 128:(u + 1) * 128], AF.Identity, [pb, g1T, shT], [hTa],
                          scale=g1T[:, kc, 0:1], bias=shT[:, kc, 0:1])
                else:
                    for b in range(4):
                        k.act(hTa[:, kc, 1024 + 8 * b:1032 + 8 * b], pbv[:, u * 128 + 8 * b:u * 128 + 8 * b + 8], AF.Identity,
                              [pb, g1T, shT], [hTa], scale=g1T[:, kc, 1 + b:2 + b], bias=shT[:, kc, 1 + b:2 + b])
    wn = [0]

    def loadW(c0, n):
        W_ = Wt[wn[0] % 2]
        wn[0] += 1
        k.dma("sp", W_[:, :, 0:n], Wb.ap[:, :, c0:c0 + n], [Wb], [W_])
        return W_

    def proj_nat(c0, n, tiles, consumer):
        W_ = loadW(c0, n)
        for (t, NT, tok0) in tiles:
            pb = k.ps()
            for kc in range(KC):
                k.mm(pb[0:NT, 0:n], hTa[:, kc, tok0:tok0 + NT], W_[:, kc, 0:n], kc == 0, kc == KC - 1, [hTa, W_], [pb])
            consumer(t, NT, tok0, pb)

    CH = [(0, 512), (512, 512), (1024, 32)]

    def proj_T(W_, cc0, n, consumer):
        for (t0, nt) in CH:
            pb = k.ps()
            for kc in range(KC):
                k.mm(pb[0:n, 0:nt], W_[:, kc, cc0:cc0 + n], hTa[:, kc, t0:t0 + nt], kc == 0, kc == KC - 1, [hTa, W_], [pb])
            consumer(t0, nt, pb)

    rn = [0]

    def q_cons(a):
        def f(t, NT, tok0, pb):
            RT, QR = rtq[rn[0] % 2], qrot[rn[0] % 2]
            rn[0] += 1
            cosb = rq[0:NT, t, 0:32].unsqueeze(1).broadcast_to([NT, 8, 32])
            sinb = rq[0:NT, t, 32:64].unsqueeze(1).broadcast_to([NT, 8, 32])
            xv = pb[0:NT, 0:512].rearrange("p (h t e) -> p h t e", h=8, t=2, e=32)
            x1, x2 = xv[:, :, 0, :], xv[:, :, 1, :]
            k.tt("dve", RT[0:NT, 0], x1, cosb, ALU.mult, [pb, rq], [RT])
            k.tt("dve", RT[0:NT, 1], x2, sinb, ALU.mult, [pb, rq], [RT])
            k.tt("dve", RT[0:NT, 2], x2, cosb, ALU.mult, [pb, rq], [RT])
            k.tt("dve", RT[0:NT, 3], x1, sinb, ALU.mult, [pb, rq], [RT])
            qrv = QR[0:NT].rearrange("p g b (t e) -> p b g t e", t=2, e=32)
            rv = [RT[0:NT, i].rearrange("p (b g) e -> p b g e", b=2) for i in range(4)]
            k.tt("pool", qrv[:, :, :, 0, :], rv[0], rv[1], ALU.subtract, [RT], [QR])
            k.tt("pool", qrv[:, :, :, 1, :], rv[2], rv[3], ALU.add, [RT], [QR])
            pq_ = k.ps()
            pqv = pq_[:].bitcast(BF16)
            for g in range(4):
                k.tr(pqv[:, g * 128:g * 128 + NT], QR[0:NT, g].rearrange("p b d -> p (b d)"), ident_b[0:NT, 0:NT], [QR, ident_b], [pq_])
            k.cp("act", QT[:, a, :, tok0:tok0 + NT], pqv[:, 0:512].rearrange("p (g t) -> p g t", g=4)[:, :, 0:NT], [pq_], [QT])
        return f

    for a in range(2):
        proj_nat(C_QN + a * 512, 512, TILES, q_cons(a))
    proj_nat(C_GN, 48, TILES, lambda t, NT, tok0, pb: k.act(gsig[0:NT, t, :], pb[0:NT, 0:48], AF.Sigmoid, [pb], [gsig]))
    W_ = loadW(C_AG, 16)
    proj_T(W_, 0, 16, lambda t0, nt, pb: k.cp("dve", agTa[0:16, t0:t0 + nt], pb[0:16, 0:nt], [pb], [agTa]))
    W_ = loadW(C_QG, 512)
    for h in range(4):
        proj_T(W_, h * 128, 128, lambda t0, nt, pb, h=h: k.act(qgT[:, h, t0:t0 + nt], pb[:, 0:nt], AF.Identity, [pb], [qgT], scale=128.0 ** -0.5))
    W_ = loadW(C_KG, 512)
    for h in range(4):
        proj_T(W_, h * 128, 128, lambda t0, nt, pb, h=h: k.cp("act", kgT[:, h, t0:t0 + nt], pb[:, 0:nt], [pb], [kgT]))
    STILE = [TILES[8]]
    proj_nat(C_KG, 512, STILE, lambda t, NT, tok0, pb: k.cp("act", kgN[:], pb[0:32, :], [pb], [kgN]))
    for g2 in range(2):
        proj_nat(C_VG + g2 * 512, 512, TILES,
                 lambda t, NT, tok0, pb, g2=g2: k.cp("act" if t % 2 == 0 else "dve", vg[0:NT, t, g2 * 512:(g2 + 1) * 512], pb[0:NT, :], [pb], [vg]))

    def kv_cons(br):
        def f(t, NT, tok0, pb):
            RT = rtq[0]
            KS = kss[br]
            cosb = rs[:, 0:32].unsqueeze(1).broadcast_to([32, 4, 32])
            sinb = rs[:, 32:64].unsqueeze(1).broadcast_to([32, 4, 32])
            xv = pb[0:32, 0:256].rearrange("p (h t e) -> p h t e", h=4, t=2, e=32)
            ov = KS[:, 0:256].rearrange("p (h t e) -> p h t e", h=4, t=2, e=32)
            x1, x2 = xv[:, :, 0, :], xv[:, :, 1, :]
            k.tt("dve", RT[0:32, 0, 0:4], x1, cosb, ALU.mult, [pb, rs], [RT])
            k.tt("dve", RT[0:32, 1, 0:4], x2, sinb, ALU.mult, [pb, rs], [RT])
            k.tt("dve", RT[0:32, 2, 0:4], x2, cosb, ALU.mult, [pb, rs], [RT])
            k.tt("dve", RT[0:32, 3, 0:4], x1, sinb, ALU.mult, [pb, rs], [RT])
            k.tt("pool", ov[:, :, 0, :], RT[0:32, 0, 0:4], RT[0:32, 1, 0:4], ALU.subtract, [RT], [KS])
            k.tt("pool", ov[:, :, 1, :], RT[0:32, 2, 0:4], RT[0:32, 3, 0:4], ALU.add, [RT], [KS])
            k.cp("act", KS[:, 256:512], pb[0:32, 256:512], [pb], [KS])
            if br == 0:
                k.dma("sp", kv_cmp_s[:, :], KS[:], [KS], [])
            elif br == 1:
                k.dma("sp", kv_sel_s[:, :], KS[:], [KS], [])
            else:
                for b in range(4):
                    k.dma("sp", kv_win_s[b, 504:512, :], KS[8 * b:8 * b + 8, :], [KS], [])
            if br >= 1:
                j = br - 1
                C_ = c16s[br]
                k.cp("dve", C_[:], KS[:], [KS], [C_])
                k.cp("pool", Van[:, j, :, 0:64], C_[:, 256:512].rearrange("p (h e) -> p h e", h=4), [C_], [Van])
                pv = k.ps()
                pvv = pv[:].bitcast(BF16)
                for a in range(2):
                    k.tr(pvv[:, a * 32:(a + 1) * 32], C_[0:32, a * 128:(a + 1) * 128], ident_b[0:32, 0:32], [C_, ident_b], [pv])
                k.cp("act", KTn[:, j, :, :], pvv[:, 0:64].rearrange("p (a t) -> p a t", a=2), [pv], [KTn])
        return f

    for br in range(3):
        proj_nat(C_KV + br * 512, 512, STILE, kv_cons(br))
    ddb = P.buf("dram2dram")
    for b in range(4):
        k.dma("sp", kv_win_s[b, 0:504, :], cache_win[b, 8:512, :], [ddb], [])
    k.dma("sp", hTs.ap[:, :, :], hTa[:], [hTa], [hTs])
    P.barrier()
    A.reset(mB)
    phase_attn(k, env, locals())


def phase_attn(k, env, L_):
    P, A, dbg = k.P, k.A, k.dbg
    ident_f, ident_b, ones_f, Irep = (env[n] for n in ("ident_f", "ident_b", "ones_f", "Irep"))
    KTsel, KTwin, Vsel, Vwin, SS = (env[n] for n in ("KTsel", "KTwin", "Vsel", "Vwin", "SS"))
    cst, idx, wa2b, iota, qloc = (L_[n] for n in ("cst", "idx", "wa2b", "iota", "qloc"))
    QT, qgT, kgT, vg, agTa, gsig, KTn, Van, kgN = (L_[n] for n in ("QT", "qgT", "kgT", "vg", "agTa", "gsig", "KTn", "Van", "kgN"))
    KcS, VcS, cache_sel, cache_win, state_gla, gla_s, gnw_d = (L_[n] for n in ("KcS", "VcS", "cache_sel", "cache_win", "state_gla", "gla_s", "gnw_d"))
    psb = k.psb
    onsa_s = k.dscr("onsa_s", [9, 128, 1024], BF16)
    ogla_s = k.dscr("ogla_s", [9, 128, 1024], BF16)
    L_["onsa_s"], L_["ogla_s"] = onsa_s, ogla_s
    mX = A.mark()
    on1 = [A.alloc([128, 1024], BF16, f"on1_{i}") for i in range(1)]
    og1 = [A.alloc([128, 1024], BF16, f"og1_{i}") for i in range(1)]
    KTc = [A.alloc([128, 2, 512], BF16, f"KTc{i}") for i in range(2)]
    Vc = [A.alloc([128, 4, 260], BF16, f"Vc{i}") for i in range(2)]
    PT = [A.alloc([128, 512], BF16, f"PT{i}") for i in range(6)]
    accS = A.alloc([128, 16, 65], F32, "accS")
    accC = A.alloc([128, 4, 193], F32, "accC")
    accM = A.alloc([128, 4, 65], F32, "accM")
    accT = A.alloc([65, 4, 512], F32, "accT")
    KcP = A.alloc([128, 2, 512], BF16, "KcP")
    VcP = A.alloc([128, 4, 4, 193], BF16, "VcP")
    KcB = [KcP]
    VcB = [VcP]
    cmpb = [A.alloc([128, 512], BF16, f"cmpb{i}") for i in range(1)]
    base511 = A.alloc([32, 512], BF16, "base511")
    causalS = A.alloc([128, 1024], BF16, "causalS")
    winb = A.alloc([128, 1536], BF16, "winb")
    wbase = A.alloc([32, 512], BF16, "wbase")
    wbs = [A.alloc([32, 512], BF16, f"wbs{i}") for i in range(1)]
    newb16 = A.alloc([32, 32], BF16, "newb16")
    NS = [A.alloc([128, 4, 128], BF16, f"NS{i}") for i in range(2)]
    biasx = [A.alloc([128, 128], BF16, f"biasx{i}") for i in range(8)]
    qpj = A.alloc([128, 2], F32, "qpj")
    dq = A.alloc([128, 128], F32, "dq")
    f1 = A.alloc([128, 128], F32, "f1")
    f2 = A.alloc([128, 128], F32, "f2")
    forcedB = A.alloc([128, 128], F32, "forcedB")
    futN = A.alloc([128, 128], F32, "futN")
    futS = A.alloc([128, 128], F32, "futS")
    imp = A.alloc([128, 128], F32, "imp")
    impm = A.alloc([128, 128], F32, "impm")
    wk = A.alloc([128, 128], F32, "wk")
    a01 = A.alloc([128, 128], F32, "a01")
    m8 = A.alloc([128, 16], F32, "m8")
    rdc = A.alloc([128, 8], F32, "rdc")
    OA = [A.alloc([128, 16, 64], F32, f"OA{i}") for i in range(1)]
    tmpo = A.alloc([128, 16, 64], F32, "tmpo")
    rd = A.alloc([128, 16], F32, "rd")
    sc = A.alloc([128, 16], F32, "sc")
    gbs = [A.alloc([128, 512], F32, f"gbs{i}") for i in range(8)]
    KTt = [A.alloc([128, 2, 128], BF16, f"KTt{i}") for i in range(8)]
    Vt = [A.alloc([128, 4, 65], BF16, f"Vt{i}") for i in range(8)]
    Lg = A.alloc([128, 512], F32, "Lg")
    exg = Lg
    eq = A.alloc([128, 4, 128], F32, "eq")
    ek = A.alloc([128, 4, 128], F32, "ek")
    QdT = A.alloc([128, 4, 128], BF16, "QdT")
    KdT = A.alloc([128, 4, 128], BF16, "KdT")
    QdTm = A.alloc([128, 4, 4, 32], BF16, "QdTm")
    attm = A.alloc([128, 4, 128], BF16, "attm")
    S0f = [A.alloc([128, 1024], F32, f"S0f{i}") for i in range(1)]
    S0b = [A.alloc([128, 1024], BF16, f"S0b{i}") for i in range(2)]
    gnw = A.alloc([128, 256], F32, "gnw")
    gst = A.alloc([128, 16], F32, "gst")
    gjunk = A.alloc([128, 256], BF16, "gjunk")
    Eg = TT(eq.ap.rearrange("p h t -> p (h t)"), eq.b)
    Kd2g = A.alloc([32, 512], BF16, "Kd2g")
    Kd2m = [A.alloc([32, 512], BF16, f"Kd2m{i}") for i in range(2)]
    decg = A.alloc([128, 4, 4], F32, "decg")

    k.dma("sp", gnw[:], gnw_d.partition_broadcast(128), [], [gnw])
    k.dma("sp", KcP[:].rearrange("p a c -> p (a c)"), KcS.ap[0], [KcS], [KcP])
    k.dma("sp", VcP[:].rearrange("p c h w -> p (c h w)"), VcS.ap[0], [VcS], [VcP])
    for i in range(8):
        k.memset("pool", Vt[i][:], 1.0, [Vt[i]])
    k.ts("dve", causalS[:], iota[:, 0:1024], qloc, NEGB, ALU.is_gt, ALU.mult, [cst], [causalS])
    k.ts("dve", qpj[:, 1:2], qloc, 512.0, None, ALU.add, None, [cst], [qpj])
    k.ts("dve", winb[:], iota, qpj[:, 1:2], None, ALU.is_gt, None, [cst, qpj], [winb])
    k.stt(winb[:], iota, qloc, winb[:], ALU.is_le, ALU.add, [cst, winb], [winb])
    k.ts("dve", winb[:], winb[:], NEGB, None, ALU.mult, None, [winb], [winb])
    k.memset("pool", base511[:], 0.0, [base511])
    k.memset("pool", base511[:, 511:512], NEGB, [base511])
    qps = cst[0:32, O_QPS:O_QPS + 1]
    k.ts("dve", rd[0:32, 0:1], qps, -8192.0, None, ALU.add, None, [cst], [rd])
    k.ts("dve", wbase[:], iota[0:32, 0:512], rd[0:32, 0:1], NEGB, ALU.is_le, ALU.mult, [cst, rd], [wbase])
    k.cp("dve", newb16[:], cst[0:32, O_NEWB:O_NEWB + 32], [cst], [newb16])
    nb30 = cst[0:32, O_NB30:O_NB30 + 4]

    srot = [0]

    SB = (0, 1, 2, 5, 6, 7)

    def ps_s():
        b = psb[SB[srot[0] % 6]]
        srot[0] += 1
        return b

    pn = [0]
    bxn = [0]
    pvn = [0]
    SLOTS = [(j, 128, 128 * j) for j in range(8)] + [(8, 32, 1024)]
    only = dbg.get("slots")
    for (sl, NQ, tok0) in SLOTS:
        if only is not None and sl not in only:
            continue
        prompt = sl < 8
        OA_ = OA[0]
        NS_ = NS[sl % 2]

        def attend_chunk(h, tiles, parts, acc_ap, accbuf, first, pvm=None, tpv=False):
            a, b_ = h // 2, h % 2
            pts = []
            for (KT_ap, Kbuf, Vfn, Vbuf, bias_ap, Bbuf, nk) in tiles:
                ps_ = ps_s()
                out3 = ps_[0:nk, 0:4 * NQ].rearrange("p (g q) -> p g q", g=4)
                k.mm(out3, bias_ap, Irep[0:NQ, :, 0:NQ], True, False, [Bbuf, Irep], [ps_])
                k.mm(out3, KT_ap, QT[64 * b_:64 * b_ + 64, a, :, tok0:tok0 + NQ], False, True, [Kbuf, QT], [ps_])
                PT_ = PT[pn[0] % len(PT)]
                pn[0] += 1
                k.act(PT_[0:nk, 0:4 * NQ], ps_[0:nk, 0:4 * NQ], AF.Exp, [ps_], [PT_])
                pts.append(PT_)
            if tpv:
                pvT = ps_s()
                for i, (KT_ap, Kbuf, Vfn, Vbuf, bias_ap, Bbuf, nk) in enumerate(tiles):
                    k.mm(pvT[0:65, 0:4 * NQ], Vfn(0, 65), pts[i][0:nk, 0:4 * NQ], i == 0, i == len(tiles) - 1, [Vbuf, pts[i]], [pvT])
                dstT = accT[0:65, h, :]
                if first:
                    k.cp("dve", dstT, pvT[0:65, 0:512], [pvT], [accT])
                else:
                    k.tt("dve", dstT, pvT[0:65, 0:512], dstT, ALU.add, [pvT, accT], [accT])
                return
            if pvm is not None:
                for i, (KT_ap, Kbuf, Vfn, Vbuf, bias_ap, Bbuf, nk) in enumerate(tiles):
                    k.mm(pvm[0:4 * NQ, h * 65:(h + 1) * 65], pts[i][0:nk, 0:4 * NQ], Vfn(0, 65),
                         i == 0, i == len(tiles) - 1, [pts[i], Vbuf], [pvm])
                return
            for pi, (c0, c1) in enumerate(parts):
                W = c1 - c0
                pv = ps_s()
                for g in range(4):
                    for i, (KT_ap, Kbuf, Vfn, Vbuf, bias_ap, Bbuf, nk) in enumerate(tiles):
                        k.mm(pv[0:NQ, g * W:(g + 1) * W], pts[i][0:nk, g * NQ:(g + 1) * NQ], Vfn(c0, c1),
                             i == 0, i == len(tiles) - 1, [pts[i], Vbuf], [pv])
                dst = acc_ap(pi)
                src = pv[0:NQ, 0:4 * W].rearrange("p (g w) -> p g w", g=4)
                if first:
                    k.cp("dve", dst, src, [pv], [accbuf])
                else:
                    k.tt("dve", dst, src, dst, ALU.add, [pv, accbuf], [accbuf])

        P.phase = f'cmp{sl}'
        if prompt:
            k.ts("dve", qpj[:, 0:1], qloc, 1024.0 * sl, None, ALU.add, None, [cst], [qpj])
            qp = qpj[0:NQ, 0:1]
            qpb = [cst, qpj]
        else:
            qp = qps
            qpb = [cst]
        k.ts("dve", dq[0:NQ], iota[0:NQ, 0:128], -64.0, qp, ALU.mult, ALU.add, qpb, [dq])
        k.ts("dve", f1[0:NQ], dq[0:NQ], 0.0, None, ALU.is_ge, None, [dq], [f1])
        k.ts("dve", f2[0:NQ], dq[0:NQ], 128.0, BIG, ALU.is_lt, ALU.mult, [dq], [f2])
        k.tt("dve", forcedB[0:NQ], f1[0:NQ], f2[0:NQ], ALU.mult, [f1, f2], [forcedB])
        k.memset("pool", forcedB[0:NQ, 0:1], BIG, [forcedB])
        k.ts("dve", futN[0:NQ], f1[0:NQ], 2.0 * BIG, -BIG, ALU.mult, ALU.add, [f1], [futN])
        k.ts("dve", futS[0:NQ], f1[0:NQ], -1.0, -NEGB, ALU.add, ALU.mult, [f1], [futS])
        gs3 = gsig[0:NQ, sl, :].rearrange("p (i r) -> p i r", r=3)

        if prompt:
            CB = cmpb[0]
            k.ts("dve", qpj[:, 1:2], qp, -31.0, 1.0 / 16.0, ALU.add, ALU.mult, qpb, [qpj])
            k.ts("dve", CB[0:NQ], iota[0:NQ, 0:512], qpj[0:NQ, 1:2], NEGB, ALU.is_gt, ALU.mult, [cst, qpj], [CB])
        cn = 0
        for h in range(4):
            a, b_ = h // 2, h % 2
            nsets = 1 if prompt else 4
            for si in range(nsets):
                if prompt:
                    Kc_, Vc_, CBs = KcP, VcP, CB
                else:
                    Kc_, Vc_, CBs = KcB[0], VcB[0], cmpb[0]
                    cn += 1
                    k.dma("sp", Kc_[:].rearrange("p a c -> p (a c)"), KcS.ap[1 + si], [KcS], [Kc_])
                    k.dma("sp", Vc_[:].rearrange("p c h w -> p (c h w)"), VcS.ap[1 + si], [VcS], [Vc_])
                    k.ts("dve", CBs[0:32], base511[:], nb30[:, si:si + 1], None, ALU.min, None, [base511, cst], [CBs])
                tiles = [(Kc_[64 * b_:64 * b_ + 64, a, ct * 128:(ct + 1) * 128], Kc_,
                          (lambda c0, c1, ct=ct, Vc_=Vc_, h=h: Vc_[:, ct, h, c0:c1]), Vc_,
                          CBs[0:NQ, ct * 128:(ct + 1) * 128], CBs, 128) for ct in range(4)]
                attend_chunk(h, tiles, [(0, 65), (65, 193)],
                             lambda pi: accC[0:NQ, :, 0:65] if pi == 0 else accC[0:NQ, :, 65:193], accC, si == 0)
            av = [accC[0:NQ, g, :] for g in range(4)]
            ab = [accC for g in range(4)]
            for g in range(4):
                k.ts("dve", rdc[0:NQ, g:g + 1], av[g][:, 64:65], TINY, None, ALU.max, None, [ab[g]], [rdc])
            P.op("dve", lambda e, NQ=NQ: e.reciprocal(out=rdc[0:NQ, 0:4], in_=rdc[0:NQ, 0:4]), [rdc], [rdc])
            k.ts("dve", imp[0:NQ], av[0][:, 65:193], rdc[0:NQ, 0:1], None, ALU.mult, None, [ab[0], rdc], [imp])
            for g in range(1, 4):
                k.stt(imp[0:NQ], av[g][:, 65:193], rdc[0:NQ, g:g + 1], imp[0:NQ], ALU.mult, ALU.add, [ab[g], rdc, imp], [imp])
            k.tt("dve", rdc[0:NQ, 4:8], rdc[0:NQ, 0:4], gs3[:, h * 4:h * 4 + 4, 0], ALU.mult, [rdc, gsig], [rdc])
            for g in range(4):
                k.ts("dve", OA_[0:NQ, h * 4 + g, :], av[g][:, 0:64], rdc[0:NQ, 4 + g:5 + g], None, ALU.mult, None, [ab[g], rdc], [OA_])
            k.tt("dve", impm[0:NQ], imp[0:NQ], forcedB[0:NQ], ALU.max, [imp, forcedB], [impm])
            k.tt("dve", impm[0:NQ], impm[0:NQ], futN[0:NQ], ALU.min, [impm, futN], [impm])
            P.op("dve", lambda e, NQ=NQ: e.max(out=m8[0:NQ, 0:8], in_=impm[0:NQ]), [impm], [m8])
            P.op("dve", lambda e, NQ=NQ: e.match_replace(out=wk[0:NQ], in_to_replace=m8[0:NQ, 0:8], in_values=impm[0:NQ], imm_value=-3.0 * BIG), [impm, m8], [wk])
            P.op("dve", lambda e, NQ=NQ: e.max(out=m8[0:NQ, 8:16], in_=wk[0:NQ]), [wk], [m8])
            tc_ = 15 if prompt else 14
            k.ts("dve", a01[0:NQ], impm[0:NQ], m8[0:NQ, tc_:tc_ + 1], None, ALU.is_lt, None, [impm, m8], [a01])
            k.stt(NS_[0:NQ, h, :], a01[0:NQ], NEGB, futS[0:NQ], ALU.mult, ALU.min, [a01, futS], [NS_])
            if sl == 0 and h == 3:
                k.dump("accC", accC, [128, 4, 193])
                k.dump("imp", imp, [128, 128])
                k.dump("impm", impm, [128, 128])
                k.dump("m8", m8, [128, 16])
                k.dump("rdc", rdc, [128, 8])
                k.dump("OAc", OA_, [128, 16, 64])

        def evac(br):
            v = accS[0:NQ]
            k.ts("dve", rd[0:NQ, 0:16], v[:, :, 64], TINY, None, ALU.max, None, [accS], [rd])
            P.op("dve", lambda e, NQ=NQ: e.reciprocal(out=rd[0:NQ, 0:16], in_=rd[0:NQ, 0:16]), [rd], [rd])
            k.tt("dve", sc[0:NQ, 0:16], rd[0:NQ, 0:16], gs3[:, :, br], ALU.mult, [rd, gsig], [sc])
            k.tt("dve", tmpo[0:NQ], v[:, :, 0:64], sc[0:NQ, 0:16].unsqueeze(2).broadcast_to([NQ, 16, 64]),
                 ALU.mult, [accS, sc], [tmpo])
            k.tt("pool", OA_[0:NQ], OA_[0:NQ], tmpo[0:NQ], ALU.add, [OA_, tmpo], [OA_])

        def accSf(h):
            return lambda pi: accS[0:NQ, h * 4:(h + 1) * 4, :]

        def accT_relayout():
            for h in range(4):
                pr = ps_s()
                for g in range(4):
                    k.tr(pr[:, g * 65:(g + 1) * 65], accT[0:65, h, g * 128:(g + 1) * 128], ident_f[0:65, 0:65], [accT, ident_f], [pr])
                k.cp("dve" if h % 2 == 0 else "act", accS[:, h * 4:(h + 1) * 4, :], pr[:, 0:260].rearrange("p (g w) -> p g w", g=4), [pr], [accS])

        def accM_add(pvm, first):
            src = pvm[:, 0:260].rearrange("p (h w) -> p h w", h=4)
            if first:
                k.cp("dve", accM[:], src, [pvm], [accM])
            else:
                k.tt("dve", accM[:], src, accM[:], ALU.add, [pvm, accM], [accM])

        def accM_relayout():
            for g in range(4):
                pr = ps_s()
                k.mm(pr[0:32, 0:260], ident_f[:, g * 32:(g + 1) * 32], accM[:].rearrange("p h w -> p (h w)"), True, True, [ident_f, accM], [pr])
                k.cp("dve", accS[0:32].rearrange("p (h g) w -> p h g w", g=4)[:, :, g, :],
                     pr[0:32, 0:260].rearrange("p (h w) -> p h w", h=4), [pr], [accS])

        def bias_from_ns(h, blk0, extra_ap, extra_bufs, scalar_ap=None):
            BX = biasx[bxn[0] % 8]
            bxn[0] += 1
            src = NS_[0:NQ, h, blk0:blk0 + 2].unsqueeze(2).broadcast_to([NQ, 2, 64])
            dst = BX[0:NQ].rearrange("p (r s) -> p r s", r=2)
            eng = "dve" if bxn[0] % 2 == 0 else "pool"
            if extra_ap is not None:
                k.tt("dve", dst, src, extra_ap.rearrange("p (r s) -> p r s", r=2), ALU.min, [NS_] + extra_bufs, [BX])
            elif scalar_ap is not None:
                k.ts("dve", dst, src, scalar_ap, None, ALU.min, None, [NS_, cst], [BX])
            else:
                k.cp(eng, dst, src, [NS_], [BX])
            return BX

        def kv_chunk(KTd, Vd, ck, n):
            KTc_, Vc_ = KTc[n % 2], Vc[n % 2]
            k.dma("sp", KTc_[:], KTd.ap[:, :, ck * 512:(ck + 1) * 512], [KTd], [KTc_])
            k.dma("sp", Vc_[:], Vd.ap[ck * 512:(ck + 1) * 512, :].rearrange("(t p) c -> p t c", p=128), [Vd], [Vc_])
            return KTc_, Vc_

        def cache_tile(G_, n):
            KTt_, Vt_ = KTt[n % 8], Vt[n % 8]
            k.cp("pool", Vt_[:, :, 0:64], G_[:, 256:512].rearrange("p (h e) -> p h e", h=4), [G_], [Vt_])
            pb = ps_s()
            for a in range(2):
                k.tr(pb[:, a * 128:(a + 1) * 128], G_[:, a * 128:(a + 1) * 128], ident_f[:], [G_, ident_f], [pb])
            k.cp("act", KTt_[:].rearrange("p a t -> p (a t)"), pb[:, 0:256], [pb], [KTt_])
            return KTt_, Vt_

        P.phase = f'sel{sl}'
        if prompt:
            nkt = 8 * (sl + 1)
            cnk = 0
            for ck in range(nkt // 4):
                KTc_, Vc_ = kv_chunk(KTsel, Vsel, ck, cnk)
                cnk += 1
                for h in range(4):
                    a, b_ = h // 2, h % 2
                    tiles = []
                    for u in range(4):
                        kt = ck * 4 + u
                        if kt >= 8 * sl:
                            BX = bias_from_ns(h, 2 * kt, causalS[0:NQ, (kt - 8 * sl) * 128:(kt - 8 * sl + 1) * 128], [causalS])
                        else:
                            BX = bias_from_ns(h, 2 * kt, None, [])
                        tiles.append((KTc_[64 * b_:64 * b_ + 64, a, u * 128:(u + 1) * 128], KTc_,
                                      (lambda c0, c1, u=u, Vc_=Vc_, h=h: Vc_[:, u, h * 65 + c0:h * 65 + c1]), Vc_,
                                      BX[0:NQ, :], BX, 128))
                    attend_chunk(h, tiles, [(0, 65)], accSf(h), accS, ck == 0, tpv=True)
            accT_relayout()
        else:
            gn = 0
            for b in range(dbg.get("nsb", 4)):
                for j4 in range(16):
                    cts = []
                    for u in range(4):
                        j = j4 * 4 + u
                        G_ = gbs[gn % len(gbs)]
                        col = b * 64 + j
                        P.dma("pool", lambda e, G_=G_, col=col: e.indirect_dma_start(
                            out=G_[:], out_offset=None, in_=cache_sel[:, :],
                            in_offset=bass.IndirectOffsetOnAxis(ap=idx[:, col:col + 1], axis=0)), [idx.b], [G_.b])
                        cts.append(cache_tile(G_, gn))
                        gn += 1
                    pvm = psb[3 + pvn[0] % 2]
                    pvn[0] += 1
                    for h in range(4):
                        a, b_ = h // 2, h % 2
                        tiles = []
                        for u in range(4):
                            j = j4 * 4 + u
                            KTt_, Vt_ = cts[u]
                            BX = bias_from_ns(h, 2 * j, None, [], scalar_ap=nb30[:, b:b + 1])
                            tiles.append((KTt_[64 * b_:64 * b_ + 64, a, :], KTt_,
                                          (lambda c0, c1, Vt_=Vt_, h=h: Vt_[:, h, c0:c1]), Vt_, BX[0:NQ, :], BX, 128))
                        attend_chunk(h, tiles, [(0, 65)], accSf(h), accS, b == 0 and j4 == 0, pvm=pvm)
                    accM_add(pvm, b == 0 and j4 == 0)
            pvm = psb[3 + pvn[0] % 2]
            pvn[0] += 1
            for h in range(4):
                a, b_ = h // 2, h % 2
                tiles = [(KTn[64 * b_:64 * b_ + 64, 0, a, :], KTn, (lambda c0, c1, h=h: Van[:, 0, h, c0:c1]), Van,
                          newb16[:, :], newb16, 32)]
                attend_chunk(h, tiles, [(0, 65)], accSf(h), accS, False, pvm=pvm)
            accM_add(pvm, False)
            accM_relayout()
        if sl == 0:
            k.dump("NS", NS_, [128, 4, 128], BF16)
            k.dump("accS_sel", accS, [128, 16, 65])
        evac(1)
        if sl == 0:
            k.dump("OAs", OA_, [128, 16, 64])

        P.phase = f'win{sl}'
        if prompt:
            kt0 = max(8 * sl - 4, 0)
            kt1 = 8 * sl + 8
            for ck in range(kt0 // 4, kt1 // 4):
                KTc_, Vc_ = kv_chunk(KTwin, Vwin, ck, cnk)
                cnk += 1
                for h in range(4):
                    a, b_ = h // 2, h % 2
                    tiles = []
                    for u in range(4):
                        kt = ck * 4 + u
                        uu = kt - (8 * sl - 4)
                        tiles.append((KTc_[64 * b_:64 * b_ + 64, a, u * 128:(u + 1) * 128], KTc_,
                                      (lambda c0, c1, u=u, Vc_=Vc_, h=h: Vc_[:, u, h * 65 + c0:h * 65 + c1]), Vc_,
                                      winb[0:NQ, uu * 128:(uu + 1) * 128], winb, 128))
                    attend_chunk(h, tiles, [(0, 65)], accSf(h), accS, ck == kt0 // 4, tpv=True)
            accT_relayout()
        else:
            for b in range(dbg.get("nsb", 4)):
                WB = wbs[0]
                k.ts("dve", WB[:], wbase[:], nb30[:, b:b + 1], None, ALU.min, None, [wbase, cst], [WB])
                cts = []
                for u in range(4):
                    G_ = gbs[gn % len(gbs)]
                    k.dma("sp", G_[:], cache_win[b, u * 128:(u + 1) * 128, :], [], [G_])
                    cts.append(cache_tile(G_, gn))
                    gn += 1
                pvm = psb[3 + pvn[0] % 2]
                pvn[0] += 1
                for h in range(4):
                    a, b_ = h // 2, h % 2
                    tiles = []
                    for u in range(4):
                        KTt_, Vt_ = cts[u]
                        tiles.append((KTt_[64 * b_:64 * b_ + 64, a, :], KTt_, (lambda c0, c1, Vt_=Vt_, h=h: Vt_[:, h, c0:c1]), Vt_,
                                      WB[:, u * 128:(u + 1) * 128], WB, 128))
                    attend_chunk(h, tiles, [(0, 65)], accSf(h), accS, b == 0, pvm=pvm)
                accM_add(pvm, b == 0)
            pvm = psb[3 + pvn[0] % 2]
            pvn[0] += 1
            for h in range(4):
                a, b_ = h // 2, h % 2
                tiles = [(KTn[64 * b_:64 * b_ + 64, 1, a, :], KTn, (lambda c0, c1, h=h: Van[:, 1, h, c0:c1]), Van,
                          newb16[:, :], newb16, 32)]
                attend_chunk(h, tiles, [(0, 65)], accSf(h), accS, False, pvm=pvm)
            accM_add(pvm, False)
            accM_relayout()
        if sl == 0:
            k.dump("accS_win", accS, [128, 16, 65])
        evac(2)
        ON = on1[0]
        OG = og1[0]
        k.cp("act", ON[0:NQ, :], OA_[0:NQ].rearrange("p i d -> p (i d)"), [OA_], [ON])
        k.dma("sp", onsa_s.ap[sl, 0:NQ, :], ON[0:NQ, :], [ON], [onsa_s])

        P.phase = f'gla{sl}'
        NT = NQ
        nbt = 1 if prompt else 4
        o_uge = O_UGE if prompt else O_UGE32
        o_mtri = O_MTRI if prompt else O_MTRI32
        pp = ps_s()
        k.mm(pp[0:NT, :], agTa[:, tok0:tok0 + NT], wa2b[:, :], True, True, [agTa, wa2b], [pp])
        k.act(exg[0:NT], pp[0:NT, :], AF.Exp, [pp], [exg], scale=-1.0)
        k.act(Lg[0:NT], exg[0:NT], AF.Ln, [exg, ones_f], [Lg], bias=ones_f[0:NT, 0:1], scale=1.0)
        pc_ = ps_s()
        for h in range(4):
            k.mm(pc_[:, h * NT:(h + 1) * NT], Lg[0:NT, h * 128:(h + 1) * 128], cst[0:NT, o_uge:o_uge + NT], True, True, [Lg, cst], [pc_])
        eqv = eq[:].rearrange("p h t -> p (h t)")[:, 0:4 * NT]
        ekv = ek[:].rearrange("p h t -> p (h t)")[:, 0:4 * NT]
        k.act(eqv, pc_[:, 0:4 * NT], AF.Exp, [pc_], [eq])
        k.act(ekv, pc_[:, 0:4 * NT], AF.Exp, [pc_], [ek], scale=-1.0)
        qdv = QdT[:].rearrange("p h t -> p (h t)")[:, 0:4 * NT].rearrange("p (h t) -> p h t", h=4)
        kdv = KdT[:].rearrange("p h t -> p (h t)")[:, 0:4 * NT].rearrange("p (h t) -> p h t", h=4)
        k.tt("dve", qdv, qgT[:, :, tok0:tok0 + NT], eqv.rearrange("p (h t) -> p h t", h=4), ALU.mult, [qgT, eq], [QdT])
        k.tt("dve", kdv, kgT[:, :, tok0:tok0 + NT], ekv.rearrange("p (h t) -> p h t", h=4), ALU.mult, [kgT, ek], [KdT])
        pa_ = ps_s()
        for h in range(4):
            k.mm(pa_[0:NT, h * NT:(h + 1) * NT], kdv[:, h, :], qdv[:, h, :], True, True, [KdT, QdT], [pa_])
        amv = attm[0:NT].rearrange("p h t -> p (h t)")[:, 0:4 * NT].rearrange("p (h t) -> p h t", h=4)
        k.tt("dve", amv, pa_[0:NT, 0:4 * NT].rearrange("p (h t) -> p h t", h=4),
             cst[0:NT, o_mtri:o_mtri + NT].unsqueeze(1).broadcast_to([NT, 4, NT]), ALU.mult, [pa_, cst], [attm])
        if prompt:
            S0f_, S0b_ = S0f[0], S0b[sl % 2]
            k.dma("sp", S0f_[:], SS.ap[sl], [SS], [S0f_])
            k.cp("pool", S0b_[:], S0f_[:], [S0f_], [S0b_])
            states = [(S0f_, S0b_)]
        else:
            indrow = cst[:, O_INDROW:O_INDROW + 128].rearrange("p (b t) -> p b t", b=4)
            k.tt("dve", QdTm[:], qdv.unsqueeze(1).broadcast_to([128, 4, 4, 32]), indrow.unsqueeze(2).broadcast_to([128, 4, 4, 32]),
                 ALU.mult, [QdT, cst], [QdTm])
            states = []
        po = [psb[3], psb[4]]
        sn = 0
        for h in range(4):
            oap = po[h // 2][0:NT, (h % 2) * 256:(h % 2) * 256 + 256]
            k.mm(oap, amv[:, h, :], vg[0:NT, sl, h * 256:(h + 1) * 256], True, False, [attm, vg], [po[h // 2]])
            if prompt:
                k.mm(oap, qdv[:, h, :], S0b_[:, h * 256:(h + 1) * 256], False, True, [QdT, S0b_], [po[h // 2]])
            else:
                for b in range(4):
                    Sf, Sb = S0f[0], S0b[sn % 2]
                    sn += 1
                    k.dma("sp", Sf[:, h * 256:(h + 1) * 256], state_gla[b, h], [], [Sf])
                    k.cp("pool", Sb[:, h * 256:(h + 1) * 256], Sf[:, h * 256:(h + 1) * 256], [Sf], [Sb])
                    k.mm(oap, QdTm[:, b, h, :], Sb[:, h * 256:(h + 1) * 256], False, b == 3, [QdTm, Sb], [po[h // 2]])
        for h in range(4):
            oap = po[h // 2][0:NT, (h % 2) * 256:(h % 2) * 256 + 256]
            k.act(gjunk[0:NT], oap, AF.Square, [po[h // 2]], [gjunk, gst], accum_out=gst[0:NT, h:h + 1])
        k.ts("dve", gst[0:NT, 4:8], gst[0:NT, 0:4], 1.0 / 256.0, EPS, ALU.mult, ALU.add, [gst], [gst])
        k.act(gst[0:NT, 8:12], gst[0:NT, 4:8], AF.Sqrt, [gst], [gst])
        P.op("dve", lambda e, NT=NT: e.reciprocal(out=gst[0:NT, 12:16], in_=gst[0:NT, 8:12]), [gst], [gst])
        for h in range(4):
            oap = po[h // 2][0:NT, (h % 2) * 256:(h % 2) * 256 + 256]
            k.stt(OG[0:NT, h * 256:(h + 1) * 256], oap, gst[0:NT, 12 + h:13 + h], gnw[0:NT, :], ALU.mult, ALU.mult,
                  [po[h // 2], gst, gnw], [OG])
        k.dma("sp", ogla_s.ap[sl, 0:NT, :], OG[0:NT, :], [OG], [ogla_s])
        if not prompt:
            pd = ps_s()
            k.mm(pd[0:32, :], cst[0:32, O_UGT32:O_UGT32 + 32], Lg[0:32, :], True, True, [cst, Lg], [pd])
            k.act(Eg[0:32], pd[0:32, :], AF.Exp, [pd], [Eg])
            k.tt("dve", Kd2g[:], kgN[:], Eg[0:32], ALU.mult, [kgN, Eg], [Kd2g])
            pl = ps_s()
            for h in range(4):
                k.mm(pl[:, h * 4:(h + 1) * 4], Lg[0:32, h * 128:(h + 1) * 128], cst[0:32, O_NEGIND:O_NEGIND + 4], True, True, [Lg, cst], [pl])
            k.act(decg[:].rearrange("p h b -> p (h b)"), pl[:, 0:16], AF.Exp, [pl], [decg])
            for b in range(4):
                Km = Kd2m[b % 2]
                k.ts("dve", Km[:], Kd2g[:], cst[0:32, O_IND + b:O_IND + b + 1], None, ALU.mult, None, [Kd2g, cst], [Km])
                Sf = S0f[0]
                Sn = Sf
                k.dma("sp", Sf[:].rearrange("p (h v) -> p h v", h=4), state_gla[b].rearrange("h k v -> k h v"), [], [Sf])
                for hp in range(2):
                    pb = ps_s()
                    for u in range(2):
                        h = hp * 2 + u
                        k.mm(pb[:, u * 256:(u + 1) * 256], Km[:, h * 128:(h + 1) * 128], vg[0:32, 8, h * 256:(h + 1) * 256], True, True, [Km, vg], [pb])
                    for u in range(2):
                        h = hp * 2 + u
                        k.stt(Sn[:, h * 256:(h + 1) * 256], Sf[:, h * 256:(h + 1) * 256], decg[:, h, b:b + 1], pb[:, u * 256:(u + 1) * 256],
                              ALU.mult, ALU.add, [Sf, decg, pb], [Sn])
                k.dma("sp", gla_s[b], Sn[:], [Sn], [])
    if dbg.get("dump_o"):
        o1 = k.dout("dbg_onsa", [9, 128, 1024], BF16)
        o2 = k.dout("dbg_ogla", [9, 128, 1024], BF16)
        k.dma("sp", o1[:, :, :], onsa_s.ap[:, :, :], [onsa_s], [])
        k.dma("sp", o2[:, :, :], ogla_s.ap[:, :, :], [ogla_s], [])
    P.barrier()
    phase_late(k, env, L_, mX)


def phase_late(k, env, L_, mX):
    P, A, dbg = k.P, k.A, k.dbg
    ident_b, Wb, gsc, Won, Wog, Wout, xl, xs = (env[n] for n in ("ident_b", "Wb", "gsc", "Won", "Wog", "Wout", "xl", "xs"))
    regC, mB, hTs, onsa_s, ogla_s, fnw_d, y_l, y_s = (L_[n] for n in ("regC", "mB", "hTs", "onsa_s", "ogla_s", "fnw_d", "y_l", "y_s"))
    TILES = [(t, 128, 128 * t) for t in range(8)] + [(8, 32, 1024)]
    P.phase = 'late_a'
    regC.reset()
    onT = regC.alloc([128, 8, 1056], BF16, "onT")
    ogT = regC.alloc([128, 8, 1056], BF16, "ogT")
    A.reset(mB)
    mT = A.alloc([128, KC, 1056], BF16, "mT")
    mMT = A.mark()
    hTa = A.alloc([128, KC, 1056], BF16, "hTa2")
    k.dma("sp", hTa[:], hTs.ap[:, :, :], [hTs], [hTa])
    ma = A.mark()
    Wt2 = [A.alloc([128, KC, 512], BF16, f"Wt2_{i}") for i in range(2)]
    o1 = [A.alloc([128, 512], BF16, f"o1_{i}") for i in range(2)]
    zs = [A.alloc([128, 512], F32, f"zs{i}") for i in range(2)]
    wn = [0]
    zn = [0]
    for (scr, c0, dstT) in ((onsa_s, C_ZN, onT), (ogla_s, C_ZG, ogT)):
        for g2 in range(2):
            W_ = Wt2[wn[0] % 2]
            wn[0] += 1
            k.dma("sp", W_[:], Wb.ap[:, :, c0 + g2 * 512:c0 + (g2 + 1) * 512], [Wb], [W_])
            for (t, NT, tok0) in TILES:
                pb = k.ps()
                for kc in range(KC):
                    k.mm(pb[0:NT, :], hTa[:, kc, tok0:tok0 + NT], W_[:, kc, :], kc == 0, kc == KC - 1, [hTa, W_], [pb])
                Z_, O_ = zs[zn[0] % 2], o1[zn[0] % 2]
                zn[0] += 1
                k.act(Z_[0:NT], pb[0:NT, :], AF.Silu, [pb], [Z_])
                k.dma("sp", O_[0:NT], scr.ap[t, 0:NT, g2 * 512:(g2 + 1) * 512], [scr], [O_])
                k.tt("dve", O_[0:NT], O_[0:NT], Z_[0:NT], ALU.mult, [O_, Z_], [O_])
                pt = k.ps()
                ptv = pt[:].bitcast(BF16)
                for u in range(4):
                    k.tr(ptv[:, u * 128:u * 128 + NT], O_[0:NT, u * 128:(u + 1) * 128], ident_b[0:NT, 0:NT], [O_, ident_b], [pt])
                k.cp("act", dstT[:, g2 * 4:(g2 + 1) * 4, tok0:tok0 + NT],
                     ptv[:, 0:512].rearrange("p (u t) -> p u t", u=4)[:, :, 0:NT], [pt], [dstT])
    P.barrier()
    P.phase = 'late_b'
    A.reset(ma)
    Wsets = []
    for i, ar in enumerate((A, regC)):
        Wsets.append((ar.alloc([128, KC, 256], BF16, f"Wmn{i}"), ar.alloc([128, KC, 256], BF16, f"Wmg{i}"),
                      ar.alloc([128, 8, 256], BF16, f"Won_t{i}"), ar.alloc([128, 8, 256], BF16, f"Wog_t{i}")))
    s1 = [A.alloc([128, 512], F32, f"s1_{i}") for i in range(2)]
    t1 = [A.alloc([128, 512], F32, f"t1_{i}") for i in range(2)]
    m16 = [A.alloc([128, 256], BF16, f"m16_{i}") for i in range(2)]
    mn = 0
    for cg in range(8):
        Wmn, Wmg, Won_t, Wog_t = Wsets[cg % 2]
        k.dma("sp", Wmn[:], Wb.ap[:, :, C_MN + cg * 256:C_MN + (cg + 1) * 256], [Wb], [Wmn])
        k.dma("sp", Wmg[:], Wb.ap[:, :, C_MG + cg * 256:C_MG + (cg + 1) * 256], [Wb], [Wmg])
        k.dma("sp", Won_t[:], Won.ap[:, :, cg * 256:(cg + 1) * 256], [Won], [Won_t])
        k.dma("sp", Wog_t[:], Wog.ap[:, :, cg * 256:(cg + 1) * 256], [Wog], [Wog_t])
        for (t, NT, tok0) in TILES:
            pg = k.ps()
            px = k.ps()
            for kc in range(KC):
                k.mm(pg[0:NT, 0:256], hTa[:, kc, tok0:tok0 + NT], Wmn[:, kc, :], kc == 0, kc == KC - 1, [hTa, Wmn], [pg])
            for kc in range(KC):
                k.mm(pg[0:NT, 256:512], hTa[:, kc, tok0:tok0 + NT], Wmg[:, kc, :], kc == 0, kc == KC - 1, [hTa, Wmg], [pg])
            for kc in range(8):
                k.mm(px[0:NT, 0:256], onT[:, kc, tok0:tok0 + NT], Won_t[:, kc, :], kc == 0, kc == 7, [onT, Won_t], [px])
            for kc in range(8):
                k.mm(px[0:NT, 256:512], ogT[:, kc, tok0:tok0 + NT], Wog_t[:, kc, :], kc == 0, kc == 7, [ogT, Wog_t], [px])
            S_, T_, M_ = s1[mn % 2], t1[mn % 2], m16[mn % 2]
            mn += 1
            k.act(S_[0:NT], pg[0:NT, :], AF.Sigmoid, [pg], [S_])
            k.tt("dve", T_[0:NT], px[0:NT, :], S_[0:NT], ALU.mult, [px, S_], [T_])
            k.tt("pool", M_[0:NT], T_[0:NT, 0:256], T_[0:NT, 256:512], ALU.add, [T_], [M_])
            pt = k.ps()
            ptv = pt[:].bitcast(BF16)
            for u in range(2):
                k.tr(ptv[:, u * 128:u * 128 + NT], M_[0:NT, u * 128:(u + 1) * 128], ident_b[0:NT, 0:NT], [M_, ident_b], [pt])
            k.cp("act", mT[:, cg * 2:cg * 2 + 2, tok0:tok0 + NT],
                 ptv[:, 0:256].rearrange("p (u t) -> p u t", u=2)[:, :, 0:NT], [pt], [mT])
    P.barrier()
    P.phase = 'late_c'
    A.reset(mMT)
    Wo = A.alloc([128, KC, D], BF16, "Wo")
    for q in range(4):
        k.dma("sp", Wo[:, q * 4:(q + 1) * 4, :], Wout.ap[:, q * 4:(q + 1) * 4, :], [Wout], [Wo])
    regC.reset()
    gateP = regC.alloc([128, D], F32, "gateP")
    gateS = regC.alloc([32, D], F32, "gateS")
    fnw = regC.alloc([128, D], F32, "fnw")
    xt3 = [regC.alloc([128, D], F32, f"xt3_{i}") for i in range(2)]
    yt = [regC.alloc([128, D], F32, f"yt{i}") for i in range(2)]
    st3 = [regC.alloc([128, 8], F32, f"st3_{i}") for i in range(2)]
    k.dma("sp", gateP[:], gsc.ap[0:1, :].partition_broadcast(128), [gsc], [gateP])
    for b in range(4):
        k.dma("sp", gateS[8 * b:8 * b + 8], gsc.ap[1 + b:2 + b, :].partition_broadcast(8), [gsc], [gateS])
    k.dma("sp", fnw[:], fnw_d.partition_broadcast(128), [], [fnw])
    for (t, NT, tok0) in TILES:
        X, Y, ST = xt3[t % 2], yt[t % 2], st3[t % 2]
        G_ = gateP if t < 8 else gateS
        if t < 8:
            k.dma("sp", X[:], xl[t], [], [X])
        else:
            k.dma("sp", X[0:32], xs[:, :], [], [X])
        for c4 in range(4):
            pb = k.ps()
            for kc in range(KC):
                k.mm(pb[0:NT, :], mT[:, kc, tok0:tok0 + NT], Wo[:, kc, c4 * 512:(c4 + 1) * 512], kc == 0, kc == KC - 1, [mT, Wo], [pb])
            k.tt("dve", Y[0:NT, c4 * 512:(c4 + 1) * 512], pb[0:NT, :], G_[0:NT, c4 * 512:(c4 + 1) * 512], ALU.mult, [pb, G_], [Y])
        k.tt("pool", Y[0:NT], Y[0:NT], X[0:NT], ALU.add, [Y, X], [Y])
        k.act(X[0:NT], Y[0:NT], AF.Square, [Y], [X, ST], accum_out=ST[0:NT, 0:1])
        k.ts("dve", ST[0:NT, 1:2], ST[0:NT, 0:1], 1.0 / D, EPS, ALU.mult, ALU.add, [ST], [ST])
        k.act(ST[0:NT, 2:3], ST[0:NT, 1:2], AF.Sqrt, [ST], [ST])
        P.op("dve", lambda e, ST=ST, NT=NT: e.reciprocal(out=ST[0:NT, 3:4], in_=ST[0:NT, 2:3]), [ST], [ST])
        k.stt(Y[0:NT], Y[0:NT], ST[0:NT, 3:4], fnw[0:NT], ALU.mult, ALU.mult, [Y, ST, fnw], [Y])
        if t < 8:
            k.dma("sp", y_l[t], Y[:], [Y], [])
        else:
            k.dma("sp", y_s[:, :], Y[0:32], [Y], [])


def rope_table(pos):
    half = 32
    inv = (10000.0 ** (-np.arange(half, dtype=np.float32) / half)).astype(np.float32)
    ang = pos.astype(np.float32)[:, None] * inv[None, :]
    return np.concatenate([np.cos(ang), np.sin(ang)], axis=1).astype(np.float32)


def core_inputs(c, inp):
    f = np.ascontiguousarray
    f32 = np.float32
    xp = inp["x_prompt"][0]
    m = {}
    m["x_prompt"] = xp
    m["xl"] = f(xp.reshape(NSLOT, 8, 128, D)[:, c])
    m["xs"] = f(inp["x_sample"][4 * c:4 * c + 4].reshape(32, D))
    m["c_all"] = f(np.concatenate([inp["c_prompt"], inp["c_sample"][4 * c:4 * c + 4]], axis=0))
    m["norm_w"] = inp["norm_w"]
    m["w_ada"] = inp["w_ada"][0]
    m["b_ada"] = inp["b_ada"]
    m["w_in"] = inp["w_in"][0]
    m["w_a2"] = inp["w_a2"][0]
    m["b_a"] = inp["b_a"]
    m["w_o_nsa"] = inp["w_o_nsa"][0]
    m["w_o_gla"] = inp["w_o_gla"][0]
    m["w_out"] = inp["w_out"][0]
    m["cmp_pos"] = inp["cmp_pos"][0].reshape(64, 64)
    m["cmp_w1"] = inp["cmp_w1"][0]
    m["cmp_b1"] = inp["cmp_b1"][0]
    m["cmp_w2"] = inp["cmp_w2"][0]
    m["cmp_b2"] = inp["cmp_b2"][0]
    m["gla_norm_w"] = inp["gla_norm_w"]
    m["final_norm_w"] = inp["final_norm_w"].reshape(1, D)
    m["cache_cmp"] = inp["cache_kv_cmp"].reshape(-1, 512)
    m["cache_sel"] = inp["cache_kv_sel"].reshape(-1, 512)
    m["cache_win"] = f(inp["cache_kv_win"][0, 4 * c:4 * c + 4].reshape(4, 512, 512))
    m["state_gla"] = f(inp["state_gla"][0, 4 * c:4 * c + 4])
    m["page_table"] = f(inp["page_table"][4 * c:4 * c + 4].reshape(1, 256).astype(np.int32))
    m["rope_g"] = rope_table(np.arange(SEQ))
    mc = np.zeros((128, NTG), f32)
    mc[:, c::8] = 1.0
    m["mcol"] = mc
    s_, t_ = np.meshgrid(np.arange(128), np.arange(128), indexing="ij")
    m["cU"] = np.where(s_ > t_, -1.0 / 16.0, 0.0).astype(f32)
    cst = np.zeros((128, NCST), f32)
    cst[:, O_UGT:O_UGT + 128] = np.where(s_ > t_, -1.0 / 16.0, 0.0)
    cst[:, O_UGE:O_UGE + 128] = np.where(s_ <= t_, -1.0 / 16.0, 0.0)
    cst[:, O_MTRI:O_MTRI + 128] = np.where(s_ <= t_, 1.0, 0.0)
    s3, t3 = s_[:32, :32], t_[:32, :32]
    same = (s3 // 8) == (t3 // 8)
    cst[:32, O_UGE32:O_UGE32 + 32] = np.where(same & (s3 <= t3), -1.0 / 16.0, 0.0)
    cst[:32, O_MTRI32:O_MTRI32 + 32] = np.where(same & (s3 <= t3), 1.0, 0.0)
    cst[:32, O_UGT32:O_UGT32 + 32] = np.where(same & (s3 > t3), -1.0 / 16.0, 0.0)
    p32 = np.arange(32)
    ind = (p32[:, None] // 8 == np.arange(4)[None, :]).astype(f32)
    cst[:32, O_NEGIND:O_NEGIND + 4] = -ind / 16.0
    cst[:32, O_IND:O_IND + 4] = ind
    cst[:, O_INDROW:O_INDROW + 128] = ind.T.reshape(1, 128)
    q_, k_ = np.meshgrid(p32, p32, indexing="ij")
    cst[:32, O_NEWB:O_NEWB + 32] = np.where(((q_ // 8) != (k_ // 8)) | ((k_ % 8) > (q_ % 8)), NEGB, 0.0)
    cst[:, O_QLOC] = 128 * c + np.arange(128)
    cst[:32, O_QPS] = SEQ + (p32 % 8)
    cst[:32, O_NB30:O_NB30 + 4] = (1.0 - ind) * NEGB
    cst[:, O_IOTA:O_IOTA + 1536] = np.arange(1536, dtype=f32)[None, :]
    m["cst"] = cst
    cc = np.arange(512)[:, None] * 16
    js = np.arange(128)[None, :] * 64
    cm = ((cc < js + 64) & (cc + 32 > js)).astype(f32)
    m["cm2s"] = f(cm.reshape(4, 128, 128).transpose(1, 0, 2))
    rl = np.zeros((9, 128, 64), f32)
    for t in range(8):
        rl[t] = rope_table(128 * (8 * t + c) + np.arange(128)) / 8.0
    rsm = rope_table(SEQ + (p32 % 8))
    rl[8, :32] = rsm / 8.0
    m["ropel"] = rl
    m["ropes"] = rsm
    return m


def kernel(**inp):
    inp = {k_: np.asarray(v) for k_, v in inp.items()}
    nc = build()
    in_maps = [core_inputs(c, inp) for c in range(NCORES)]
    res = run_bass_kernel_spmd(nc, in_maps, core_ids=list(range(NCORES)))
    R = res.results
    f32 = np.float32
    kvshape = (2, 4, 64)

    def get(c, name, shape):
        if name in R[c]:
            return np.asarray(R[c][name], dtype=f32).reshape(shape)
        return np.zeros(shape, f32)

    y_prompt = np.zeros((1, SEQ, D), f32)
    yv = y_prompt.reshape(NSLOT, 8, 128, D)
    for c in range(NCORES):
        yv[:, c] = get(c, "y_l", (NSLOT, 128, D))
    y_sample = np.concatenate([get(c, "y_s", (4, 8, D)) for c in range(NCORES)], axis=0)
    kv_cmp_prompt = get(0, "kv_cmp_p", (1, 1, SEQ) + kvshape)
    kv_sel_prompt = get(0, "kv_sel_p", (1, 1, SEQ) + kvshape)
    kv_win_prompt = get(0, "kv_win_p", (1, 1, 512) + kvshape)
    kv_cmp_sample = np.concatenate([get(c, "kv_cmp_s", (1, 4, 8) + kvshape) for c in range(NCORES)], axis=1)
    kv_sel_sample = np.concatenate([get(c, "kv_sel_s", (1, 4, 8) + kvshape) for c in range(NCORES)], axis=1)
    kv_win_sample = np.concatenate([get(c, "kv_win_s", (1, 4, 512) + kvshape) for c in range(NCORES)], axis=1)
    gp = get(0, "gla_p", (128, 4, 256))
    gla_state_prompt = np.ascontiguousarray(gp.transpose(1, 0, 2)).reshape(1, 1, 4, 128, 256)
    gs = [get(c, "gla_s", (4, 128, 4, 256)) for c in range(NCORES)]
    gla_state_sample = np.concatenate([np.ascontiguousarray(g.transpose(0, 2, 1, 3)) for g in gs], axis=0).reshape(1, 32, 4, 128, 256)
    return (y_prompt, y_sample, kv_cmp_prompt, kv_cmp_sample, kv_sel_prompt, kv_sel_sample,
            kv_win_prompt, kv_win_sample, gla_state_prompt, gla_state_sample)
```

```python
import numpy as np
from contextlib import ExitStack
import concourse.bass as bass
import concourse.mybir as mybir
from concourse.bass_utils import run_bass_kernel_spmd

F32 = mybir.dt.float32
BF16 = mybir.dt.bfloat16
I32 = mybir.dt.int32
U8 = mybir.dt.uint8
AF = mybir.ActivationFunctionType
ALU = mybir.AluOpType

NCORES = 8
D = 2048
KC = 16
SEQ = 8192
NTG = SEQ // 128
NSLOT = 8
INW = 10816
C_QN, C_KV, C_GN, C_ZN, C_QG, C_KG, C_VG, C_AG, C_ZG, C_MN, C_MG = (
    0, 1024, 2560, 2608, 3632, 4144, 4656, 5680, 5696, 6720, 8768)
EPS = 1e-6
NEGB = -30000.0
BIG = 1e30


class Buf:
    __slots__ = ("name", "lw", "rd", "sem", "cnt", "lastop")

    def __init__(self, name):
        self.name = name
        self.lw = None
        self.rd = []
        self.sem = None
        self.cnt = 0
        self.lastop = None


class Op:
    __slots__ = ("eng", "fn", "deps", "signal", "idx", "is_dma", "dsem", "dval", "phase")

    def __init__(self, eng, fn, is_dma=False):
        self.eng = eng
        self.fn = fn
        self.deps = []
        self.signal = False
        self.idx = 0
        self.is_dma = is_dma
        self.dsem = None
        self.dval = 0


ENGS = ("pe", "act", "dve", "pool", "sp")


class TT:
    __slots__ = ("ap", "b")

    def __init__(self, ap, b):
        self.ap = ap
        self.b = b

    def __getitem__(self, k):
        return self.ap[k]


def _b(x):
    return x.b if isinstance(x, TT) else x


class Prog:
    def __init__(self, nc, es):
        self.nc = nc
        self.es = es
        self.ops = {e: [] for e in ENGS}
        self.dma_bufs = []
        self.nbuf = 0
        self.last = {e: None for e in ENGS}
        self.bar_deps = []
        self.bar_id = 0
        self.bar_seen = {e: 0 for e in ENGS}
        self.free_sems = []
        self.sem_tot = {}
        self.phase = "p0"

    def buf(self, name=None):
        self.nbuf += 1
        return Buf(name or f"b{self.nbuf}")

    def barrier(self):
        deps = [op for op in self.last.values() if op is not None and not op.is_dma]
        for b in self.dma_bufs:
            if b.lastop is not None:
                deps.append(b.lastop)
            self.free_sems.append((b.sem, b.cnt))
            b.sem = None
            b.lastop = None
        self.dma_bufs = []
        self.bar_deps = deps
        self.bar_id += 1

    def _add(self, op, reads, writes):
        deps = []
        if self.bar_seen[op.eng] < self.bar_id:
            self.bar_seen[op.eng] = self.bar_id
            deps.extend(self.bar_deps)
        for b in reads:
            if b.lw is not None:
                deps.append(b.lw)
        for b in writes:
            if b.lw is not None:
                deps.append(b.lw)
            deps.extend(b.rd)
        seen = set()
        for d in deps:
            if d is op or id(d) in seen:
                continue
            seen.add(id(d))
            if d.eng == "pe" and op.eng == "pe" and not d.is_dma and not op.is_dma:
                continue
            op.deps.append(d)
            if not d.is_dma:
                d.signal = True
        for b in reads:
            b.rd.append(op)
        for b in writes:
            b.lw = op
            b.rd = []
        op.phase = self.phase
        self.ops[op.eng].append(op)
        if not op.is_dma:
            self.last[op.eng] = op
        return op

    def op(self, eng, fn, reads=(), writes=()):
        return self._add(Op(eng, fn), [_b(x) for x in reads], [_b(x) for x in writes])

    def dma(self, q, fn, reads=(), writes=()):
        reads = [_b(x) for x in reads]
        writes = [_b(x) for x in writes]
        op = Op(q, fn, is_dma=True)
        prim = writes[0] if writes else reads[0]
        if prim.sem is None:
            if self.free_sems:
                prim.sem, prim.cnt = self.free_sems.pop()
            else:
                prim.sem, prim.cnt = self.es.enter_context(self.nc.semaphore(f"d{len(self.sem_tot)}")), 0
            self.dma_bufs.append(prim)
        prim.cnt += 1
        op.dsem = prim.sem
        op.dval = 16 * prim.cnt
        self.sem_tot[id(prim.sem)] = (prim.sem, op.dval)
        r = self._add(op, reads, writes)
        prim.lastop = op
        return r

    def emit(self):
        nc, es = self.nc, self.es
        psem = {e: es.enter_context(nc.semaphore(f"p_{e}")) for e in ENGS}
        for e in ENGS:
            k = 0
            for op in self.ops[e]:
                if op.signal:
                    k += 1
                    op.idx = k
        block = es.enter_context(nc.Block())
        final = list(self.sem_tot.values())

        def run(ename, eng):
            known = {}
            for op in self.ops[ename]:
                for d in op.deps:
                    if d.is_dma:
                        s, v = d.dsem, d.dval
                    else:
                        s, v = psem[d.eng], d.idx
                    key = id(s)
                    if known.get(key, 0) >= v:
                        continue
                    known[key] = v
                    eng.wait_ge(s, v)
                ins = op.fn(eng)
                if op.is_dma:
                    ins.then_inc(op.dsem, 16)
                elif op.signal:
                    ins.then_inc(psem[ename], 1)
            if ename == "sp":
                for s, v in final:
                    if known.get(id(s), 0) < v:
                        eng.wait_ge(s, v)

        @block.tensor
        def _(eng):
            run("pe", eng)

        @block.scalar
        def _(eng):
            run("act", eng)

        @block.vector
        def _(eng):
            run("dve", eng)

        @block.gpsimd
        def _(eng):
            run("pool", eng)

        @block.sync
        def _(eng):
            run("sp", eng)


class Arena:
    def __init__(self, P, h, lo, hi, name="a"):
        self.P, self.h, self.lo, self.hi, self.off, self.name = P, h, lo, hi, lo, name
        self.n = 0

    def alloc(self, shape, dt, name=None):
        isz = {F32: 4, BF16: 2, I32: 4, U8: 1}[dt]
        n = 1
        for s in shape[1:]:
            n *= s
        nb = (n * isz + 31) // 32 * 32
        assert self.off + nb <= self.hi, f"arena {self.name} overflow: need {nb} at {self.off} (hi {self.hi})"
        v = self.h[0:shape[0], self.off:self.off + n * isz].bitcast(dt)
        self.off += nb
        if len(shape) == 3:
            v = v.rearrange("p (a b) -> p a b", a=shape[1], b=shape[2])
        elif len(shape) == 4:
            v = v.rearrange("p (a b c) -> p a b c", a=shape[1], b=shape[2], c=shape[3])
        elif len(shape) == 5:
            v = v.rearrange("p (a b c d) -> p a b c d", a=shape[1], b=shape[2], c=shape[3], d=shape[4])
        self.n += 1
        return TT(v, self.P.buf(name or f"{self.name}{self.n}"))

    def sub(self, nbytes, name):
        nbytes = (nbytes + 31) // 32 * 32
        assert self.off + nbytes <= self.hi, f"arena {self.name} overflow (sub {name})"
        a = Arena(self.P, self.h, self.off, self.off + nbytes, name)
        self.off += nbytes
        return a

    def mark(self):
        return self.off

    def reset(self, m=None):
        self.off = self.lo if m is None else m


class K:
    def __init__(self, nc, es, dbg):
        self.nc, self.es, self.dbg = nc, es, dbg
        self.P = Prog(nc, es)
        self.dram = {}
        sb = es.enter_context(nc.sbuf_tensor("arena", [128, 192 * 1024], U8))
        self.A = Arena(self.P, sb, 0, 192 * 1024, "A")
        ps = es.enter_context(nc.psum_tensor("psum", [128, 8 * 512], F32))
        self.psb = [TT(ps[:, k * 512:(k + 1) * 512], self.P.buf(f"ps{k}")) for k in range(8)]
        self.rr = 0
        self.rot = list(range(8))

    def din(self, name, shape, dt=F32):
        t = self.nc.dram_tensor(name, list(shape), dt, kind="ExternalInput").ap()
        self.dram[name] = t
        return t

    def dout(self, name, shape, dt=F32):
        t = self.nc.dram_tensor(name, list(shape), dt, kind="ExternalOutput").ap()
        self.dram[name] = t
        return t

    def dscr(self, name, shape, dt):
        t = self.nc.dram_tensor(name, list(shape), dt, kind="Internal").ap()
        return TT(t, self.P.buf(name))

    def dump(self, name, t, shape, dt=F32):
        if not self.dbg.get("dumps"):
            return
        o = self.dout("dbg_" + name, shape, dt)
        self.dma("sp", o, t.ap if isinstance(t, TT) else t, [t], [])

    def ps(self):
        k = self.rot[self.rr % len(self.rot)]
        self.rr += 1
        return self.psb[k]

    def mm(self, out, lhsT, rhs, start, stop, R, W):
        self.P.op("pe", lambda e: e.matmul(out, lhsT=lhsT, rhs=rhs, start=start, stop=stop), R, W)

    def tr(self, out, in_, ident, R, W):
        self.P.op("pe", lambda e: e.transpose(out, in_, ident), R, W)

    def act(self, out, in_, func, R, W, **kw):
        self.P.op("act", lambda e: e.activation(out=out, in_=in_, func=func, **kw), R, W)

    def cp(self, eng, out, in_, R, W):
        if eng == "act":
            self.P.op("act", lambda e: e.copy(out=out, in_=in_), R, W)
        else:
            self.P.op(eng, lambda e: e.tensor_copy(out=out, in_=in_), R, W)

    def tt(self, eng, out, in0, in1, op, R, W):
        self.P.op(eng, lambda e: e.tensor_tensor(out=out, in0=in0, in1=in1, op=op), R, W)

    def ts(self, eng, out, in0, s1, s2, op0, op1, R, W):
        if op1 is None:
            self.P.op(eng, lambda e: e.tensor_scalar(out=out, in0=in0, scalar1=s1, scalar2=None, op0=op0), R, W)
        else:
            self.P.op(eng, lambda e: e.tensor_scalar(out=out, in0=in0, scalar1=s1, scalar2=s2, op0=op0, op1=op1), R, W)

    def stt(self, out, in0, scalar, in1, op0, op1, R, W):
        self.P.op("dve", lambda e: e.scalar_tensor_tensor(out=out, in0=in0, scalar=scalar, in1=in1, op0=op0, op1=op1), R, W)

    def memset(self, eng, ap, val, W):
        self.P.op(eng, lambda e: e.memset(ap, val), (), W)

    def dma(self, q, out, in_, R, W, **kw):
        self.P.dma(q, lambda e: e.dma_start(out=out, in_=in_, **kw), R, W)


def build(dbg=None):
    dbg = dbg or {}
    nc = bass.Bass("TRN2", target_bir_lowering=False)
    es = ExitStack()
    with es:
        k = K(nc, es, dbg)
        build_program(k)
        k.P.emit()
        if dbg.get("want_prog"):
            dbg["prog"] = k.P
    return nc


def build_program(k):
    P, A, dbg = k.P, k.A, k.dbg
    x_prompt = k.din("x_prompt", [SEQ, D])
    xl = k.din("xl", [NSLOT, 128, D])
    xs = k.din("xs", [32, D])
    c_all = k.din("c_all", [5, D])
    norm_w = k.din("norm_w", [1, D])
    w_ada = k.din("w_ada", [D, 3 * D])
    b_ada = k.din("b_ada", [1, 3 * D])
    w_in = k.din("w_in", [D, INW])
    w_a2 = k.din("w_a2", [16, 512])
    b_a = k.din("b_a", [1, 512])
    rope_g = k.din("rope_g", [SEQ, 64])
    mcol = k.din("mcol", [128, NTG])
    cU = k.din("cU", [128, 128])
    kv_cmp_p = k.dout("kv_cmp_p", [SEQ, 512])
    kv_sel_p = k.dout("kv_sel_p", [SEQ, 512])
    kv_win_p = k.dout("kv_win_p", [512, 512])
    gla_p = k.dout("gla_p", [128, 1024])

    Wb = k.dscr("Wb", [128, KC, INW], BF16)
    gsc = k.dscr("gsc", [5, D], F32)
    KTsel = k.dscr("KTsel", [128, 2, SEQ], BF16)
    KTwin = k.dscr("KTwin", [128, 2, SEQ], BF16)
    Vsel = k.dscr("Vsel", [SEQ, 260], BF16)
    Vwin = k.dscr("Vwin", [SEQ, 260], BF16)
    kvcT = k.dscr("kvcT", [128, 4, SEQ + 16], BF16)
    SS = k.dscr("SS", [NSLOT, 128, 1024], F32)

    ident_f = A.alloc([128, 128], F32, "ident_f")
    ident_b = A.alloc([128, 128], BF16, "ident_b")
    ones_f = A.alloc([128, 128], F32, "ones_f")
    Irep = A.alloc([128, 4, 128], BF16, "Irep")
    g1T = A.alloc([128, KC, 5], F32, "g1T")
    shT = A.alloc([128, KC, 5], F32, "shT")
    k.memset("pool", ones_f[:], 1.0, [ones_f])
    P.op("pool", lambda e: e.affine_select(out=ident_f[:], in_=ones_f[:], pattern=[[-1, 128]], compare_op=ALU.is_equal,
                                           fill=0.0, base=0, channel_multiplier=1), [ones_f], [ident_f])
    k.cp("dve", ident_b[:], ident_f[:], [ident_f], [ident_b])
    for g in range(4):
        k.cp("dve", Irep[:, g, :], ident_f[:], [ident_f], [Irep])
    pm = A.mark()

    c_nat = A.alloc([5, D], F32, "c_nat")
    cT = A.alloc([128, KC, 5], F32, "cT")
    modN = A.alloc([6, 3 * D], F32, "modN")
    bada = A.alloc([5, 3 * D], F32, "bada")
    wa = [A.alloc([128, 3072], F32, f"wa{i}") for i in range(2)]
    modT = A.alloc([128, 32, 6], F32, "modT")
    k.dma("sp", c_nat[:], c_all[:, :], [], [c_nat])
    k.dma("sp", bada[:], b_ada.partition_broadcast(5), [], [bada])
    k.memset("pool", modN[:], 0.0, [modN])
    pc = k.psb[7]
    for kc in range(KC):
        k.tr(pc[:, kc * 5:(kc + 1) * 5], c_nat[0:5, kc * 128:(kc + 1) * 128], ident_f[0:5, 0:5], [c_nat, ident_f], [pc])
    k.cp("dve", cT[:].rearrange("p a b -> p (a b)"), pc[:, 0:80], [pc], [cT])
    n = 0
    for half in range(2):
        for kc in range(KC):
            w = wa[n % 2]
            n += 1
            k.dma("sp", w[:], w_ada[kc * 128:(kc + 1) * 128, half * 3072:(half + 1) * 3072], [], [w])
            for g in range(6):
                k.mm(k.psb[g][0:5, :], cT[:, kc, :], w[:, g * 512:(g + 1) * 512], kc == 0, kc == KC - 1, [cT, w], [k.psb[g]])
        for g in range(6):
            c0 = half * 3072 + g * 512
            k.tt("dve", modN[0:5, c0:c0 + 512], k.psb[g][0:5, :], bada[:, c0:c0 + 512], ALU.add, [k.psb[g], bada], [modN])
    k.dma("sp", modN[5:6, 0:D], norm_w[:, :], [], [modN])
    k.ts("dve", modN[0:5, D:2 * D], modN[0:5, D:2 * D], 1.0, None, ALU.add, None, [modN], [modN])
    for blk in range(32):
        pb = k.psb[blk // 16]
        k.tr(pb[:, (blk % 16) * 6:(blk % 16) * 6 + 6], modN[0:6, blk * 128:(blk + 1) * 128], ident_f[0:6, 0:6], [modN, ident_f], [pb])
    for hb in range(2):
        k.cp("dve", modT[:, hb * 16:(hb + 1) * 16, :].rearrange("p a b -> p (a b)"), k.psb[hb][:, 0:96], [k.psb[hb]], [modT])
    k.tt("dve", g1T[:], modT[:, 16:32, 0:5], modT[:, 0:16, 5:6].broadcast_to([128, 16, 5]), ALU.mult, [modT], [g1T])
    k.cp("dve", shT[:], modT[:, 0:16, 0:5], [modT], [shT])
    k.dma("sp", gsc[:, :], modN[0:5, 2 * D:3 * D], [modN], [gsc])
    if dbg.get("mod"):
        o = k.dout("dbg_mod", [6, 3 * D])
        k.dma("sp", o[:, :], modN[:], [modN], [])
    P.barrier()
    A.reset(pm)

    P.phase = 'prep'
    w_o_nsa = k.din("w_o_nsa", [1024, D])
    w_o_gla = k.din("w_o_gla", [1024, D])
    w_out = k.din("w_out", [D, D])
    Won = k.dscr("Won", [128, 8, D], BF16)
    Wog = k.dscr("Wog", [128, 8, D], BF16)
    Wout = k.dscr("Wout", [128, KC, D], BF16)
    prep_n = [0]

    def prep_piece(stg, wbf, src, dst_ap, dst, kc, c0, c1):
        n_ = prep_n[0]
        prep_n[0] += 1
        s_, wb_ = stg[n_ % 2], wbf[n_ % 2]
        w = c1 - c0
        k.dma("sp", s_[:, 0:w], src[kc * 128:(kc + 1) * 128, c0:c1], [], [s_])
        k.cp("dve" if n_ % 2 == 0 else "act", wb_[:, 0:w], s_[:, 0:w], [s_], [wb_])
        k.dma("pool", dst_ap[:, kc, c0:c1], wb_[:, 0:w], [wb_], [dst])

    stg = [A.alloc([128, 2704], F32, f"stg{i}") for i in range(2)]
    wbf = [A.alloc([128, 2704], BF16, f"wbf{i}") for i in range(2)]
    for kc in range(KC):
        for (c0, c1) in ((C_KV, C_KV + 1536), (C_KG, C_KG + 1552)):
            prep_piece(stg, wbf, w_in, Wb.ap, Wb, kc, c0, c1)
    prepB = []
    for kc in range(KC):
        for (c0, c1) in ((0, 1024), (2560, 4144), (5696, 8256), (8256, INW)):
            prepB.append((w_in, Wb, kc, c0, c1))
    for src, dst, nk in ((w_o_nsa, Won, 8), (w_o_gla, Wog, 8), (w_out, Wout, KC)):
        for kc in range(nk):
            prepB.append((src, dst, kc, 0, D))
    P.barrier()
    A.reset(pm)

    P.phase = 'G'
    NG = 3088
    WG = A.alloc([128, KC, NG], BF16, "WG")
    k.dma("sp", WG[:, :, 0:1536], Wb[:, :, C_KV:C_KV + 1536], [Wb], [WG])
    k.dma("pool", WG[:, :, 1536:NG], Wb[:, :, C_KG:C_KG + 1552], [Wb], [WG])
    wa2b = A.alloc([17, 512], F32, "wa2b")
    k.dma("sp", wa2b[0:16, :], w_a2[:, :], [], [wa2b])
    k.dma("sp", wa2b[16:17, :], b_a[:, :], [], [wa2b])
    Ugt = A.alloc([128, 128], F32, "Ugt")
    k.dma("sp", Ugt[:], cU[:, :], [], [Ugt])
    negc = A.alloc([128, 1], F32, "negc")
    k.memset("pool", negc[:], -1.0 / 16.0, [negc])
    mc = A.alloc([128, NTG], F32, "mc")
    k.dma("sp", mc[:], mcol[:, :], [], [mc])
    S = A.alloc([128, 1024], F32, "S")
    k.memset("pool", S[:], 0.0, [S])
    Ssv = [A.alloc([128, 1024], F32, f"Ssv{i}") for i in range(2)]
    xt = [A.alloc([128, D], F32, f"xt{i}") for i in range(2)]
    xn = [A.alloc([128, D], BF16, f"xn{i}") for i in range(2)]
    hT = [A.alloc([128, KC, 128], BF16, f"hT{i}") for i in range(2)]
    cs = [A.alloc([128, 64], F32, f"cs{i}") for i in range(2)]
    st = [A.alloc([128, 8], F32, f"st{i}") for i in range(2)]
    kvst = [[A.alloc([128, 512], F32, f"kvst{i}_{b}") for b in range(3)] for i in range(2)]
    rt = [A.alloc([128, 4, 4, 32], F32, f"rt{i}") for i in range(2)]
    c16 = [A.alloc([128, 3, 512], BF16, f"c16_{i}") for i in range(2)]
    tT = [A.alloc([128, 8, 128], BF16, f"tT{i}") for i in range(2)]
    vaug = [A.alloc([128, 2, 4, 65], BF16, f"vaug{i}") for i in range(2)]
    agT = [A.alloc([17, 128], F32, f"agT{i}") for i in range(2)]
    ex = A.alloc([128, 512], F32, "ex")
    L = [A.alloc([128, 512], F32, f"L{i}") for i in range(2)]
    E = A.alloc([128, 512], F32, "E")
    Kd2 = [A.alloc([128, 512], BF16, f"Kd2_{i}") for i in range(2)]
    Vb = [A.alloc([128, 1024], BF16, f"Vb{i}") for i in range(2)]
    dec = [A.alloc([128, 4], F32, f"dec{i}") for i in range(2)]
    kgs = [A.alloc([128, 512], BF16, f"kgs{i}") for i in range(2)]
    for i in range(2):
        k.memset("pool", vaug[i][:], 1.0, [vaug[i]])
        k.memset("pool", agT[i][:], 1.0, [agT[i]])
    ntg = dbg.get("ntg", NTG)

    def g_pre(i):
        p = i % 2
        r0 = i * 128
        X, XN, H, CS, ST, RT = xt[p], xn[p], hT[p], cs[p], st[p], rt[p]
        k.dma("sp", X[:], x_prompt[r0:r0 + 128, :], [], [X])
        k.dma("sp", CS[:], rope_g[r0:r0 + 128, :], [], [CS])
        k.act(XN[:], X[:], AF.Square, [X], [XN, ST], accum_out=ST[:, 0:1])
        k.ts("dve", ST[:, 1:2], ST[:, 0:1], 1.0 / D, EPS, ALU.mult, ALU.add, [ST], [ST])
        k.act(ST[:, 2:3], ST[:, 1:2], AF.Sqrt, [ST], [ST])
        P.op("dve", lambda e, ST=ST: e.reciprocal(out=ST[:, 3:4], in_=ST[:, 2:3]), [ST], [ST])
        k.ts("dve", XN[:], X[:], ST[:, 3:4], None, ALU.mult, None, [X, ST], [XN])
        for q4 in range(4):
            pb = k.ps()
            pbv = pb[:].bitcast(BF16)
            for u in range(4):
                kc = q4 * 4 + u
                k.tr(pbv[:, u * 128:(u + 1) * 128], XN[:, kc * 128:(kc + 1) * 128], ident_b[:], [XN, ident_b], [pb])
            for u in range(4):
                kc = q4 * 4 + u
                k.act(H[:, kc, :], pbv[:, u * 128:(u + 1) * 128], AF.Identity, [pb, g1T, shT], [H],
                      scale=g1T[:, kc, 0:1], bias=shT[:, kc, 0:1])

    def g_tailA(i):
        p = i % 2
        LL, KD, DC = L[p], Kd2[p], dec[p]
        pd = k.ps()
        k.mm(pd[:], Ugt[:], LL[:], True, True, [Ugt, LL], [pd])
        k.act(E[:], pd[:], AF.Exp, [pd], [E])
        k.tt("dve", KD[:], kgs[p][:], E[:], ALU.mult, [kgs[p], E], [KD])
        pl = k.ps()
        for h in range(4):
            k.mm(pl[:, h:h + 1], LL[:, h * 128:(h + 1) * 128], negc[:, 0:1], True, True, [LL, negc], [pl])
        k.act(DC[:], pl[:, 0:4], AF.Exp, [pl], [DC])

    def g_tailB(i):
        p = i % 2
        KD, VB, DC = Kd2[p], Vb[p], dec[p]
        SV = Ssv[(i // 8) % 2]
        if i % 8 == 0:
            k.ts("pool", SV[:], S[:], mc[:, i:i + 1], None, ALU.mult, None, [S, mc], [SV])
        else:
            k.stt(SV[:], S[:], mc[:, i:i + 1], SV[:], ALU.mult, ALU.add, [S, mc, SV], [SV])
        if i % 8 == 7:
            k.dma("pool", SS.ap[i // 8], SV[:], [SV], [SS])
        for hp in range(2):
            pb = k.ps()
            for u in range(2):
                h = hp * 2 + u
                k.mm(pb[:, u * 256:(u + 1) * 256], KD[:, h * 128:(h + 1) * 128], VB[:, h * 256:(h + 1) * 256], True, True, [KD, VB], [pb])
            for u in range(2):
                h = hp * 2 + u
                k.stt(S[:, h * 256:(h + 1) * 256], S[:, h * 256:(h + 1) * 256], DC[:, h:h + 1], pb[:, u * 256:(u + 1) * 256],
                      ALU.mult, ALU.add, [S, DC, pb], [S])

    g_pre(0)
    for i in range(ntg):
        if i + 1 < ntg:
            g_pre(i + 1)
        p = i % 2
        r0 = i * 128
        X, XN, H, CS, ST, RT = xt[p], xn[p], hT[p], cs[p], st[p], rt[p]
        pa = k.ps()
        for kc in range(KC):
            k.mm(pa[0:16, 0:128], WG[:, kc, 3072:3088], H[:, kc, :], kc == 0, kc == KC - 1, [WG, H], [pa])
        AG = agT[p]
        k.cp("dve", AG[0:16, :], pa[0:16, 0:128], [pa], [AG])
        if i > 0:
            g_tailA(i - 1)
        pk = []
        for g in range(6):
            pb = k.ps()
            for kc in range(KC):
                k.mm(pb[:], H[:, kc, :], WG[:, kc, g * 512:(g + 1) * 512], kc == 0, kc == KC - 1, [WG, H], [pb])
            pk.append(pb)
        cosb = CS[:, 0:32].unsqueeze(1).broadcast_to([128, 4, 32])
        sinb = CS[:, 32:64].unsqueeze(1).broadcast_to([128, 4, 32])
        C16 = c16[p]
        for br in range(3):
            pb = pk[br]
            KS = kvst[p][br]
            x1 = pb[:, 0:256].rearrange("p (h t e) -> p h t e", h=4, t=2, e=32)[:, :, 0, :]
            x2 = pb[:, 0:256].rearrange("p (h t e) -> p h t e", h=4, t=2, e=32)[:, :, 1, :]
            o1 = KS[:, 0:256].rearrange("p (h t e) -> p h t e", h=4, t=2, e=32)[:, :, 0, :]
            o2 = KS[:, 0:256].rearrange("p (h t e) -> p h t e", h=4, t=2, e=32)[:, :, 1, :]
            k.tt("dve", RT[:, 0], x1, cosb, ALU.mult, [pb, CS], [RT])
            k.tt("dve", RT[:, 1], x2, sinb, ALU.mult, [pb, CS], [RT])
            k.tt("dve", RT[:, 2], x2, cosb, ALU.mult, [pb, CS], [RT])
            k.tt("dve", RT[:, 3], x1, sinb, ALU.mult, [pb, CS], [RT])
            k.tt("pool", o1, RT[:, 0], RT[:, 1], ALU.subtract, [RT], [KS])
            k.tt("pool", o2, RT[:, 2], RT[:, 3], ALU.add, [RT], [KS])
            k.cp("act", KS[:, 256:512], pb[:, 256:512], [pb], [KS])
            k.cp("pool", C16[:, br, :], KS[:], [KS], [C16])
        k.dma("pool", kv_cmp_p[r0:r0 + 128, :], kvst[p][0][:], [kvst[p][0]], [])
        k.dma("pool", kv_sel_p[r0:r0 + 128, :], kvst[p][1][:], [kvst[p][1]], [])
        if i >= NTG - 4:
            w0 = (i - (NTG - 4)) * 128
            k.dma("pool", kv_win_p[w0:w0 + 128, :], kvst[p][2][:], [kvst[p][2]], [])
        VA = vaug[p]
        for j, br in enumerate((1, 2)):
            k.cp("pool", VA[:, j, :, 0:64], C16[:, br, 256:512].rearrange("p (h e) -> p h e", h=4), [C16], [VA])
        k.dma("pool", Vsel.ap[r0:r0 + 128, :], VA[:, 0].rearrange("p h e -> p (h e)"), [VA], [Vsel])
        k.dma("pool", Vwin.ap[r0:r0 + 128, :], VA[:, 1].rearrange("p h e -> p (h e)"), [VA], [Vwin])
        pb = k.ps()
        pbv = pb[:].bitcast(BF16)
        srcs = [(0, 0), (0, 128), (0, 256), (0, 384), (1, 0), (1, 128), (2, 0), (2, 128)]
        for u, (br, c0) in enumerate(srcs):
            k.tr(pbv[:, u * 128:(u + 1) * 128], C16[:, br, c0:c0 + 128], ident_b[:], [C16, ident_b], [pb])
        TTl = tT[p]
        k.cp("act", TTl[:].rearrange("p a b -> p (a b)"), pbv[:, 0:1024], [pb], [TTl])
        k.dma("pool", kvcT.ap[:, :, r0:r0 + 128], TTl[:, 0:4, :], [TTl], [kvcT])
        k.dma("pool", KTsel.ap[:, :, r0:r0 + 128], TTl[:, 4:6, :], [TTl], [KTsel])
        k.dma("pool", KTwin.ap[:, :, r0:r0 + 128], TTl[:, 6:8, :], [TTl], [KTwin])
        VB = Vb[p]
        k.cp("act", VB[:, 0:512], pk[4][:], [pk[4]], [VB])
        k.cp("act", VB[:, 512:1024], pk[5][:], [pk[5]], [VB])
        pp = k.ps()
        k.mm(pp[:], AG[:, :], wa2b[:, :], True, True, [AG, wa2b], [pp])
        k.act(ex[:], pp[:], AF.Exp, [pp], [ex], scale=-1.0)
        LL = L[p]
        k.act(LL[:], ex[:], AF.Ln, [ex, ones_f], [LL], bias=ones_f[:, 0:1], scale=1.0)
        k.cp("dve", kgs[p][:], pk[3][:], [pk[3]], [kgs[p]])
        if i > 0:
            g_tailB(i - 1)
    g_tailA(ntg - 1)
    g_tailB(ntg - 1)
    k.dma("pool", gla_p[:, :], S[:], [S], [])
    P.barrier()
    A.reset(pm)
    if dbg.get("stop_after_g"):
        return
    env = dict(ident_f=ident_f, ident_b=ident_b, ones_f=ones_f, Irep=Irep, g1T=g1T, shT=shT,
               Wb=Wb, gsc=gsc, KTsel=KTsel, KTwin=KTwin, Vsel=Vsel, Vwin=Vwin, kvcT=kvcT, SS=SS,
               Won=Won, Wog=Wog, Wout=Wout, xl=xl, xs=xs, w_a2=w_a2, b_a=b_a, prepB=prepB, prep_piece=prep_piece)
    phase_rest(k, env)


O_UGT, O_UGE, O_MTRI, O_UGE32, O_MTRI32, O_UGT32 = 0, 128, 256, 384, 416, 448
O_NEGIND, O_IND, O_INDROW, O_NEWB, O_QLOC, O_QPS, O_NB30, O_IOTA = 480, 484, 488, 616, 648, 649, 650, 654
NCST = O_IOTA + 1536
TINY = 1e-30


def phase_rest(k, env):
    P, A, dbg = k.P, k.A, k.dbg
    ident_f, ident_b, ones_f, Irep, g1T, shT = (env[n] for n in ("ident_f", "ident_b", "ones_f", "Irep", "g1T", "shT"))
    Wb, gsc, KTsel, KTwin, Vsel, Vwin, kvcT, SS = (env[n] for n in ("Wb", "gsc", "KTsel", "KTwin", "Vsel", "Vwin", "kvcT", "SS"))
    Won, Wog, Wout, xl, xs, w_a2, b_a = (env[n] for n in ("Won", "Wog", "Wout", "xl", "xs", "w_a2", "b_a"))
    cmp_pos = k.din("cmp_pos", [64, 64])
    cmp_w1 = k.din("cmp_w1", [2, 32, 64, 128])
    cmp_b1 = k.din("cmp_b1", [2, 128])
    cmp_w2 = k.din("cmp_w2", [2, 128, 64])
    cmp_b2 = k.din("cmp_b2", [2, 64])
    gnw_d = k.din("gla_norm_w", [1, 256])
    fnw_d = k.din("final_norm_w", [1, D])
    cache_cmp = k.din("cache_cmp", [327680, 512])
    cache_sel = k.din("cache_sel", [327680, 512])
    cache_win = k.din("cache_win", [4, 512, 512])
    state_gla = k.din("state_gla", [4, 4, 128, 256])
    page_table = k.din("page_table", [1, 256], I32)
    cst_d = k.din("cst", [128, NCST])
    cm2s = k.din("cm2s", [128, 4, 128])
    ropel = k.din("ropel", [9, 128, 64])
    ropes = k.din("ropes", [32, 64])
    y_l = k.dout("y_l", [NSLOT, 128, D])
    y_s = k.dout("y_s", [32, D])
    kv_cmp_s = k.dout("kv_cmp_s", [32, 512])
    kv_sel_s = k.dout("kv_sel_s", [32, 512])
    kv_win_s = k.dout("kv_win_s", [4, 512, 512])
    gla_s = k.dout("gla_s", [4, 128, 1024])
    KcS = k.dscr("KcS", [5, 128, 1024], BF16)
    VcS = k.dscr("VcS", [5, 128, 3088], BF16)
    hTs = k.dscr("hTs", [128, KC, 1056], BF16)

    cst = A.alloc([128, NCST], F32, "cst")
    k.dma("sp", cst[:], cst_d[:, :], [], [cst])
    idx = A.alloc([128, 256], I32, "idx")
    wa2b = A.alloc([17, 512], F32, "wa2b2")
    k.dma("sp", wa2b[0:16, :], w_a2[:, :], [], [wa2b])
    k.dma("sp", wa2b[16:17, :], b_a[:, :], [], [wa2b])
    pm2 = A.mark()
    iota = cst[:, O_IOTA:O_IOTA + 1536]
    qloc = cst[:, O_QLOC:O_QLOC + 1]

    P.phase = 'C'
    w1T = A.alloc([128, 2, 32, 128], BF16, "w1T")
    w1s = A.alloc([128, 32, 128], F32, "w1s")
    for x in range(2):
        for b in range(2):
            k.dma("sp", w1s[64 * b:64 * b + 64], cmp_w1[x].rearrange("l d e -> d l e"), [], [w1s])
        k.cp("dve", w1T[:, x], w1s[:], [w1s], [w1T])
    posn = A.alloc([64, 64], F32, "posn")
    k.dma("sp", posn[:], cmp_pos[:, :], [], [posn])
    pp = k.ps()
    k.tr(pp[0:64, 0:64], posn[:], ident_f[0:64, 0:64], [posn, ident_f], [pp])
    posT = A.alloc([64, 64], BF16, "posT")
    k.cp("dve", posT[:], pp[0:64, 0:64], [pp], [posT])
    pq = k.ps()
    for x in range(2):
        for l in range(32):
            k.mm(pq[:, x:x + 1], w1T[0:64, x, l, :], posT[0:64, x * 32 + l:x * 32 + l + 1], l == 0, l == 31, [w1T, posT], [pq])
    b1c = A.alloc([128, 2], F32, "b1c")
    k.dma("sp", b1c[:], cmp_b1.rearrange("x e -> e x"), [], [b1c], allow_slow_non_contiguous=True)
    posb = A.alloc([128, 2], F32, "posb")
    k.tt("dve", posb[:], pq[:, 0:2], b1c[:], ALU.add, [pq, b1c], [posb])
    w2s = A.alloc([128, 2, 64], F32, "w2s")
    k.dma("sp", w2s[:], cmp_w2.rearrange("x e d -> e x d"), [], [w2s])
    w2pad = A.alloc([128, 2, 128], BF16, "w2pad")
    k.memset("pool", w2pad[:], 0.0, [w2pad])
    k.cp("dve", w2pad[:, 0, 0:64], w2s[:, 0, :], [w2s], [w2pad])
    k.cp("dve", w2pad[:, 1, 64:128], w2s[:, 0, :], [w2s], [w2pad])
    w2v = A.alloc([128, 64], BF16, "w2v")
    k.cp("dve", w2v[:], w2s[:, 1, :], [w2s], [w2v])
    b2k = A.alloc([128, 1], F32, "b2k")
    for b in range(2):
        k.dma("sp", b2k[64 * b:64 * b + 64], cmp_b2[0:1, :].rearrange("o d -> d o"), [], [b2k], allow_slow_non_contiguous=True)
    b2v = A.alloc([128, 64], F32, "b2v")
    k.dma("sp", b2v[:], cmp_b2[1:2, :].partition_broadcast(128), [], [b2v])
    cm = A.alloc([128, 4, 128], F32, "cm")
    k.dma("sp", cm[:], cm2s[:, :, :], [], [cm])
    KcT = A.alloc([128, 2, 512], BF16, "KcT")
    Vca = A.alloc([128, 4, 4, 193], BF16, "Vca")
    k.memset("pool", Vca[:], 1.0, [Vca])
    for h in range(4):
        k.cp("pool", Vca[:, :, h, 65:193], cm[:], [cm], [Vca])
    kvT = A.alloc([128, 4, 16, 513], BF16, "kvT")
    kvg = [TT(kvT.ap, P.buf(f"kvTg{g}")) for g in range(9)]
    hid = [A.alloc([128, 512], BF16, f"hid{i}") for i in range(2)]
    hn = [0]

    def compress(dst):
        for x in range(2):
            for a in range(2):
                pK = k.ps() if x == 0 else None
                for b in range(2):
                    h = 2 * a + b
                    gi = x * 2 + a
                    ph = k.ps()
                    for l in range(32):
                        j, s_ = l // 16, l % 16
                        k.mm(ph[:], w1T[64 * b:64 * b + 64, x, l, :], kvT[64 * b:64 * b + 64, gi, s_, j:j + 512],
                             l == 0, l == 31, [w1T] + kvg, [ph])
                    H_ = hid[hn[0] % 2]
                    hn[0] += 1
                    k.act(H_[:], ph[:], AF.Silu, [ph, posb], [H_], bias=posb[:, x:x + 1])
                    k.memset("pool", H_[:, 511:512], 0.0, [H_])
                    if x == 0:
                        k.mm(pK[:], w2pad[:, b, :], H_[:], b == 0, b == 1, [w2pad, H_], [pK])
                    else:
                        pv = k.ps()
                        for ct in range(4):
                            k.mm(pv[:, ct * 64:(ct + 1) * 64], H_[:, ct * 128:(ct + 1) * 128], w2v[:], True, True, [H_, w2v], [pv])
                        k.tt("dve", Vca[:, :, h, 0:64], pv[:, 0:256].rearrange("p (c d) -> p c d", c=4),
                             b2v[:].unsqueeze(1).broadcast_to([128, 4, 64]), ALU.add, [pv, b2v], [Vca])
                if x == 0:
                    k.act(KcT[:, a, :], pK[:], AF.Identity, [pK, b2k], [KcT], bias=b2k[:, 0:1])
        k.dma("sp", KcS.ap[dst], KcT[:].rearrange("p a c -> p (a c)"), [KcT], [KcS])
        k.dma("sp", VcS.ap[dst], Vca[:].rearrange("p c h w -> p (c h w)"), [Vca], [VcS])

    kst = [A.alloc([128, 4, 1024], BF16, f"kst{i}") for i in range(2)]
    for g in range(8):
        KS_ = kst[g % 2]
        k.dma("sp", KS_[:], kvcT.ap[:, :, g * 1024:(g + 1) * 1024], [kvcT], [KS_])
        k.cp(("act", "dve", "pool")[g % 3], kvT[:, :, :, g * 64:(g + 1) * 64], KS_[:].rearrange("p g (c s) -> p g s c", s=16), [KS_], [kvg[g]])
    k.memset("pool", kvT[:, :, :, 512:513], 0.0, [kvg[8]])
    compress(0)

    P.phase = 'SC'
    ptb = A.alloc([128, 256], I32, "ptb")
    iop = A.alloc([128, 1], F32, "iop")
    k.dma("sp", ptb[:], page_table.partition_broadcast(128), [], [ptb])
    P.op("pool", lambda e: e.iota(iop[:], pattern=[[0, 1]], base=0, channel_multiplier=1, allow_small_or_imprecise_dtypes=True), [], [iop])
    k.ts("dve", idx[:], ptb[:], 128.0, iop[:, 0:1], ALU.mult, ALU.add, [ptb, iop], [idx])
    gb = [A.alloc([128, 512], F32, f"gb{i}") for i in range(8)]

    def gather(dst, cache, col):
        P.dma("pool", lambda e: e.indirect_dma_start(out=dst[:], out_offset=None, in_=cache[:, :],
                                                     in_offset=bass.IndirectOffsetOnAxis(ap=idx[:, col:col + 1], axis=0)), [idx.b], [dst.b])

    n = 0
    nsb = dbg.get("nsb", 4)
    stgB = [A.alloc([128, 2704], F32, f"stgB{i}") for i in range(2)]
    wbfB = [A.alloc([128, 2704], BF16, f"wbfB{i}") for i in range(2)]
    prepB = list(env["prepB"])

    pend = []
    pbn = [0]

    def prep_finish():
        if pend:
            s_, wb_, dst, kc, c0, c1 = pend.pop(0)
            w = c1 - c0
            k.cp("act", wb_[:, 0:w], s_[:, 0:w], [s_], [wb_])
            k.dma("act", dst.ap[:, kc, c0:c1], wb_[:, 0:w], [wb_], [dst])

    def prep_next():
        if prepB:
            src, dst, kc, c0, c1 = prepB.pop(0)
            s_, wb_ = stgB[pbn[0] % 2], wbfB[pbn[0] % 2]
            pbn[0] += 1
            k.dma("act", s_[:, 0:c1 - c0], src[kc * 128:(kc + 1) * 128, c0:c1], [], [s_])
            prep_finish()
            pend.append((s_, wb_, dst, kc, c0, c1))
        else:
            prep_finish()

    for b in range(nsb):
        for j in range(64):
            if j % 2 == 0:
                prep_next()
            G_ = gb[n % 8]
            gather(G_, cache_cmp, b * 64 + j)
            pb = k.ps()
            for u in range(4):
                k.tr(pb[:, u * 128:(u + 1) * 128], G_[:, u * 128:(u + 1) * 128], ident_f[:], [G_, ident_f], [pb])
            k.cp("dve", kvT[:, :, :, j * 8:(j + 1) * 8],
                 pb[:, 0:512].rearrange("p (g c s) -> p g s c", g=4, s=16), [pb], [kvg[j // 8]])
            n += 1
        compress(1 + b)
    while prepB or pend:
        prep_next()
    P.barrier()
    A.reset(pm2)

    P.phase = 'Learly'
    regC = A.sub(60 * 1024, "C")
    QT = regC.alloc([128, 2, 4, 1056], BF16, "QT")
    qgT = regC.alloc([128, 4, 1056], BF16, "qgT")
    kgT = regC.alloc([128, 4, 1056], BF16, "kgT")
    vg = regC.alloc([128, 9, 1024], BF16, "vg")
    agTa = regC.alloc([17, 1056], F32, "agTa")
    gsig = regC.alloc([128, 9, 48], F32, "gsig")
    KTn = regC.alloc([128, 2, 2, 32], BF16, "KTn")
    Van = regC.alloc([32, 2, 4, 65], BF16, "Van")
    kgN = regC.alloc([32, 512], BF16, "kgN")
    mB = A.mark()
    hTa = A.alloc([128, KC, 1056], BF16, "hTa")
    xt2 = [A.alloc([128, D], F32, f"xt2_{i}") for i in range(2)]
    xn2 = [A.alloc([128, D], BF16, f"xn2_{i}") for i in range(2)]
    st2 = [A.alloc([128, 8], F32, f"st2_{i}") for i in range(2)]
    rq = A.alloc([128, 9, 64], F32, "rq")
    rs = A.alloc([32, 64], F32, "rs")
    Wt = [A.alloc([128, KC, 512], BF16, f"Wt{i}") for i in range(2)]
    rtq = [A.alloc([128, 4, 8, 32], F32, f"rtq{i}") for i in range(2)]
    qrot = [A.alloc([128, 4, 2, 64], BF16, f"qrot{i}") for i in range(2)]
    kss = [A.alloc([32, 512], F32, f"kss{i}") for i in range(3)]
    c16s = [A.alloc([32, 512], BF16, f"c16s{i}") for i in range(3)]
    k.dma("sp", rq[:], ropel.rearrange("t p c -> p t c"), [], [rq])
    k.dma("sp", rs[:], ropes[:, :], [], [rs])
    k.memset("pool", agTa[:], 1.0, [agTa])
    k.memset("pool", Van[:], 1.0, [Van])
    TILES = [(t, 128, 128 * t) for t in range(8)] + [(8, 32, 1024)]
    for (t, NT, tok0) in TILES:
        X, XN, ST = xt2[t % 2], xn2[t % 2], st2[t % 2]
        if t < 8:
            k.dma("sp", X[:], xl[t], [], [X])
        else:
            k.dma("sp", X[0:32], xs[:, :], [], [X])
        k.act(XN[0:NT], X[0:NT], AF.Square, [X], [XN, ST], accum_out=ST[0:NT, 0:1])
        k.ts("dve", ST[0:NT, 1:2], ST[0:NT, 0:1], 1.0 / D, EPS, ALU.mult, ALU.add, [ST], [ST])
        k.act(ST[0:NT, 2:3], ST[0:NT, 1:2], AF.Sqrt, [ST], [ST])
        P.op("dve", lambda e, ST=ST, NT=NT: e.reciprocal(out=ST[0:NT, 3:4], in_=ST[0:NT, 2:3]), [ST], [ST])
        k.ts("dve", XN[0:NT], X[0:NT], ST[0:NT, 3:4], None, ALU.mult, None, [X, ST], [XN])
        for q4 in range(4):
            pb = k.ps()
            pbv = pb[:].bitcast(BF16)
            for u in range(4):
                kc = q4 * 4 + u
                k.tr(pbv[:, u * 128:u * 128 + NT], XN[0:NT, kc * 128:(kc + 1) * 128], ident_b[0:NT, 0:NT], [XN, ident_b], [pb])
            for u in range(4):
                kc = q4 * 4 + u
                if t < 8:
                    k.act(hTa[:, kc, tok0:tok0 + 128], pbv[:, u * 128:(u + 1) * 128], AF.Identity, [pb, g1T, shT], [hTa],
                          scale=g1T[:, kc, 0:1], bias=shT[:, kc, 0:1])
                else:
                    for b in range(4):
                        k.act(hTa[:, kc, 1024 + 8 * b:1032 + 8 * b], pbv[:, u * 128 + 8 * b:u * 128 + 8 * b + 8], AF.Identity,
                              [pb, g1T, shT], [hTa], scale=g1T[:, kc, 1 + b:2 + b], bias=shT[:, kc, 1 + b:2 + b])
    wn = [0]

    def loadW(c0, n):
        W_ = Wt[wn[0] % 2]
        wn[0] += 1
        k.dma("sp", W_[:, :, 0:n], Wb.ap[:, :, c0:c0 + n], [Wb], [W_])
        return W_

    def proj_nat(c0, n, tiles, consumer):
        W_ = loadW(c0, n)
        for (t, NT, tok0) in tiles:
            pb = k.ps()
            for kc in range(KC):
                k.mm(pb[0:NT, 0:n], hTa[:, kc, tok0:tok0 + NT], W_[:, kc, 0:n], kc == 0, kc == KC - 1, [hTa, W_], [pb])
            consumer(t, NT, tok0, pb)

    CH = [(0, 512), (512, 512), (1024, 32)]

    def proj_T(W_, cc0, n, consumer):
        for (t0, nt) in CH:
            pb = k.ps()
            for kc in range(KC):
                k.mm(pb[0:n, 0:nt], W_[:, kc, cc0:cc0 + n], hTa[:, kc, t0:t0 + nt], kc == 0, kc == KC - 1, [hTa, W_], [pb])
            consumer(t0, nt, pb)

    rn = [0]

    def q_cons(a):
        def f(t, NT, tok0, pb):
            RT, QR = rtq[rn[0] % 2], qrot[rn[0] % 2]
            rn[0] += 1
            cosb = rq[0:NT, t, 0:32].unsqueeze(1).broadcast_to([NT, 8, 32])
            sinb = rq[0:NT, t, 32:64].unsqueeze(1).broadcast_to([NT, 8, 32])
            xv = pb[0:NT, 0:512].rearrange("p (h t e) -> p h t e", h=8, t=2, e=32)
            x1, x2 = xv[:, :, 0, :], xv[:, :, 1, :]
            k.tt("dve", RT[0:NT, 0], x1, cosb, ALU.mult, [pb, rq], [RT])
            k.tt("dve", RT[0:NT, 1], x2, sinb, ALU.mult, [pb, rq], [RT])
            k.tt("dve", RT[0:NT, 2], x2, cosb, ALU.mult, [pb, rq], [RT])
            k.tt("dve", RT[0:NT, 3], x1, sinb, ALU.mult, [pb, rq], [RT])
            qrv = QR[0:NT].rearrange("p g b (t e) -> p b g t e", t=2, e=32)
            rv = [RT[0:NT, i].rearrange("p (b g) e -> p b g e", b=2) for i in range(4)]
            k.tt("pool", qrv[:, :, :, 0, :], rv[0], rv[1], ALU.subtract, [RT], [QR])
            k.tt("pool", qrv[:, :, :, 1, :], rv[2], rv[3], ALU.add, [RT], [QR])
            pq_ = k.ps()
            pqv = pq_[:].bitcast(BF16)
            for g in range(4):
                k.tr(pqv[:, g * 128:g * 128 + NT], QR[0:NT, g].rearrange("p b d -> p (b d)"), ident_b[0:NT, 0:NT], [QR, ident_b], [pq_])
            k.cp("act", QT[:, a, :, tok0:tok0 + NT], pqv[:, 0:512].rearrange("p (g t) -> p g t", g=4)[:, :, 0:NT], [pq_], [QT])
        return f

    for a in range(2):
        proj_nat(C_QN + a * 512, 512, TILES, q_cons(a))
    proj_nat(C_GN, 48, TILES, lambda t, NT, tok0, pb: k.act(gsig[0:NT, t, :], pb[0:NT, 0:48], AF.Sigmoid, [pb], [gsig]))
    W_ = loadW(C_AG, 16)
    proj_T(W_, 0, 16, lambda t0, nt, pb: k.cp("dve", agTa[0:16, t0:t0 + nt], pb[0:16, 0:nt], [pb], [agTa]))
    W_ = loadW(C_QG, 512)
    for h in range(4):
        proj_T(W_, h * 128, 128, lambda t0, nt, pb, h=h: k.act(qgT[:, h, t0:t0 + nt], pb[:, 0:nt], AF.Identity, [pb], [qgT], scale=128.0 ** -0.5))
    W_ = loadW(C_KG, 512)
    for h in range(4):
        proj_T(W_, h * 128, 128, lambda t0, nt, pb, h=h: k.cp("act", kgT[:, h, t0:t0 + nt], pb[:, 0:nt], [pb], [kgT]))
    STILE = [TILES[8]]
    proj_nat(C_KG, 512, STILE, lambda t, NT, tok0, pb: k.cp("act", kgN[:], pb[0:32, :], [pb], [kgN]))
    for g2 in range(2):
        proj_nat(C_VG + g2 * 512, 512, TILES,
                 lambda t, NT, tok0, pb, g2=g2: k.cp("act" if t % 2 == 0 else "dve", vg[0:NT, t, g2 * 512:(g2 + 1) * 512], pb[0:NT, :], [pb], [vg]))

    def kv_cons(br):
        def f(t, NT, tok0, pb):
            RT = rtq[0]
            KS = kss[br]
            cosb = rs[:, 0:32].unsqueeze(1).broadcast_to([32, 4, 32])
            sinb = rs[:, 32:64].unsqueeze(1).broadcast_to([32, 4, 32])
            xv = pb[0:32, 0:256].rearrange("p (h t e) -> p h t e", h=4, t=2, e=32)
            ov = KS[:, 0:256].rearrange("p (h t e) -> p h t e", h=4, t=2, e=32)
            x1, x2 = xv[:, :, 0, :], xv[:, :, 1, :]
            k.tt("dve", RT[0:32, 0, 0:4], x1, cosb, ALU.mult, [pb, rs], [RT])
            k.tt("dve", RT[0:32, 1, 0:4], x2, sinb, ALU.mult, [pb, rs], [RT])
            k.tt("dve", RT[0:32, 2, 0:4], x2, cosb, ALU.mult, [pb, rs], [RT])
            k.tt("dve", RT[0:32, 3, 0:4], x1, sinb, ALU.mult, [pb, rs], [RT])
            k.tt("pool", ov[:, :, 0, :], RT[0:32, 0, 0:4], RT[0:32, 1, 0:4], ALU.subtract, [RT], [KS])
            k.tt("pool", ov[:, :, 1, :], RT[0:32, 2, 0:4], RT[0:32, 3, 0:4], ALU.add, [RT], [KS])
            k.cp("act", KS[:, 256:512], pb[0:32, 256:512], [pb], [KS])
            if br == 0:
                k.dma("sp", kv_cmp_s[:, :], KS[:], [KS], [])
            elif br == 1:
                k.dma("sp", kv_sel_s[:, :], KS[:], [KS], [])
            else:
                for b in range(4):
                    k.dma("sp", kv_win_s[b, 504:512, :], KS[8 * b:8 * b + 8, :], [KS], [])
            if br >= 1:
                j = br - 1
                C_ = c16s[br]
                k.cp("dve", C_[:], KS[:], [KS], [C_])
                k.cp("pool", Van[:, j, :, 0:64], C_[:, 256:512].rearrange("p (h e) -> p h e", h=4), [C_], [Van])
                pv = k.ps()
                pvv = pv[:].bitcast(BF16)
                for a in range(2):
                    k.tr(pvv[:, a * 32:(a + 1) * 32], C_[0:32, a * 128:(a + 1) * 128], ident_b[0:32, 0:32], [C_, ident_b], [pv])
                k.cp("act", KTn[:, j, :, :], pvv[:, 0:64].rearrange("p (a t) -> p a t", a=2), [pv], [KTn])
        return f

    for br in range(3):
        proj_nat(C_KV + br * 512, 512, STILE, kv_cons(br))
    ddb = P.buf("dram2dram")
    for b in range(4):
        k.dma("sp", kv_win_s[b, 0:504, :], cache_win[b, 8:512, :], [ddb], [])
    k.dma("sp", hTs.ap[:, :, :], hTa[:], [hTa], [hTs])
    P.barrier()
    A.reset(mB)
    phase_attn(k, env, locals())


def phase_attn(k, env, L_):
    P, A, dbg = k.P, k.A, k.dbg
    ident_f, ident_b, ones_f, Irep = (env[n] for n in ("ident_f", "ident_b", "ones_f", "Irep"))
    KTsel, KTwin, Vsel, Vwin, SS = (env[n] for n in ("KTsel", "KTwin", "Vsel", "Vwin", "SS"))
    cst, idx, wa2b, iota, qloc = (L_[n] for n in ("cst", "idx", "wa2b", "iota", "qloc"))
    QT, qgT, kgT, vg, agTa, gsig, KTn, Van, kgN = (L_[n] for n in ("QT", "qgT", "kgT", "vg", "agTa", "gsig", "KTn", "Van", "kgN"))
    KcS, VcS, cache_sel, cache_win, state_gla, gla_s, gnw_d = (L_[n] for n in ("KcS", "VcS", "cache_sel", "cache_win", "state_gla", "gla_s", "gnw_d"))
    psb = k.psb
    onsa_s = k.dscr("onsa_s", [9, 128, 1024], BF16)
    ogla_s = k.dscr("ogla_s", [9, 128, 1024], BF16)
    L_["onsa_s"], L_["ogla_s"] = onsa_s, ogla_s
    mX = A.mark()
    on1 = [A.alloc([128, 1024], BF16, f"on1_{i}") for i in range(1)]
    og1 = [A.alloc([128, 1024], BF16, f"og1_{i}") for i in range(1)]
    KTc = [A.alloc([128, 2, 512], BF16, f"KTc{i}") for i in range(2)]
    Vc = [A.alloc([128, 4, 260], BF16, f"Vc{i}") for i in range(2)]
    PT = [A.alloc([128, 512], BF16, f"PT{i}") for i in range(6)]
    accS = A.alloc([128, 16, 65], F32, "accS")
    accC = A.alloc([128, 4, 193], F32, "accC")
    accM = A.alloc([128, 4, 65], F32, "accM")
    accT = A.alloc([65, 4, 512], F32, "accT")
    KcP = A.alloc([128, 2, 512], BF16, "KcP")
    VcP = A.alloc([128, 4, 4, 193], BF16, "VcP")
    KcB = [KcP]
    VcB = [VcP]
    cmpb = [A.alloc([128, 512], BF16, f"cmpb{i}") for i in range(1)]
    base511 = A.alloc([32, 512], BF16, "base511")
    causalS = A.alloc([128, 1024], BF16, "causalS")
    winb = A.alloc([128, 1536], BF16, "winb")
    wbase = A.alloc([32, 512], BF16, "wbase")
    wbs = [A.alloc([32, 512], BF16, f"wbs{i}") for i in range(1)]
    newb16 = A.alloc([32, 32], BF16, "newb16")
    NS = [A.alloc([128, 4, 128], BF16, f"NS{i}") for i in range(2)]
    biasx = [A.alloc([128, 128], BF16, f"biasx{i}") for i in range(8)]
    qpj = A.alloc([128, 2], F32, "qpj")
    dq = A.alloc([128, 128], F32, "dq")
    f1 = A.alloc([128, 128], F32, "f1")
    f2 = A.alloc([128, 128], F32, "f2")
    forcedB = A.alloc([128, 128], F32, "forcedB")
    futN = A.alloc([128, 128], F32, "futN")
    futS = A.alloc([128, 128], F32, "futS")
    imp = A.alloc([128, 128], F32, "imp")
    impm = A.alloc([128, 128], F32, "impm")
    wk = A.alloc([128, 128], F32, "wk")
    a01 = A.alloc([128, 128], F32, "a01")
    m8 = A.alloc([128, 16], F32, "m8")
    rdc = A.alloc([128, 8], F32, "rdc")
    OA = [A.alloc([128, 16, 64], F32, f"OA{i}") for i in range(1)]
    tmpo = A.alloc([128, 16, 64], F32, "tmpo")
    rd = A.alloc([128, 16], F32, "rd")
    sc = A.alloc([128, 16], F32, "sc")
    gbs = [A.alloc([128, 512], F32, f"gbs{i}") for i in range(8)]
    KTt = [A.alloc([128, 2, 128], BF16, f"KTt{i}") for i in range(8)]
    Vt = [A.alloc([128, 4, 65], BF16, f"Vt{i}") for i in range(8)]
    Lg = A.alloc([128, 512], F32, "Lg")
    exg = Lg
    eq = A.alloc([128, 4, 128], F32, "eq")
    ek = A.alloc([128, 4, 128], F32, "ek")
    QdT = A.alloc([128, 4, 128], BF16, "QdT")
    KdT = A.alloc([128, 4, 128], BF16, "KdT")
    QdTm = A.alloc([128, 4, 4, 32], BF16, "QdTm")
    attm = A.alloc([128, 4, 128], BF16, "attm")
    S0f = [A.alloc([128, 1024], F32, f"S0f{i}") for i in range(1)]
    S0b = [A.alloc([128, 1024], BF16, f"S0b{i}") for i in range(2)]
    gnw = A.alloc([128, 256], F32, "gnw")
    gst = A.alloc([128, 16], F32, "gst")
    gjunk = A.alloc([128, 256], BF16, "gjunk")
    Eg = TT(eq.ap.rearrange("p h t -> p (h t)"), eq.b)
    Kd2g = A.alloc([32, 512], BF16, "Kd2g")
    Kd2m = [A.alloc([32, 512], BF16, f"Kd2m{i}") for i in range(2)]
    decg = A.alloc([128, 4, 4], F32, "decg")

    k.dma("sp", gnw[:], gnw_d.partition_broadcast(128), [], [gnw])
    k.dma("sp", KcP[:].rearrange("p a c -> p (a c)"), KcS.ap[0], [KcS], [KcP])
    k.dma("sp", VcP[:].rearrange("p c h w -> p (c h w)"), VcS.ap[0], [VcS], [VcP])
    for i in range(8):
        k.memset("pool", Vt[i][:], 1.0, [Vt[i]])
    k.ts("dve", causalS[:], iota[:, 0:1024], qloc, NEGB, ALU.is_gt, ALU.mult, [cst], [causalS])
    k.ts("dve", qpj[:, 1:2], qloc, 512.0, None, ALU.add, None, [cst], [qpj])
    k.ts("dve", winb[:], iota, qpj[:, 1:2], None, ALU.is_gt, None, [cst, qpj], [winb])
    k.stt(winb[:], iota, qloc, winb[:], ALU.is_le, ALU.add, [cst, winb], [winb])
    k.ts("dve", winb[:], winb[:], NEGB, None, ALU.mult, None, [winb], [winb])
    k.memset("pool", base511[:], 0.0, [base511])
    k.memset("pool", base511[:, 511:512], NEGB, [base511])
    qps = cst[0:32, O_QPS:O_QPS + 1]
    k.ts("dve", rd[0:32, 0:1], qps, -8192.0, None, ALU.add, None, [cst], [rd])
    k.ts("dve", wbase[:], iota[0:32, 0:512], rd[0:32, 0:1], NEGB, ALU.is_le, ALU.mult, [cst, rd], [wbase])
    k.cp("dve", newb16[:], cst[0:32, O_NEWB:O_NEWB + 32], [cst], [newb16])
    nb30 = cst[0:32, O_NB30:O_NB30 + 4]

    srot = [0]

    SB = (0, 1, 2, 5, 6, 7)

    def ps_s():
        b = psb[SB[srot[0] % 6]]
        srot[0] += 1
        return b

    pn = [0]
    bxn = [0]
    pvn = [0]
    SLOTS = [(j, 128, 128 * j) for j in range(8)] + [(8, 32, 1024)]
    only = dbg.get("slots")
    for (sl, NQ, tok0) in SLOTS:
        if only is not None and sl not in only:
            continue
        prompt = sl < 8
        OA_ = OA[0]
        NS_ = NS[sl % 2]

        def attend_chunk(h, tiles, parts, acc_ap, accbuf, first, pvm=None, tpv=False):
            a, b_ = h // 2, h % 2
            pts = []
            for (KT_ap, Kbuf, Vfn, Vbuf, bias_ap, Bbuf, nk) in tiles:
                ps_ = ps_s()
                out3 = ps_[0:nk, 0:4 * NQ].rearrange("p (g q) -> p g q", g=4)
                k.mm(out3, bias_ap, Irep[0:NQ, :, 0:NQ], True, False, [Bbuf, Irep], [ps_])
                k.mm(out3, KT_ap, QT[64 * b_:64 * b_ + 64, a, :, tok0:tok0 + NQ], False, True, [Kbuf, QT], [ps_])
                PT_ = PT[pn[0] % len(PT)]
                pn[0] += 1
                k.act(PT_[0:nk, 0:4 * NQ], ps_[0:nk, 0:4 * NQ], AF.Exp, [ps_], [PT_])
                pts.append(PT_)
            if tpv:
                pvT = ps_s()
                for i, (KT_ap, Kbuf, Vfn, Vbuf, bias_ap, Bbuf, nk) in enumerate(tiles):
                    k.mm(pvT[0:65, 0:4 * NQ], Vfn(0, 65), pts[i][0:nk, 0:4 * NQ], i == 0, i == len(tiles) - 1, [Vbuf, pts[i]], [pvT])
                dstT = accT[0:65, h, :]
                if first:
                    k.cp("dve", dstT, pvT[0:65, 0:512], [pvT], [accT])
                else:
                    k.tt("dve", dstT, pvT[0:65, 0:512], dstT, ALU.add, [pvT, accT], [accT])
                return
            if pvm is not None:
                for i, (KT_ap, Kbuf, Vfn, Vbuf, bias_ap, Bbuf, nk) in enumerate(tiles):
                    k.mm(pvm[0:4 * NQ, h * 65:(h + 1) * 65], pts[i][0:nk, 0:4 * NQ], Vfn(0, 65),
                         i == 0, i == len(tiles) - 1, [pts[i], Vbuf], [pvm])
                return
            for pi, (c0, c1) in enumerate(parts):
                W = c1 - c0
                pv = ps_s()
                for g in range(4):
                    for i, (KT_ap, Kbuf, Vfn, Vbuf, bias_ap, Bbuf, nk) in enumerate(tiles):
                        k.mm(pv[0:NQ, g * W:(g + 1) * W], pts[i][0:nk, g * NQ:(g + 1) * NQ], Vfn(c0, c1),
                             i == 0, i == len(tiles) - 1, [pts[i], Vbuf], [pv])
                dst = acc_ap(pi)
                src = pv[0:NQ, 0:4 * W].rearrange("p (g w) -> p g w", g=4)
                if first:
                    k.cp("dve", dst, src, [pv], [accbuf])
                else:
                    k.tt("dve", dst, src, dst, ALU.add, [pv, accbuf], [accbuf])

        P.phase = f'cmp{sl}'
        if prompt:
            k.ts("dve", qpj[:, 0:1], qloc, 1024.0 * sl, None, ALU.add, None, [cst], [qpj])
            qp = qpj[0:NQ, 0:1]
            qpb = [cst, qpj]
        else:
            qp = qps
            qpb = [cst]
        k.ts("dve", dq[0:NQ], iota[0:NQ, 0:128], -64.0, qp, ALU.mult, ALU.add, qpb, [dq])
        k.ts("dve", f1[0:NQ], dq[0:NQ], 0.0, None, ALU.is_ge, None, [dq], [f1])
        k.ts("dve", f2[0:NQ], dq[0:NQ], 128.0, BIG, ALU.is_lt, ALU.mult, [dq], [f2])
        k.tt("dve", forcedB[0:NQ], f1[0:NQ], f2[0:NQ], ALU.mult, [f1, f2], [forcedB])
        k.memset("pool", forcedB[0:NQ, 0:1], BIG, [forcedB])
        k.ts("dve", futN[0:NQ], f1[0:NQ], 2.0 * BIG, -BIG, ALU.mult, ALU.add, [f1], [futN])
        k.ts("dve", futS[0:NQ], f1[0:NQ], -1.0, -NEGB, ALU.add, ALU.mult, [f1], [futS])
        gs3 = gsig[0:NQ, sl, :].rearrange("p (i r) -> p i r", r=3)

        if prompt:
            CB = cmpb[0]
            k.ts("dve", qpj[:, 1:2], qp, -31.0, 1.0 / 16.0, ALU.add, ALU.mult, qpb, [qpj])
            k.ts("dve", CB[0:NQ], iota[0:NQ, 0:512], qpj[0:NQ, 1:2], NEGB, ALU.is_gt, ALU.mult, [cst, qpj], [CB])
        cn = 0
        for h in range(4):
            a, b_ = h // 2, h % 2
            nsets = 1 if prompt else 4
            for si in range(nsets):
                if prompt:
                    Kc_, Vc_, CBs = KcP, VcP, CB
                else:
                    Kc_, Vc_, CBs = KcB[0], VcB[0], cmpb[0]
                    cn += 1
                    k.dma("sp", Kc_[:].rearrange("p a c -> p (a c)"), KcS.ap[1 + si], [KcS], [Kc_])
                    k.dma("sp", Vc_[:].rearrange("p c h w -> p (c h w)"), VcS.ap[1 + si], [VcS], [Vc_])
                    k.ts("dve", CBs[0:32], base511[:], nb30[:, si:si + 1], None, ALU.min, None, [base511, cst], [CBs])
                tiles = [(Kc_[64 * b_:64 * b_ + 64, a, ct * 128:(ct + 1) * 128], Kc_,
                          (lambda c0, c1, ct=ct, Vc_=Vc_, h=h: Vc_[:, ct, h, c0:c1]), Vc_,
                          CBs[0:NQ, ct * 128:(ct + 1) * 128], CBs, 128) for ct in range(4)]
                attend_chunk(h, tiles, [(0, 65), (65, 193)],
                             lambda pi: accC[0:NQ, :, 0:65] if pi == 0 else accC[0:NQ, :, 65:193], accC, si == 0)
            av = [accC[0:NQ, g, :] for g in range(4)]
            ab = [accC for g in range(4)]
            for g in range(4):
                k.ts("dve", rdc[0:NQ, g:g + 1], av[g][:, 64:65], TINY, None, ALU.max, None, [ab[g]], [rdc])
            P.op("dve", lambda e, NQ=NQ: e.reciprocal(out=rdc[0:NQ, 0:4], in_=rdc[0:NQ, 0:4]), [rdc], [rdc])
            k.ts("dve", imp[0:NQ], av[0][:, 65:193], rdc[0:NQ, 0:1], None, ALU.mult, None, [ab[0], rdc], [imp])
            for g in range(1, 4):
                k.stt(imp[0:NQ], av[g][:, 65:193], rdc[0:NQ, g:g + 1], imp[0:NQ], ALU.mult, ALU.add, [ab[g], rdc, imp], [imp])
            k.tt("dve", rdc[0:NQ, 4:8], rdc[0:NQ, 0:4], gs3[:, h * 4:h * 4 + 4, 0], ALU.mult, [rdc, gsig], [rdc])
            for g in range(4):
                k.ts("dve", OA_[0:NQ, h * 4 + g, :], av[g][:, 0:64], rdc[0:NQ, 4 + g:5 + g], None, ALU.mult, None, [ab[g], rdc], [OA_])
            k.tt("dve", impm[0:NQ], imp[0:NQ], forcedB[0:NQ], ALU.max, [imp, forcedB], [impm])
            k.tt("dve", impm[0:NQ], impm[0:NQ], futN[0:NQ], ALU.min, [impm, futN], [impm])
            P.op("dve", lambda e, NQ=NQ: e.max(out=m8[0:NQ, 0:8], in_=impm[0:NQ]), [impm], [m8])
            P.op("dve", lambda e, NQ=NQ: e.match_replace(out=wk[0:NQ], in_to_replace=m8[0:NQ, 0:8], in_values=impm[0:NQ], imm_value=-3.0 * BIG), [impm, m8], [wk])
            P.op("dve", lambda e, NQ=NQ: e.max(out=m8[0:NQ, 8:16], in_=wk[0:NQ]), [wk], [m8])
            tc_ = 15 if prompt else 14
            k.ts("dve", a01[0:NQ], impm[0:NQ], m8[0:NQ, tc_:tc_ + 1], None, ALU.is_lt, None, [impm, m8], [a01])
            k.stt(NS_[0:NQ, h, :], a01[0:NQ], NEGB, futS[0:NQ], ALU.mult, ALU.min, [a01, futS], [NS_])
            if sl == 0 and h == 3:
                k.dump("accC", accC, [128, 4, 193])
                k.dump("imp", imp, [128, 128])
                k.dump("impm", impm, [128, 128])
                k.dump("m8", m8, [128, 16])
                k.dump("rdc", rdc, [128, 8])
                k.dump("OAc", OA_, [128, 16, 64])

        def evac(br):
            v = accS[0:NQ]
            k.ts("dve", rd[0:NQ, 0:16], v[:, :, 64], TINY, None, ALU.max, None, [accS], [rd])
            P.op("dve", lambda e, NQ=NQ: e.reciprocal(out=rd[0:NQ, 0:16], in_=rd[0:NQ, 0:16]), [rd], [rd])
            k.tt("dve", sc[0:NQ, 0:16], rd[0:NQ, 0:16], gs3[:, :, br], ALU.mult, [rd, gsig], [sc])
            k.tt("dve", tmpo[0:NQ], v[:, :, 0:64], sc[0:NQ, 0:16].unsqueeze(2).broadcast_to([NQ, 16, 64]),
                 ALU.mult, [accS, sc], [tmpo])
            k.tt("pool", OA_[0:NQ], OA_[0:NQ], tmpo[0:NQ], ALU.add, [OA_, tmpo], [OA_])

        def accSf(h):
            return lambda pi: accS[0:NQ, h * 4:(h + 1) * 4, :]

        def accT_relayout():
            for h in range(4):
                pr = ps_s()
                for g in range(4):
                    k.tr(pr[:, g * 65:(g + 1) * 65], accT[0:65, h, g * 128:(g + 1) * 128], ident_f[0:65, 0:65], [accT, ident_f], [pr])
                k.cp("dve" if h % 2 == 0 else "act", accS[:, h * 4:(h + 1) * 4, :], pr[:, 0:260].rearrange("p (g w) -> p g w", g=4), [pr], [accS])

        def accM_add(pvm, first):
            src = pvm[:, 0:260].rearrange("p (h w) -> p h w", h=4)
            if first:
                k.cp("dve", accM[:], src, [pvm], [accM])
            else:
                k.tt("dve", accM[:], src, accM[:], ALU.add, [pvm, accM], [accM])

        def accM_relayout():
            for g in range(4):
                pr = ps_s()
                k.mm(pr[0:32, 0:260], ident_f[:, g * 32:(g + 1) * 32], accM[:].rearrange("p h w -> p (h w)"), True, True, [ident_f, accM], [pr])
                k.cp("dve", accS[0:32].rearrange("p (h g) w -> p h g w", g=4)[:, :, g, :],
                     pr[0:32, 0:260].rearrange("p (h w) -> p h w", h=4), [pr], [accS])

        def bias_from_ns(h, blk0, extra_ap, extra_bufs, scalar_ap=None):
            BX = biasx[bxn[0] % 8]
            bxn[0] += 1
            src = NS_[0:NQ, h, blk0:blk0 + 2].unsqueeze(2).broadcast_to([NQ, 2, 64])
            dst = BX[0:NQ].rearrange("p (r s) -> p r s", r=2)
            eng = "dve" if bxn[0] % 2 == 0 else "pool"
            if extra_ap is not None:
                k.tt("dve", dst, src, extra_ap.rearrange("p (r s) -> p r s", r=2), ALU.min, [NS_] + extra_bufs, [BX])
            elif scalar_ap is not None:
                k.ts("dve", dst, src, scalar_ap, None, ALU.min, None, [NS_, cst], [BX])
            else:
                k.cp(eng, dst, src, [NS_], [BX])
            return BX

        def kv_chunk(KTd, Vd, ck, n):
            KTc_, Vc_ = KTc[n % 2], Vc[n % 2]
            k.dma("sp", KTc_[:], KTd.ap[:, :, ck * 512:(ck + 1) * 512], [KTd], [KTc_])
            k.dma("sp", Vc_[:], Vd.ap[ck * 512:(ck + 1) * 512, :].rearrange("(t p) c -> p t c", p=128), [Vd], [Vc_])
            return KTc_, Vc_

        def cache_tile(G_, n):
            KTt_, Vt_ = KTt[n % 8], Vt[n % 8]
            k.cp("pool", Vt_[:, :, 0:64], G_[:, 256:512].rearrange("p (h e) -> p h e", h=4), [G_], [Vt_])
            pb = ps_s()
            for a in range(2):
                k.tr(pb[:, a * 128:(a + 1) * 128], G_[:, a * 128:(a + 1) * 128], ident_f[:], [G_, ident_f], [pb])
            k.cp("act", KTt_[:].rearrange("p a t -> p (a t)"), pb[:, 0:256], [pb], [KTt_])
            return KTt_, Vt_

        P.phase = f'sel{sl}'
        if prompt:
            nkt = 8 * (sl + 1)
            cnk = 0
            for ck in range(nkt // 4):
                KTc_, Vc_ = kv_chunk(KTsel, Vsel, ck, cnk)
                cnk += 1
                for h in range(4):
                    a, b_ = h // 2, h % 2
                    tiles = []
                    for u in range(4):
                        kt = ck * 4 + u
                        if kt >= 8 * sl:
                            BX = bias_from_ns(h, 2 * kt, causalS[0:NQ, (kt - 8 * sl) * 128:(kt - 8 * sl + 1) * 128], [causalS])
                        else:
                            BX = bias_from_ns(h, 2 * kt, None, [])
                        tiles.append((KTc_[64 * b_:64 * b_ + 64, a, u * 128:(u + 1) * 128], KTc_,
                                      (lambda c0, c1, u=u, Vc_=Vc_, h=h: Vc_[:, u, h * 65 + c0:h * 65 + c1]), Vc_,
                                      BX[0:NQ, :], BX, 128))
                    attend_chunk(h, tiles, [(0, 65)], accSf(h), accS, ck == 0, tpv=True)
            accT_relayout()
        else:
            gn = 0
            for b in range(dbg.get("nsb", 4)):
                for j4 in range(16):
                    cts = []
                    for u in range(4):
                        j = j4 * 4 + u
                        G_ = gbs[gn % len(gbs)]
                        col = b * 64 + j
                        P.dma("pool", lambda e, G_=G_, col=col: e.indirect_dma_start(
                            out=G_[:], out_offset=None, in_=cache_sel[:, :],
                            in_offset=bass.IndirectOffsetOnAxis(ap=idx[:, col:col + 1], axis=0)), [idx.b], [G_.b])
                        cts.append(cache_tile(G_, gn))
                        gn += 1
                    pvm = psb[3 + pvn[0] % 2]
                    pvn[0] += 1
                    for h in range(4):
                        a, b_ = h // 2, h % 2
                        tiles = []
                        for u in range(4):
                            j = j4 * 4 + u
                            KTt_, Vt_ = cts[u]
                            BX = bias_from_ns(h, 2 * j, None, [], scalar_ap=nb30[:, b:b + 1])
                            tiles.append((KTt_[64 * b_:64 * b_ + 64, a, :], KTt_,
                                          (lambda c0, c1, Vt_=Vt_, h=h: Vt_[:, h, c0:c1]), Vt_, BX[0:NQ, :], BX, 128))
                        attend_chunk(h, tiles, [(0, 65)], accSf(h), accS, b == 0 and j4 == 0, pvm=pvm)
                    accM_add(pvm, b == 0 and j4 == 0)
            pvm = psb[3 + pvn[0] % 2]
            pvn[0] += 1
            for h in range(4):
                a, b_ = h // 2, h % 2
                tiles = [(KTn[64 * b_:64 * b_ + 64, 0, a, :], KTn, (lambda c0, c1, h=h: Van[:, 0, h, c0:c1]), Van,
                          newb16[:, :], newb16, 32)]
                attend_chunk(h, tiles, [(0, 65)], accSf(h), accS, False, pvm=pvm)
            accM_add(pvm, False)
            accM_relayout()
        if sl == 0:
            k.dump("NS", NS_, [128, 4, 128], BF16)
            k.dump("accS_sel", accS, [128, 16, 65])
        evac(1)
        if sl == 0:
            k.dump("OAs", OA_, [128, 16, 64])

        P.phase = f'win{sl}'
        if prompt:
            kt0 = max(8 * sl - 4, 0)
            kt1 = 8 * sl + 8
            for ck in range(kt0 // 4, kt1 // 4):
                KTc_, Vc_ = kv_chunk(KTwin, Vwin, ck, cnk)
                cnk += 1
                for h in range(4):
                    a, b_ = h // 2, h % 2
                    tiles = []
                    for u in range(4):
                        kt = ck * 4 + u
                        uu = kt - (8 * sl - 4)
                        tiles.append((KTc_[64 * b_:64 * b_ + 64, a, u * 128:(u + 1) * 128], KTc_,
                                      (lambda c0, c1, u=u, Vc_=Vc_, h=h: Vc_[:, u, h * 65 + c0:h * 65 + c1]), Vc_,
                                      winb[0:NQ, uu * 128:(uu + 1) * 128], winb, 128))
                    attend_chunk(h, tiles, [(0, 65)], accSf(h), accS, ck == kt0 // 4, tpv=True)
            accT_relayout()
        else:
            for b in range(dbg.get("nsb", 4)):
                WB = wbs[0]
                k.ts("dve", WB[:], wbase[:], nb30[:, b:b + 1], None, ALU.min, None, [wbase, cst], [WB])
                cts = []
                for u in range(4):
                    G_ = gbs[gn % len(gbs)]
                    k.dma("sp", G_[:], cache_win[b, u * 128:(u + 1) * 128, :], [], [G_])
                    cts.append(cache_tile(G_, gn))
                    gn += 1
                pvm = psb[3 + pvn[0] % 2]
                pvn[0] += 1
                for h in range(4):
                    a, b_ = h // 2, h % 2
                    tiles = []
                    for u in range(4):
                        KTt_, Vt_ = cts[u]
                        tiles.append((KTt_[64 * b_:64 * b_ + 64, a, :], KTt_, (lambda c0, c1, Vt_=Vt_, h=h: Vt_[:, h, c0:c1]), Vt_,
                                      WB[:, u * 128:(u + 1) * 128], WB, 128))
                    attend_chunk(h, tiles, [(0, 65)], accSf(h), accS, b == 0, pvm=pvm)
                accM_add(pvm, b == 0)
            pvm = psb[3 + pvn[0] % 2]
            pvn[0] += 1
            for h in range(4):
                a, b_ = h // 2, h % 2
                tiles = [(KTn[64 * b_:64 * b_ + 64, 1, a, :], KTn, (lambda c0, c1, h=h: Van[:, 1, h, c0:c1]), Van,
                          newb16[:, :], newb16, 32)]
                attend_chunk(h, tiles, [(0, 65)], accSf(h), accS, False, pvm=pvm)
            accM_add(pvm, False)
            accM_relayout()
        if sl == 0:
            k.dump("accS_win", accS, [128, 16, 65])
        evac(2)
        ON = on1[0]
        OG = og1[0]
        k.cp("act", ON[0:NQ, :], OA_[0:NQ].rearrange("p i d -> p (i d)"), [OA_], [ON])
        k.dma("sp", onsa_s.ap[sl, 0:NQ, :], ON[0:NQ, :], [ON], [onsa_s])

        P.phase = f'gla{sl}'
        NT = NQ
        nbt = 1 if prompt else 4
        o_uge = O_UGE if prompt else O_UGE32
        o_mtri = O_MTRI if prompt else O_MTRI32
        pp = ps_s()
        k.mm(pp[0:NT, :], agTa[:, tok0:tok0 + NT], wa2b[:, :], True, True, [agTa, wa2b], [pp])
        k.act(exg[0:NT], pp[0:NT, :], AF.Exp, [pp], [exg], scale=-1.0)
        k.act(Lg[0:NT], exg[0:NT], AF.Ln, [exg, ones_f], [Lg], bias=ones_f[0:NT, 0:1], scale=1.0)
        pc_ = ps_s()
        for h in range(4):
            k.mm(pc_[:, h * NT:(h + 1) * NT], Lg[0:NT, h * 128:(h + 1) * 128], cst[0:NT, o_uge:o_uge + NT], True, True, [Lg, cst], [pc_])
        eqv = eq[:].rearrange("p h t -> p (h t)")[:, 0:4 * NT]
        ekv = ek[:].rearrange("p h t -> p (h t)")[:, 0:4 * NT]
        k.act(eqv, pc_[:, 0:4 * NT], AF.Exp, [pc_], [eq])
        k.act(ekv, pc_[:, 0:4 * NT], AF.Exp, [pc_], [ek], scale=-1.0)
        qdv = QdT[:].rearrange("p h t -> p (h t)")[:, 0:4 * NT].rearrange("p (h t) -> p h t", h=4)
        kdv = KdT[:].rearrange("p h t -> p (h t)")[:, 0:4 * NT].rearrange("p (h t) -> p h t", h=4)
        k.tt("dve", qdv, qgT[:, :, tok0:tok0 + NT], eqv.rearrange("p (h t) -> p h t", h=4), ALU.mult, [qgT, eq], [QdT])
        k.tt("dve", kdv, kgT[:, :, tok0:tok0 + NT], ekv.rearrange("p (h t) -> p h t", h=4), ALU.mult, [kgT, ek], [KdT])
        pa_ = ps_s()
        for h in range(4):
            k.mm(pa_[0:NT, h * NT:(h + 1) * NT], kdv[:, h, :], qdv[:, h, :], True, True, [KdT, QdT], [pa_])
        amv = attm[0:NT].rearrange("p h t -> p (h t)")[:, 0:4 * NT].rearrange("p (h t) -> p h t", h=4)
        k.tt("dve", amv, pa_[0:NT, 0:4 * NT].rearrange("p (h t) -> p h t", h=4),
             cst[0:NT, o_mtri:o_mtri + NT].unsqueeze(1).broadcast_to([NT, 4, NT]), ALU.mult, [pa_, cst], [attm])
        if prompt:
            S0f_, S0b_ = S0f[0], S0b[sl % 2]
            k.dma("sp", S0f_[:], SS.ap[sl], [SS], [S0f_])
            k.cp("pool", S0b_[:], S0f_[:], [S0f_], [S0b_])
            states = [(S0f_, S0b_)]
        else:
            indrow = cst[:, O_INDROW:O_INDROW + 128].rearrange("p (b t) -> p b t", b=4)
            k.tt("dve", QdTm[:], qdv.unsqueeze(1).broadcast_to([128, 4, 4, 32]), indrow.unsqueeze(2).broadcast_to([128, 4, 4, 32]),
                 ALU.mult, [QdT, cst], [QdTm])
            states = []
        po = [psb[3], psb[4]]
        sn = 0
        for h in range(4):
            oap = po[h // 2][0:NT, (h % 2) * 256:(h % 2) * 256 + 256]
            k.mm(oap, amv[:, h, :], vg[0:NT, sl, h * 256:(h + 1) * 256], True, False, [attm, vg], [po[h // 2]])
            if prompt:
                k.mm(oap, qdv[:, h, :], S0b_[:, h * 256:(h + 1) * 256], False, True, [QdT, S0b_], [po[h // 2]])
            else:
                for b in range(4):
                    Sf, Sb = S0f[0], S0b[sn % 2]
                    sn += 1
                    k.dma("sp", Sf[:, h * 256:(h + 1) * 256], state_gla[b, h], [], [Sf])
                    k.cp("pool", Sb[:, h * 256:(h + 1) * 256], Sf[:, h * 256:(h + 1) * 256], [Sf], [Sb])
                    k.mm(oap, QdTm[:, b, h, :], Sb[:, h * 256:(h + 1) * 256], False, b == 3, [QdTm, Sb], [po[h // 2]])
        for h in range(4):
            oap = po[h // 2][0:NT, (h % 2) * 256:(h % 2) * 256 + 256]
            k.act(gjunk[0:NT], oap, AF.Square, [po[h // 2]], [gjunk, gst], accum_out=gst[0:NT, h:h + 1])
        k.ts("dve", gst[0:NT, 4:8], gst[0:NT, 0:4], 1.0 / 256.0, EPS, ALU.mult, ALU.add, [gst], [gst])
        k.act(gst[0:NT, 8:12], gst[0:NT, 4:8], AF.Sqrt, [gst], [gst])
        P.op("dve", lambda e, NT=NT: e.reciprocal(out=gst[0:NT, 12:16], in_=gst[0:NT, 8:12]), [gst], [gst])
        for h in range(4):
            oap = po[h // 2][0:NT, (h % 2) * 256:(h % 2) * 256 + 256]
            k.stt(OG[0:NT, h * 256:(h + 1) * 256], oap, gst[0:NT, 12 + h:13 + h], gnw[0:NT, :], ALU.mult, ALU.mult,
                  [po[h // 2], gst, gnw], [OG])
        k.dma("sp", ogla_s.ap[sl, 0:NT, :], OG[0:NT, :], [OG], [ogla_s])
        if not prompt:
            pd = ps_s()
            k.mm(pd[0:32, :], cst[0:32, O_UGT32:O_UGT32 + 32], Lg[0:32, :], True, True, [cst, Lg], [pd])
            k.act(Eg[0:32], pd[0:32, :], AF.Exp, [pd], [Eg])
            k.tt("dve", Kd2g[:], kgN[:], Eg[0:32], ALU.mult, [kgN, Eg], [Kd2g])
            pl = ps_s()
            for h in range(4):
                k.mm(pl[:, h * 4:(h + 1) * 4], Lg[0:32, h * 128:(h + 1) * 128], cst[0:32, O_NEGIND:O_NEGIND + 4], True, True, [Lg, cst], [pl])
            k.act(decg[:].rearrange("p h b -> p (h b)"), pl[:, 0:16], AF.Exp, [pl], [decg])
            for b in range(4):
                Km = Kd2m[b % 2]
                k.ts("dve", Km[:], Kd2g[:], cst[0:32, O_IND + b:O_IND + b + 1], None, ALU.mult, None, [Kd2g, cst], [Km])
                Sf = S0f[0]
                Sn = Sf
                k.dma("sp", Sf[:].rearrange("p (h v) -> p h v", h=4), state_gla[b].rearrange("h k v -> k h v"), [], [Sf])
                for hp in range(2):
                    pb = ps_s()
                    for u in range(2):
                        h = hp * 2 + u
                        k.mm(pb[:, u * 256:(u + 1) * 256], Km[:, h * 128:(h + 1) * 128], vg[0:32, 8, h * 256:(h + 1) * 256], True, True, [Km, vg], [pb])
                    for u in range(2):
                        h = hp * 2 + u
                        k.stt(Sn[:, h * 256:(h + 1) * 256], Sf[:, h * 256:(h + 1) * 256], decg[:, h, b:b + 1], pb[:, u * 256:(u + 1) * 256],
                              ALU.mult, ALU.add, [Sf, decg, pb], [Sn])
                k.dma("sp", gla_s[b], Sn[:], [Sn], [])
    if dbg.get("dump_o"):
        o1 = k.dout("dbg_onsa", [9, 128, 1024], BF16)
        o2 = k.dout("dbg_ogla", [9, 128, 1024], BF16)
        k.dma("sp", o1[:, :, :], onsa_s.ap[:, :, :], [onsa_s], [])
        k.dma("sp", o2[:, :, :], ogla_s.ap[:, :, :], [ogla_s], [])
    P.barrier()
    phase_late(k, env, L_, mX)


def phase_late(k, env, L_, mX):
    P, A, dbg = k.P, k.A, k.dbg
    ident_b, Wb, gsc, Won, Wog, Wout, xl, xs = (env[n] for n in ("ident_b", "Wb", "gsc", "Won", "Wog", "Wout", "xl", "xs"))
    regC, mB, hTs, onsa_s, ogla_s, fnw_d, y_l, y_s = (L_[n] for n in ("regC", "mB", "hTs", "onsa_s", "ogla_s", "fnw_d", "y_l", "y_s"))
    TILES = [(t, 128, 128 * t) for t in range(8)] + [(8, 32, 1024)]
    P.phase = 'late_a'
    regC.reset()
    onT = regC.alloc([128, 8, 1056], BF16, "onT")
    ogT = regC.alloc([128, 8, 1056], BF16, "ogT")
    A.reset(mB)
    mT = A.alloc([128, KC, 1056], BF16, "mT")
    mMT = A.mark()
    hTa = A.alloc([128, KC, 1056], BF16, "hTa2")
    k.dma("sp", hTa[:], hTs.ap[:, :, :], [hTs], [hTa])
    ma = A.mark()
    Wt2 = [A.alloc([128, KC, 512], BF16, f"Wt2_{i}") for i in range(2)]
    o1 = [A.alloc([128, 512], BF16, f"o1_{i}") for i in range(2)]
    zs = [A.alloc([128, 512], F32, f"zs{i}") for i in range(2)]
    wn = [0]
    zn = [0]
    for (scr, c0, dstT) in ((onsa_s, C_ZN, onT), (ogla_s, C_ZG, ogT)):
        for g2 in range(2):
            W_ = Wt2[wn[0] % 2]
            wn[0] += 1
            k.dma("sp", W_[:], Wb.ap[:, :, c0 + g2 * 512:c0 + (g2 + 1) * 512], [Wb], [W_])
            for (t, NT, tok0) in TILES:
                pb = k.ps()
                for kc in range(KC):
                    k.mm(pb[0:NT, :], hTa[:, kc, tok0:tok0 + NT], W_[:, kc, :], kc == 0, kc == KC - 1, [hTa, W_], [pb])
                Z_, O_ = zs[zn[0] % 2], o1[zn[0] % 2]
                zn[0] += 1
                k.act(Z_[0:NT], pb[0:NT, :], AF.Silu, [pb], [Z_])
                k.dma("sp", O_[0:NT], scr.ap[t, 0:NT, g2 * 512:(g2 + 1) * 512], [scr], [O_])
                k.tt("dve", O_[0:NT], O_[0:NT], Z_[0:NT], ALU.mult, [O_, Z_], [O_])
                pt = k.ps()
                ptv = pt[:].bitcast(BF16)
                for u in range(4):
                    k.tr(ptv[:, u * 128:u * 128 + NT], O_[0:NT, u * 128:(u + 1) * 128], ident_b[0:NT, 0:NT], [O_, ident_b], [pt])
                k.cp("act", dstT[:, g2 * 4:(g2 + 1) * 4, tok0:tok0 + NT],
                     ptv[:, 0:512].rearrange("p (u t) -> p u t", u=4)[:, :, 0:NT], [pt], [dstT])
    P.barrier()
    P.phase = 'late_b'
    A.reset(ma)
    Wsets = []
    for i, ar in enumerate((A, regC)):
        Wsets.append((ar.alloc([128, KC, 256], BF16, f"Wmn{i}"), ar.alloc([128, KC, 256], BF16, f"Wmg{i}"),
                      ar.alloc([128, 8, 256], BF16, f"Won_t{i}"), ar.alloc([128, 8, 256], BF16, f"Wog_t{i}")))
    s1 = [A.alloc([128, 512], F32, f"s1_{i}") for i in range(2)]
    t1 = [A.alloc([128, 512], F32, f"t1_{i}") for i in range(2)]
    m16 = [A.alloc([128, 256], BF16, f"m16_{i}") for i in range(2)]
    mn = 0
    for cg in range(8):
        Wmn, Wmg, Won_t, Wog_t = Wsets[cg % 2]
        k.dma("sp", Wmn[:], Wb.ap[:, :, C_MN + cg * 256:C_MN + (cg + 1) * 256], [Wb], [Wmn])
        k.dma("sp", Wmg[:], Wb.ap[:, :, C_MG + cg * 256:C_MG + (cg + 1) * 256], [Wb], [Wmg])
        k.dma("sp", Won_t[:], Won.ap[:, :, cg * 256:(cg + 1) * 256], [Won], [Won_t])
        k.dma("sp", Wog_t[:], Wog.ap[:, :, cg * 256:(cg + 1) * 256], [Wog], [Wog_t])
        for (t, NT, tok0) in TILES:
            pg = k.ps()
            px = k.ps()
            for kc in range(KC):
                k.mm(pg[0:NT, 0:256], hTa[:, kc, tok0:tok0 + NT], Wmn[:, kc, :], kc == 0, kc == KC - 1, [hTa, Wmn], [pg])
            for kc in range(KC):
                k.mm(pg[0:NT, 256:512], hTa[:, kc, tok0:tok0 + NT], Wmg[:, kc, :], kc == 0, kc == KC - 1, [hTa, Wmg], [pg])
            for kc in range(8):
                k.mm(px[0:NT, 0:256], onT[:, kc, tok0:tok0 + NT], Won_t[:, kc, :], kc == 0, kc == 7, [onT, Won_t], [px])
            for kc in range(8):
                k.mm(px[0:NT, 256:512], ogT[:, kc, tok0:tok0 + NT], Wog_t[:, kc, :], kc == 0, kc == 7, [ogT, Wog_t], [px])
            S_, T_, M_ = s1[mn % 2], t1[mn % 2], m16[mn % 2]
            mn += 1
            k.act(S_[0:NT], pg[0:NT, :], AF.Sigmoid, [pg], [S_])
            k.tt("dve", T_[0:NT], px[0:NT, :], S_[0:NT], ALU.mult, [px, S_], [T_])
            k.tt("pool", M_[0:NT], T_[0:NT, 0:256], T_[0:NT, 256:512], ALU.add, [T_], [M_])
            pt = k.ps()
            ptv = pt[:].bitcast(BF16)
            for u in range(2):
                k.tr(ptv[:, u * 128:u * 128 + NT], M_[0:NT, u * 128:(u + 1) * 128], ident_b[0:NT, 0:NT], [M_, ident_b], [pt])
            k.cp("act", mT[:, cg * 2:cg * 2 + 2, tok0:tok0 + NT],
                 ptv[:, 0:256].rearrange("p (u t) -> p u t", u=2)[:, :, 0:NT], [pt], [mT])
    P.barrier()
    P.phase = 'late_c'
    A.reset(mMT)
    Wo = A.alloc([128, KC, D], BF16, "Wo")
    for q in range(4):
        k.dma("sp", Wo[:, q * 4:(q + 1) * 4, :], Wout.ap[:, q * 4:(q + 1) * 4, :], [Wout], [Wo])
    regC.reset()
    gateP = regC.alloc([128, D], F32, "gateP")
    gateS = regC.alloc([32, D], F32, "gateS")
    fnw = regC.alloc([128, D], F32, "fnw")
    xt3 = [regC.alloc([128, D], F32, f"xt3_{i}") for i in range(2)]
    yt = [regC.alloc([128, D], F32, f"yt{i}") for i in range(2)]
    st3 = [regC.alloc([128, 8], F32, f"st3_{i}") for i in range(2)]
    k.dma("sp", gateP[:], gsc.ap[0:1, :].partition_broadcast(128), [gsc], [gateP])
    for b in range(4):
        k.dma("sp", gateS[8 * b:8 * b + 8], gsc.ap[1 + b:2 + b, :].partition_broadcast(8), [gsc], [gateS])
    k.dma("sp", fnw[:], fnw_d.partition_broadcast(128), [], [fnw])
    for (t, NT, tok0) in TILES:
        X, Y, ST = xt3[t % 2], yt[t % 2], st3[t % 2]
        G_ = gateP if t < 8 else gateS
        if t < 8:
            k.dma("sp", X[:], xl[t], [], [X])
        else:
            k.dma("sp", X[0:32], xs[:, :], [], [X])
        for c4 in range(4):
            pb = k.ps()
            for kc in range(KC):
                k.mm(pb[0:NT, :], mT[:, kc, tok0:tok0 + NT], Wo[:, kc, c4 * 512:(c4 + 1) * 512], kc == 0, kc == KC - 1, [mT, Wo], [pb])
            k.tt("dve", Y[0:NT, c4 * 512:(c4 + 1) * 512], pb[0:NT, :], G_[0:NT, c4 * 512:(c4 + 1) * 512], ALU.mult, [pb, G_], [Y])
        k.tt("pool", Y[0:NT], Y[0:NT], X[0:NT], ALU.add, [Y, X], [Y])
        k.act(X[0:NT], Y[0:NT], AF.Square, [Y], [X, ST], accum_out=ST[0:NT, 0:1])
        k.ts("dve", ST[0:NT, 1:2], ST[0:NT, 0:1], 1.0 / D, EPS, ALU.mult, ALU.add, [ST], [ST])
        k.act(ST[0:NT, 2:3], ST[0:NT, 1:2], AF.Sqrt, [ST], [ST])
        P.op("dve", lambda e, ST=ST, NT=NT: e.reciprocal(out=ST[0:NT, 3:4], in_=ST[0:NT, 2:3]), [ST], [ST])
        k.stt(Y[0:NT], Y[0:NT], ST[0:NT, 3:4], fnw[0:NT], ALU.mult, ALU.mult, [Y, ST, fnw], [Y])
        if t < 8:
            k.dma("sp", y_l[t], Y[:], [Y], [])
        else:
            k.dma("sp", y_s[:, :], Y[0:32], [Y], [])


def rope_table(pos):
    half = 32
    inv = (10000.0 ** (-np.arange(half, dtype=np.float32) / half)).astype(np.float32)
    ang = pos.astype(np.float32)[:, None] * inv[None, :]
    return np.concatenate([np.cos(ang), np.sin(ang)], axis=1).astype(np.float32)


def core_inputs(c, inp):
    f = np.ascontiguousarray
    f32 = np.float32
    xp = inp["x_prompt"][0]
    m = {}
    m["x_prompt"] = xp
    m["xl"] = f(xp.reshape(NSLOT, 8, 128, D)[:, c])
    m["xs"] = f(inp["x_sample"][4 * c:4 * c + 4].reshape(32, D))
    m["c_all"] = f(np.concatenate([inp["c_prompt"], inp["c_sample"][4 * c:4 * c + 4]], axis=0))
    m["norm_w"] = inp["norm_w"]
    m["w_ada"] = inp["w_ada"][0]
    m["b_ada"] = inp["b_ada"]
    m["w_in"] = inp["w_in"][0]
    m["w_a2"] = inp["w_a2"][0]
    m["b_a"] = inp["b_a"]
    m["w_o_nsa"] = inp["w_o_nsa"][0]
    m["w_o_gla"] = inp["w_o_gla"][0]
    m["w_out"] = inp["w_out"][0]
    m["cmp_pos"] = inp["cmp_pos"][0].reshape(64, 64)
    m["cmp_w1"] = inp["cmp_w1"][0]
    m["cmp_b1"] = inp["cmp_b1"][0]
    m["cmp_w2"] = inp["cmp_w2"][0]
    m["cmp_b2"] = inp["cmp_b2"][0]
    m["gla_norm_w"] = inp["gla_norm_w"]
    m["final_norm_w"] = inp["final_norm_w"].reshape(1, D)
    m["cache_cmp"] = inp["cache_kv_cmp"].reshape(-1, 512)
    m["cache_sel"] = inp["cache_kv_sel"].reshape(-1, 512)
    m["cache_win"] = f(inp["cache_kv_win"][0, 4 * c:4 * c + 4].reshape(4, 512, 512))
    m["state_gla"] = f(inp["state_gla"][0, 4 * c:4 * c + 4])
    m["page_table"] = f(inp["page_table"][4 * c:4 * c + 4].reshape(1, 256).astype(np.int32))
    m["rope_g"] = rope_table(np.arange(SEQ))
    mc = np.zeros((128, NTG), f32)
    mc[:, c::8] = 1.0
    m["mcol"] = mc
    s_, t_ = np.meshgrid(np.arange(128), np.arange(128), indexing="ij")
    m["cU"] = np.where(s_ > t_, -1.0 / 16.0, 0.0).astype(f32)
    cst = np.zeros((128, NCST), f32)
    cst[:, O_UGT:O_UGT + 128] = np.where(s_ > t_, -1.0 / 16.0, 0.0)
    cst[:, O_UGE:O_UGE + 128] = np.where(s_ <= t_, -1.0 / 16.0, 0.0)
    cst[:, O_MTRI:O_MTRI + 128] = np.where(s_ <= t_, 1.0, 0.0)
    s3, t3 = s_[:32, :32], t_[:32, :32]
    same = (s3 // 8) == (t3 // 8)
    cst[:32, O_UGE32:O_UGE32 + 32] = np.where(same & (s3 <= t3), -1.0 / 16.0, 0.0)
    cst[:32, O_MTRI32:O_MTRI32 + 32] = np.where(same & (s3 <= t3), 1.0, 0.0)
    cst[:32, O_UGT32:O_UGT32 + 32] = np.where(same & (s3 > t3), -1.0 / 16.0, 0.0)
    p32 = np.arange(32)
    ind = (p32[:, None] // 8 == np.arange(4)[None, :]).astype(f32)
    cst[:32, O_NEGIND:O_NEGIND + 4] = -ind / 16.0
    cst[:32, O_IND:O_IND + 4] = ind
    cst[:, O_INDROW:O_INDROW + 128] = ind.T.reshape(1, 128)
    q_, k_ = np.meshgrid(p32, p32, indexing="ij")
    cst[:32, O_NEWB:O_NEWB + 32] = np.where(((q_ // 8) != (k_ // 8)) | ((k_ % 8) > (q_ % 8)), NEGB, 0.0)
    cst[:, O_QLOC] = 128 * c + np.arange(128)
    cst[:32, O_QPS] = SEQ + (p32 % 8)
    cst[:32, O_NB30:O_NB30 + 4] = (1.0 - ind) * NEGB
    cst[:, O_IOTA:O_IOTA + 1536] = np.arange(1536, dtype=f32)[None, :]
    m["cst"] = cst
    cc = np.arange(512)[:, None] * 16
    js = np.arange(128)[None, :] * 64
    cm = ((cc < js + 64) & (cc + 32 > js)).astype(f32)
    m["cm2s"] = f(cm.reshape(4, 128, 128).transpose(1, 0, 2))
    rl = np.zeros((9, 128, 64), f32)
    for t in range(8):
        rl[t] = rope_table(128 * (8 * t + c) + np.arange(128)) / 8.0
    rsm = rope_table(SEQ + (p32 % 8))
    rl[8, :32] = rsm / 8.0
    m["ropel"] = rl
    m["ropes"] = rsm
    return m


def kernel(**inp):
    inp = {k_: np.asarray(v) for k_, v in inp.items()}
    nc = build()
    in_maps = [core_inputs(c, inp) for c in range(NCORES)]
    res = run_bass_kernel_spmd(nc, in_maps, core_ids=list(range(NCORES)))
    R = res.results
    f32 = np.float32
    kvshape = (2, 4, 64)

    def get(c, name, shape):
        if name in R[c]:
            return np.asarray(R[c][name], dtype=f32).reshape(shape)
        return np.zeros(shape, f32)

    y_prompt = np.zeros((1, SEQ, D), f32)
    yv = y_prompt.reshape(NSLOT, 8, 128, D)
    for c in range(NCORES):
        yv[:, c] = get(c, "y_l", (NSLOT, 128, D))
    y_sample = np.concatenate([get(c, "y_s", (4, 8, D)) for c in range(NCORES)], axis=0)
    kv_cmp_prompt = get(0, "kv_cmp_p", (1, 1, SEQ) + kvshape)
    kv_sel_prompt = get(0, "kv_sel_p", (1, 1, SEQ) + kvshape)
    kv_win_prompt = get(0, "kv_win_p", (1, 1, 512) + kvshape)
    kv_cmp_sample = np.concatenate([get(c, "kv_cmp_s", (1, 4, 8) + kvshape) for c in range(NCORES)], axis=1)
    kv_sel_sample = np.concatenate([get(c, "kv_sel_s", (1, 4, 8) + kvshape) for c in range(NCORES)], axis=1)
    kv_win_sample = np.concatenate([get(c, "kv_win_s", (1, 4, 512) + kvshape) for c in range(NCORES)], axis=1)
    gp = get(0, "gla_p", (128, 4, 256))
    gla_state_prompt = np.ascontiguousarray(gp.transpose(1, 0, 2)).reshape(1, 1, 4, 128, 256)
    gs = [get(c, "gla_s", (4, 128, 4, 256)) for c in range(NCORES)]
    gla_state_sample = np.concatenate([np.ascontiguousarray(g.transpose(0, 2, 1, 3)) for g in gs], axis=0).reshape(1, 32, 4, 128, 256)
    return (y_prompt, y_sample, kv_cmp_prompt, kv_cmp_sample, kv_sel_prompt, kv_sel_sample,
            kv_win_prompt, kv_win_sample, gla_state_prompt, gla_state_sample)
```
